# Optimizing a Trainium2 kernel written in Bass

```python
import math, functools
import jax, jax.numpy as jnp
from jax import lax
import numpy as np

D_MODEL = 2048
BATCH = 2
SEQ = 8192
DEPTH = 1
DEC_BATCH = 32
DEC_SEQ = 32
PAST_LEN = 2048

CHUNK = 64
D_MIX = D_MODEL
D_SSM = D_MIX // 2
SSM_GROUP = 16
N_SSM_GROUPS = D_SSM // SSM_GROUP
SSM_STATE = 64
D_ATTN = D_MIX - D_SSM
HEAD_DIM = 128
N_HEADS = D_ATTN // HEAD_DIM
D_IN = D_SSM + 3 * D_ATTN + N_HEADS
D_FF = 4 * D_MODEL
Q_BLOCK = 128
EPS = 1e-6
DT_MIN = 1e-3
DT_MAX = 1e-1
FORGET_BIAS = 3.0
NEG_INF = -1e30

kernel_name = 'hymba_s5_fox_stream_step'


def rmsnorm(x, g):
    x32 = x.astype(jnp.float32)
    y = x32 * lax.rsqrt(jnp.mean(x32 * x32, axis=-1, keepdims=True) + EPS)
    return (y * g.astype(jnp.float32)).astype(x.dtype)


def _cmul(ar, ai, br, bi):
    return ar * br - ai * bi, ar * bi + ai * br


def _scan_combine(e1, e2):
    a1r, a1i, b1r, b1i = e1
    a2r, a2i, b2r, b2i = e2
    ar, ai = _cmul(a2r, a2i, a1r, a1i)
    br, bi = _cmul(a2r, a2i, b1r, b1i)
    return ar, ai, br + b2r, bi + b2i


def s5_mixer(u, h0_re, h0_im, a_re, a_im, log_step, b_re, b_im, c_re, c_im, d, w_glu):
    n, l, _ = u.shape
    f32 = jnp.float32
    u32 = u.astype(f32).reshape(n, l, N_SSM_GROUPS, SSM_GROUP)
    a_re = a_re.astype(f32)
    a_im = a_im.astype(f32)
    step = jnp.exp(log_step.astype(f32))[:, None]
    mag = jnp.exp(a_re * step)
    abar_re = mag * jnp.cos(a_im * step)
    abar_im = mag * jnp.sin(a_im * step)
    den = a_re * a_re + a_im * a_im
    nr = abar_re - 1.0
    ni = abar_im
    fr = (nr * a_re + ni * a_im) / den
    fi = (ni * a_re - nr * a_im) / den
    b_re = b_re.astype(f32)
    b_im = b_im.astype(f32)
    bbar_re = fr[..., None] * b_re - fi[..., None] * b_im
    bbar_im = fr[..., None] * b_im + fi[..., None] * b_re
    bu_re = jnp.einsum('nlgh,gph->nlgp', u32, bbar_re)
    bu_im = jnp.einsum('nlgh,gph->nlgp', u32, bbar_im)
    ih_re, ih_im = _cmul(abar_re, abar_im, h0_re.astype(f32), h0_im.astype(f32))
    bu_re = bu_re.at[:, 0].add(ih_re)
    bu_im = bu_im.at[:, 0].add(ih_im)
    a_br = jnp.broadcast_to(abar_re, bu_re.shape)
    a_bi = jnp.broadcast_to(abar_im, bu_im.shape)
    _, _, h_re, h_im = lax.associative_scan(_scan_combine, (a_br, a_bi, bu_re, bu_im), axis=1)
    y = (jnp.einsum('nlgp,ghp->nlgh', h_re, c_re.astype(f32))
         - jnp.einsum('nlgp,ghp->nlgh', h_im, c_im.astype(f32))
         + d.astype(f32) * u32)
    y = y.reshape(n, l, D_SSM)
    gy = jax.nn.gelu(y)
    out = gy * jax.nn.sigmoid(gy @ w_glu.astype(f32))
    return out.astype(u.dtype), h_re[:, -1], h_im[:, -1]


def _attend(q, cq, q_pos, k, v, ck, k_pos):
    s = jnp.einsum('nqhd,nkhd->nhqk', q, k).astype(jnp.float32) * (HEAD_DIM ** -0.5)
    s = s + jnp.transpose(cq, (0, 2, 1))[..., None] - jnp.transpose(ck, (0, 2, 1))[:, :, None, :]
    mask = k_pos[None, :] <= q_pos[:, None]
    s = jnp.where(mask, s, NEG_INF)
    p = jax.nn.softmax(s, axis=-1)
    return jnp.einsum('nhqk,nkhd->nqhd', p.astype(v.dtype), v)


def fox_prompt(q, k, v, logf):
    n, l = q.shape[0], q.shape[1]
    c = jnp.cumsum(logf.astype(jnp.float32), axis=1)
    pos = jnp.arange(l)

    def block(i):
        start = i * Q_BLOCK
        qb = lax.dynamic_slice_in_dim(q, start, Q_BLOCK, axis=1)
        cb = lax.dynamic_slice_in_dim(c, start, Q_BLOCK, axis=1)
        qp = start + jnp.arange(Q_BLOCK)
        return _attend(qb, cb, qp, k, v, c, pos)

    o = lax.map(block, jnp.arange(l // Q_BLOCK))
    return jnp.transpose(o, (1, 0, 2, 3, 4)).reshape(n, l, D_ATTN)


def fox_sample(q, k, v, logf, cache_k, cache_v, cache_logf):
    n, s = q.shape[0], q.shape[1]
    past = cache_k.shape[1]
    k_all = jnp.concatenate([cache_k.astype(k.dtype), k], axis=1)
    v_all = jnp.concatenate([cache_v.astype(v.dtype), v], axis=1)
    c_all = jnp.cumsum(jnp.concatenate([cache_logf.astype(jnp.float32), logf.astype(jnp.float32)], axis=1), axis=1)
    k_pos = jnp.arange(past + s)
    q_pos = past + jnp.arange(s)
    o = _attend(q, c_all[:, past:], q_pos, k_all, v_all, c_all, k_pos)
    return o.reshape(n, s, D_ATTN)


def trunk_layer(x, h0_re, h0_im, attn_fn, g_norm_mix, w_in, b_f, a_re, a_im, log_step,
                b_re, b_im, c_re, c_im, d, w_glu, g_q, g_k, g_out_ssm, g_out_attn,
                w_out, g_norm_mlp, w_up, w_down):
    lead = x.shape[:2]
    h = rmsnorm(x, g_norm_mix)
    z = h @ w_in
    u = z[..., :D_SSM]
    q = z[..., D_SSM:D_SSM + D_ATTN].reshape(lead + (N_HEADS, HEAD_DIM))
    k = z[..., D_SSM + D_ATTN:D_SSM + 2 * D_ATTN].reshape(lead + (N_HEADS, HEAD_DIM))
    v = z[..., D_SSM + 2 * D_ATTN:D_SSM + 3 * D_ATTN].reshape(lead + (N_HEADS, HEAD_DIM))
    logf = jax.nn.log_sigmoid((z[..., D_SSM + 3 * D_ATTN:] + b_f).astype(jnp.float32))
    q = rmsnorm(q, g_q)
    k = rmsnorm(k, g_k)
    ssm_out, h_re, h_im = s5_mixer(u, h0_re, h0_im, a_re, a_im, log_step, b_re, b_im, c_re, c_im, d, w_glu)
    attn_out = attn_fn(q, k, v, logf)
    mix = jnp.concatenate([rmsnorm(ssm_out, g_out_ssm), rmsnorm(attn_out.astype(x.dtype), g_out_attn)], axis=-1)
    x = x + mix @ w_out
    hm = rmsnorm(x, g_norm_mlp)
    x = x + jnp.square(jax.nn.relu(hm @ w_up)) @ w_down
    return x, k, v, logf, h_re, h_im


def setup_inputs(seed: int = 0) -> dict:
    key = jax.random.key(seed)
    ks = jax.random.split(key, 32)
    f32 = jnp.float32

    def nrm(k, shape, scale):
        return jax.random.normal(k, shape, f32) * scale

    def gain(k, shape):
        return 1.0 + 0.02 * jax.random.normal(k, shape, f32)

    G, P, H = N_SSM_GROUPS, SSM_STATE, SSM_GROUP
    x_prompt = nrm(ks[0], (BATCH, SEQ, D_MODEL), 1.0)
    x_sample = nrm(ks[1], (DEC_BATCH, DEC_SEQ, D_MODEL), 1.0)
    cache_k = nrm(ks[2], (DEPTH, DEC_BATCH, PAST_LEN, N_HEADS, HEAD_DIM), 1.0)
    cache_v = nrm(ks[3], (DEPTH, DEC_BATCH, PAST_LEN, N_HEADS, HEAD_DIM), 1.0)
    cache_logf = jax.nn.log_sigmoid(FORGET_BIAS + jax.random.normal(ks[4], (DEPTH, DEC_BATCH, PAST_LEN, N_HEADS), f32))
    state_ssm_re = nrm(ks[5], (DEPTH, DEC_BATCH, G, P), 0.5)
    state_ssm_im = nrm(ks[6], (DEPTH, DEC_BATCH, G, P), 0.5)
    g_norm_mix = gain(ks[7], (DEPTH, D_MODEL))
    w_in = nrm(ks[8], (DEPTH, D_MODEL, D_IN), D_MODEL ** -0.5)
    b_f = FORGET_BIAS + nrm(ks[9], (DEPTH, N_HEADS), 0.1)
    n_idx = jnp.arange(P, dtype=f32)
    ssm_a_re = -0.5 + nrm(ks[10], (DEPTH, G, P), 0.01)
    ssm_a_im = math.pi * n_idx + nrm(ks[11], (DEPTH, G, P), 0.01)
    ssm_log_step = jax.random.uniform(ks[12], (DEPTH, G), f32, math.log(DT_MIN), math.log(DT_MAX))
    ssm_b_re = nrm(ks[13], (DEPTH, G, P, H), (2 * H) ** -0.5)
    ssm_b_im = nrm(ks[14], (DEPTH, G, P, H), (2 * H) ** -0.5)
    ssm_c_re = nrm(ks[15], (DEPTH, G, H, P), P ** -0.5)
    ssm_c_im = nrm(ks[16], (DEPTH, G, H, P), P ** -0.5)
    ssm_d = nrm(ks[17], (DEPTH, G, H), 1.0)
    w_glu = nrm(ks[18], (DEPTH, D_SSM, D_SSM), D_SSM ** -0.5)
    g_q = gain(ks[19], (DEPTH, HEAD_DIM))
    g_k = gain(ks[20], (DEPTH, HEAD_DIM))
    g_out_ssm = gain(ks[21], (DEPTH, D_SSM))
    g_out_attn = gain(ks[22], (DEPTH, D_ATTN))
    w_out = nrm(ks[23], (DEPTH, D_MIX, D_MODEL), D_MIX ** -0.5)
    g_norm_mlp = gain(ks[24], (DEPTH, D_MODEL))
    w_up = nrm(ks[25], (DEPTH, D_MODEL, D_FF), D_MODEL ** -0.5)
    w_down = nrm(ks[26], (DEPTH, D_FF, D_MODEL), D_FF ** -0.5)
    return {'x_prompt': x_prompt, 'x_sample': x_sample,
            'cache_k': cache_k, 'cache_v': cache_v, 'cache_logf': cache_logf,
            'state_ssm_re': state_ssm_re, 'state_ssm_im': state_ssm_im,
            'g_norm_mix': g_norm_mix, 'w_in': w_in, 'b_f': b_f,
            'ssm_a_re': ssm_a_re, 'ssm_a_im': ssm_a_im, 'ssm_log_step': ssm_log_step,
            'ssm_b_re': ssm_b_re, 'ssm_b_im': ssm_b_im, 'ssm_c_re': ssm_c_re, 'ssm_c_im': ssm_c_im,
            'ssm_d': ssm_d, 'w_glu': w_glu, 'g_q': g_q, 'g_k': g_k,
            'g_out_ssm': g_out_ssm, 'g_out_attn': g_out_attn, 'w_out': w_out,
            'g_norm_mlp': g_norm_mlp, 'w_up': w_up, 'w_down': w_down}


def reference(x_prompt, x_sample, cache_k, cache_v, cache_logf, state_ssm_re, state_ssm_im,
              g_norm_mix, w_in, b_f, ssm_a_re, ssm_a_im, ssm_log_step, ssm_b_re, ssm_b_im,
              ssm_c_re, ssm_c_im, ssm_d, w_glu, g_q, g_k, g_out_ssm, g_out_attn, w_out,
              g_norm_mlp, w_up, w_down):
    y_p = x_prompt
    y_s = x_sample
    kp, vp, lfp, hrp, hip = [], [], [], [], []
    ks_, vs_, lfs, hrs, his = [], [], [], [], []
    for l in range(DEPTH):
        w = (g_norm_mix[l], w_in[l], b_f[l], ssm_a_re[l], ssm_a_im[l], ssm_log_step[l],
             ssm_b_re[l], ssm_b_im[l], ssm_c_re[l], ssm_c_im[l], ssm_d[l], w_glu[l],
             g_q[l], g_k[l], g_out_ssm[l], g_out_attn[l], w_out[l], g_norm_mlp[l], w_up[l], w_down[l])
        h0 = jnp.zeros((y_p.shape[0], N_SSM_GROUPS, SSM_STATE), jnp.float32)
        y_p, k1, v1, lf1, hr1, hi1 = trunk_layer(y_p, h0, h0, fox_prompt, *w)
        samp_attn = functools.partial(fox_sample, cache_k=cache_k[l], cache_v=cache_v[l], cache_logf=cache_logf[l])
        y_s, k2, v2, lf2, hr2, hi2 = trunk_layer(y_s, state_ssm_re[l], state_ssm_im[l], samp_attn, *w)
        kp.append(k1); vp.append(v1); lfp.append(lf1); hrp.append(hr1); hip.append(hi1)
        ks_.append(k2); vs_.append(v2); lfs.append(lf2); hrs.append(hr2); his.append(hi2)
    return (y_p, y_s,
            jnp.stack(kp), jnp.stack(vp), jnp.stack(lfp), jnp.stack(hrp), jnp.stack(hip),
            jnp.stack(ks_), jnp.stack(vs_), jnp.stack(lfs), jnp.stack(hrs), jnp.stack(his))
```

```python
import numpy as np
from contextlib import ExitStack
import concourse.bass as bass
import concourse.mybir as mybir
from concourse.bass_utils import run_bass_kernel_spmd

F32 = mybir.dt.float32
BF16 = mybir.dt.bfloat16
I32 = mybir.dt.int32
AF = mybir.ActivationFunctionType
ALU = mybir.AluOpType
AX = mybir.AxisListType

ENGINES = ("pe", "act", "dve", "pool", "sp")
WAR_GUARD = True
QS = ("sp", "pool", "act")


class Op:
    __slots__ = ("eng", "fn", "reads", "writes", "dma", "deps", "ticket", "has_dep",
                 "dsem", "dval", "prev_same_sem", "out_store")

    def __init__(self, eng, fn, reads, writes, dma=False, out_store=False):
        self.eng = eng
        self.fn = fn
        self.reads = tuple(reads)
        self.writes = tuple(writes)
        self.dma = dma
        self.deps = ()
        self.ticket = None
        self.has_dep = False
        self.dsem = None
        self.dval = None
        self.prev_same_sem = None
        self.out_store = out_store


class Prog:
    def __init__(self, nc, esems, dsems, n_dma_sems):
        self.nc = nc
        self.ops = []
        self.n = n_dma_sems
        self.esems = esems
        self.dsems = dsems
        self.cnt = {e: 0 for e in ENGINES}
        self.k = {q: 0 for q in QS}
        self.semcnt = [0] * (n_dma_sems * len(QS))
        self.nseg = 0
        self.tot_ops = 0
        self.psum_tokens = set()

    def add(self, eng, fn, reads=(), writes=(), dma=False, out_store=False):
        xs = [r + "#x" for r in reads if r in self.psum_tokens]
        if xs:
            reads = tuple(reads) + tuple(xs)
            writes = tuple(writes) + tuple(xs)
        self.ops.append(Op(eng, fn, reads, writes, dma, out_store))

    def pe(self, fn, reads=(), writes=()):
        self.add("pe", fn, reads, writes)

    def act(self, fn, reads=(), writes=()):
        self.add("act", fn, reads, writes)

    def dve(self, fn, reads=(), writes=()):
        self.add("dve", fn, reads, writes)

    def pool(self, fn, reads=(), writes=()):
        self.add("pool", fn, reads, writes)

    def dma(self, q, fn, reads=(), writes=(), out_store=False):
        self.add(q, fn, reads, writes, dma=True, out_store=out_store)

    def _resolve(self):
        last_w = {}
        readers = {}
        ops = self.ops
        eng_ops = {e: [] for e in ENGINES}
        eng_pos = {}
        for i, op in enumerate(ops):
            if not op.dma:
                eng_pos[i] = len(eng_ops[op.eng])
                eng_ops[op.eng].append(i)
        for i, op in enumerate(ops):
            deps = set()
            for r in op.reads:
                w = last_w.get(r)
                if w is not None:
                    deps.add(w)
            for w_ in op.writes:
                w = last_w.get(w_)
                if w is not None:
                    deps.add(w)
                rl = readers.get(w_, ())
                lastc = {}
                for rd in rl:
                    if ops[rd].dma:
                        deps.add(rd)
                    else:
                        lastc[ops[rd].eng] = rd
                for e_, rd in lastc.items():
                    if WAR_GUARD and e_ != op.eng:
                        lst = eng_ops[e_]
                        pos = eng_pos[rd]
                        if pos + 1 < len(lst) and lst[pos + 1] < i:
                            rd = lst[pos + 1]
                    deps.add(rd)
            deps.discard(i)
            if op.eng == "pe" and not op.dma:
                deps = {d for d in deps if not (ops[d].eng == "pe" and not ops[d].dma)}
            op.deps = tuple(sorted(deps))
            for d in op.deps:
                ops[d].has_dep = True
            for r in op.reads:
                readers.setdefault(r, []).append(i)
            for w_ in op.writes:
                last_w[w_] = i
                readers[w_] = []
        last = {}
        for op in ops:
            if not op.dma:
                last[op.eng] = op
        for op in last.values():
            op.has_dep = True
        for op in ops:
            if not op.dma and op.has_dep:
                self.cnt[op.eng] += 1
                op.ticket = self.cnt[op.eng]
        n = self.n
        for op in ops:
            if op.dma:
                qi = QS.index(op.eng)
                s = qi * n + (self.k[op.eng] % n)
                self.k[op.eng] += 1
                op.prev_same_sem = self.semcnt[s]
                self.semcnt[s] += 16
                op.dsem = s
                op.dval = self.semcnt[s]

    def flush(self):
        self._resolve()
        ops = self.ops
        esems, dsems = self.esems, self.dsems
        per = {e: [] for e in ENGINES}
        for i, op in enumerate(ops):
            per[op.eng].append(i)
        fin_e = dict(self.cnt)
        fin_d = list(self.semcnt)
        seg = self.nseg

        def run(eng_name, eng):
            seen_e = {e: 0 for e in ENGINES}
            seen_d = [0] * len(dsems)
            for i in per[eng_name]:
                op = ops[i]
                for d in op.deps:
                    dop = ops[d]
                    if dop.dma:
                        if seen_d[dop.dsem] < dop.dval:
                            eng.wait_ge(dsems[dop.dsem], dop.dval)
                            seen_d[dop.dsem] = dop.dval
                    else:
                        if seen_e[dop.eng] < dop.ticket:
                            eng.wait_ge(esems[dop.eng], dop.ticket)
                            seen_e[dop.eng] = dop.ticket
                if op.dma:
                    if op.prev_same_sem > 0 and seen_d[op.dsem] < op.prev_same_sem:
                        eng.wait_ge(dsems[op.dsem], op.prev_same_sem)
                        seen_d[op.dsem] = op.prev_same_sem
                    ins = op.fn(eng)
                    ins.then_inc(dsems[op.dsem], 16)
                else:
                    ins = op.fn(eng)
                    if op.ticket is not None:
                        ins.then_inc(esems[op.eng], 1)
            for e2 in ("pe", "act", "dve", "pool"):
                if fin_e[e2] > 0 and seen_e[e2] < fin_e[e2]:
                    eng.wait_ge(esems[e2], fin_e[e2])
            for si, v in enumerate(fin_d):
                if v > 0 and seen_d[si] < v:
                    eng.wait_ge(dsems[si], v)

        with self.nc.Block(f"seg{seg}") as block:
            @block.sync
            def _(e):
                run("sp", e)

            @block.tensor
            def _(e):
                run("pe", e)

            @block.scalar
            def _(e):
                run("act", e)

            @block.vector
            def _(e):
                run("dve", e)

            @block.gpsimd
            def _(e):
                run("pool", e)
        self.tot_ops += len(ops)
        self.ops = []
        self.nseg += 1


D = 2048
KT = 16
DIN = 4104
NH = 8
DH = 128
DSSM = 1024
G = 64
PST = 64
DFF = 8192
NCTX = 8192
NPRE = 6144
NOWN = 2048
NT = 64
NPT = 48
NSMP = 128
PAST = 2048
EPS = 1e-6
NTOK = NOWN + NSMP

PHASES = ("B", "C", "D", "E")


def build_program(phases=PHASES, dbg_groups=None, dbg_chunks=None):
    nc = bass.Bass("TRN2", target_bir_lowering=False)
    es = ExitStack()

    def din(name, shape, dt=F32):
        return nc.dram_tensor(name, list(shape), dt, kind="ExternalInput").ap()

    def dout(name, shape, dt=F32):
        return nc.dram_tensor(name, list(shape), dt, kind="ExternalOutput").ap()

    def dscr(name, shape, dt=BF16):
        return nc.dram_tensor(name, list(shape), dt).ap()

    x_ctx = din("x_ctx", [NCTX, D])
    x_smp = din("x_smp", [NSMP, D])
    ck = din("ck", [4, PAST, NH, DH])
    cv = din("cv", [4, PAST, NH, DH])
    clf = din("clf", [4, PAST, NH])
    sre = din("sre", [4, G * PST])
    sim = din("sim", [4, G * PST])
    g_norm_mix = din("g_norm_mix", [D])
    w_in = din("w_in", [D, DIN])
    b_f = din("b_f", [1, NH])
    a_re = din("ssm_a_re", [1, G * PST])
    a_im = din("ssm_a_im", [1, G * PST])
    log_step = din("ssm_log_step", [1, G])
    bblk_re = din("bblk_re", [128, 8, 512])
    bblk_im = din("bblk_im", [128, 8, 512])
    cblk_re = din("cblk_re", [128, 8, 4, 128])
    cblk_im = din("cblk_im", [128, 8, 4, 128])
    bpad_re = din("bpad_re", [128, 32, 32])
    bpad_im = din("bpad_im", [128, 32, 32])
    ssm_d = din("ssm_d", [G * 16])
    w_glu = din("w_glu", [DSSM, DSSM])
    g_q = din("g_q", [1, DH])
    g_k = din("g_k", [1, DH])
    g_out_ssm = din("g_out_ssm", [DSSM])
    g_out_attn = din("g_out_attn", [DSSM])
    w_out = din("w_out", [D, D])
    g_norm_mlp = din("g_norm_mlp", [D])
    w_up = din("w_up", [D, DFF])
    w_down = din("w_down", [DFF, D])
    c_ident = din("c_ident", [128, 128])
    c_triT = din("c_triT", [128, 128])
    c_kmask = din("c_kmask", [128, NT])
    c_iota1 = din("c_iota1", [128, 128])
    c_sp1 = din("c_sp1", [128, 1])

    y_own = dout("y_own", [NOWN, D])
    y_smp = dout("y_smp", [NSMP, D])
    k_own = dout("k_own", [NOWN, NH * DH])
    v_own = dout("v_own", [NOWN, NH * DH])
    lf_own = dout("lf_own", [NOWN, NH])
    hre_own = dout("hre_own", [G * PST])
    him_own = dout("him_own", [G * PST])
    k_smp = dout("k_smp", [NSMP, NH * DH])
    v_smp = dout("v_smp", [NSMP, NH * DH])
    lf_smp = dout("lf_smp", [NSMP, NH])
    hre_smp = dout("hre_smp", [4, G * PST])
    him_smp = dout("him_smp", [4, G * PST])

    scr_win = dscr("scr_win", [8, 128, KT, 512])
    scr_wout = dscr("scr_wout", [4, 128, KT, 512])
    scr_wup = dscr("scr_wup", [16, 128, KT, 512])
    scr_wdn = dscr("scr_wdn", [4, 4, 128, KT, 512])
    kT_scr = dscr("kT_scr", [NH, 128, NCTX])
    v_scr = dscr("v_scr", [NH, 128, NT, DH])
    qT_scr = dscr("qT_scr", [NH, 128, NOWN])
    uT_scr = dscr("uT_scr", [8, 128, NTOK])
    u_scr = dscr("u_scr", [NPT, 128, DSSM])
    mixT_scr = dscr("mixT_scr", [16, 128, NTOK])

    stk = [es]

    uid = [0]

    def sb(name, shape, dt=F32):
        uid[0] += 1
        return stk[-1].enter_context(nc.sbuf_tensor(f"{name}_{uid[0]}", list(shape), dt))

    def ps(name, shape, dt=F32, toks=None):
        P.psum_tokens.update(toks if toks is not None else [name])
        uid[0] += 1
        return stk[-1].enter_context(nc.psum_tensor(f"{name}_{uid[0]}", list(shape), dt))

    NDS = 16
    sems = [es.enter_context(nc.semaphore(f"s{i}")) for i in range(4 + NDS * len(QS))]
    esems = {"pe": sems[0], "act": sems[1], "dve": sems[2], "pool": sems[3]}
    P = Prog(nc, esems, sems[4:], NDS)

    def DMA(q, out, in_, reads=(), writes=(), out_store=False, nonc=False):
        if nonc:
            def fn(e, out=out, in_=in_):
                with nc.allow_non_contiguous_dma(reason="small strided table"):
                    return e.dma_start(out=out, in_=in_)
        else:
            def fn(e, out=out, in_=in_):
                return e.dma_start(out=out, in_=in_)
        P.dma(q, fn, reads=reads, writes=writes, out_store=out_store)

    ident_f = sb("ident_f", [128, 128])
    ident_b = sb("ident_b", [128, 128], BF16)
    triT_f = sb("triT_f", [128, 128])
    triT_b = sb("triT_b", [128, 128], BF16)
    ones_b = sb("ones_b", [128, 128], BF16)
    ones_f = sb("ones_f", [128, 128])
    kmask = sb("kmask", [128, NT])
    gmixT = sb("gmixT", [128, KT])
    gmlpT = sb("gmlpT", [128, KT])
    gq_bc = sb("gq_bc", [128, DH])
    gk_bc = sb("gk_bc", [128, DH])
    bf_bc = sb("bf_bc", [128, NH])
    wf_b = sb("wf_b", [128, KT, NH], BF16)
    logf_all = sb("logf_all", [128, NT, NH])
    logf_smp = sb("logf_smp", [32, 4, NH])
    kT_s = sb("kT_s", [128, NH, NSMP], BF16)
    qT_s = sb("qT_s", [128, NH, NSMP], BF16)
    v_s = sb("v_s", [32, NH, 4, DH], BF16)

    DMA("sp", ident_f[:], c_ident, writes=["ident_f"])
    DMA("sp", triT_f[:], c_triT, writes=["triT_f"])
    DMA("sp", kmask[:], c_kmask, writes=["kmask"])
    DMA("sp", gmixT[:], g_norm_mix.rearrange("(kt p) -> p kt", p=128), writes=["gmixT"], nonc=True)
    DMA("sp", gmlpT[:], g_norm_mlp.rearrange("(kt p) -> p kt", p=128), writes=["gmlpT"], nonc=True)
    DMA("sp", gq_bc[:], g_q.broadcast_to([128, DH]), writes=["gq_bc"])
    DMA("sp", gk_bc[:], g_k.broadcast_to([128, DH]), writes=["gk_bc"])
    DMA("sp", bf_bc[:], b_f.broadcast_to([128, NH]), writes=["bf_bc"])
    P.dve(lambda e: e.tensor_copy(ident_b[:], ident_f[:]), reads=["ident_f"], writes=["ident_b"])
    P.dve(lambda e: e.tensor_copy(triT_b[:], triT_f[:]), reads=["triT_f"], writes=["triT_b"])
    P.dve(lambda e: e.memset(ones_b[:], 1.0), writes=["ones_b"])
    P.dve(lambda e: e.memset(ones_f[:], 1.0), writes=["ones_f"])

    def conv_chunk(dst, src_w, c0, tok):
        DMA("pool", dst, src_w[:, c0:c0 + 512].rearrange("(kt p) n -> p kt n", p=128), writes=[tok])

    WIN_CH = {"u0": 0, "u1": 1, "q0": 2, "q1": 3, "k0": 4, "k1": 5, "v0": 6, "v1": 7}
    DMA("pool", wf_b[:], w_in[:, 4096:4104].rearrange("(kt p) n -> p kt n", p=128), writes=["wf_b"], nonc=True)
    conv_jobs = []
    for ci in range(4):
        conv_jobs.append((scr_wout[ci], w_out[:, ci * 512:(ci + 1) * 512].rearrange("(kt p) n -> p kt n", p=128), f"scr_wout{ci}"))
    for ci in range(16):
        conv_jobs.append((scr_wup[ci], w_up[:, ci * 512:(ci + 1) * 512].rearrange("(kt p) n -> p kt n", p=128), f"scr_wup{ci}"))
    for dmc in range(4):
        for fq in range(4):
            conv_jobs.append((scr_wdn[dmc, fq],
                              w_down[fq * 2048:(fq + 1) * 2048, dmc * 512:(dmc + 1) * 512].rearrange("(kt p) n -> p kt n", p=128),
                              f"scr_wdn{dmc}_{fq}"))

    def convert_E_weights(n=None):
        k_ = 0
        while conv_jobs and (n is None or k_ < n):
            dst_, src_, tok_ = conv_jobs.pop(0)
            DMA("pool", dst_, src_, writes=[tok_])
            k_ += 1

    conv_done = [False]

    def MM(out, lhsT, rhs, start, stop, reads, writes):
        P.pe(lambda e: e.matmul(out, lhsT, rhs, start=start, stop=stop), reads, writes)

    def TR(out, in_, idn, reads, writes):
        P.pe(lambda e: e.transpose(out, in_, idn), reads, writes)

    def ACTV(out, in_, func, reads, writes, **kw):
        P.act(lambda e: e.activation(out, in_, func, **kw), reads, writes)

    def ENG(eng):
        return {"dve": P.dve, "pool": P.pool, "act": P.act}[eng]

    def TT(eng, out, in0, in1, op, reads, writes):
        ENG(eng)(lambda e: e.tensor_tensor(out, in0, in1, op), reads, writes)

    def TS(eng, out, in0, s1, s2, op0, op1, reads, writes):
        if op1 is None:
            ENG(eng)(lambda e: e.tensor_scalar(out, in0, s1, None, op0), reads, writes)
        else:
            ENG(eng)(lambda e: e.tensor_scalar(out, in0, s1, s2, op0, op1), reads, writes)

    def STT(eng, out, in0, scalar, in1, op0, op1, reads, writes):
        ENG(eng)(lambda e: e.scalar_tensor_tensor(out, in0, scalar, in1, op0, op1), reads, writes)

    def CP(eng, out, in_, reads, writes):
        if eng == "act":
            P.act(lambda e: e.copy(out, in_), reads, writes)
        else:
            ENG(eng)(lambda e: e.tensor_copy(out, in_), reads, writes)

    def RED(out, in_, reads, writes):
        P.dve(lambda e: e.tensor_reduce(out, in_, AX.X, ALU.add), reads, writes)

    def RECIP(out, in_, reads, writes):
        P.dve(lambda e: e.reciprocal(out, in_), reads, writes)

    def MEMSET(eng, ap, val, writes):
        ENG(eng)(lambda e: e.memset(ap, val), (), writes)

    def h4(ap):
        return ap.rearrange("p (h d) -> p h d", h=4)

    a0 = ExitStack()
    stk.append(a0)
    stg_f = [sb(f"stg_f{i}", [128, 2, 512]) for i in range(4)]
    stg_b = [sb(f"stg_b{i}", [128, KT, 512], BF16) for i in range(2)]
    sc_ = 0
    for n_, nm in enumerate(["u0", "u1", "k0", "k1", "v0", "v1", "q0", "q1"]):
        ci = WIN_CH[nm]
        bb = n_ % 2
        for k2 in range(KT // 2):
            fb_ = sc_ % 4
            sc_ += 1
            DMA("sp", stg_f[fb_][:], w_in[k2 * 256:(k2 + 1) * 256, ci * 512:(ci + 1) * 512].rearrange("(kt p) n -> p kt n", p=128),
                writes=[f"stg_f{fb_}"])
            CP("act" if k2 % 2 == 0 else "dve", stg_b[bb][:, 2 * k2:2 * k2 + 2, :], stg_f[fb_][:], [f"stg_f{fb_}"], [f"stg_b{bb}"])
        DMA("sp", scr_win[ci], stg_b[bb][:], reads=[f"stg_b{bb}"], writes=[f"scr_win{ci}"])
    P.flush()
    stk.pop()
    a0.close()

    class LM:
        pass

    def linear_machinery(nw=3, nh=2, with_xt=True, npm=2):
        m = LM()
        m.nw = nw
        m.wbuf = [sb(f"wbuf{i}", [128, KT, 512], BF16) for i in range(nw)]
        m.hT = [sb(f"hT{i}", [128, KT, 512], BF16) for i in range(nh)]
        m.xt = [sb(f"xt{i}", [128, D]) for i in range(2)] if with_xt else None
        m.xn = [sb(f"xn{i}", [128, D], BF16) for i in range(2)]
        m.ss = [sb(f"ss{i}", [128, 1]) for i in range(2)]
        m.rstd = [sb(f"rstd{i}", [128, 1]) for i in range(2)]
        m.pxT = ps("pxT", [128, KT, 128], BF16, toks=["pxT0", "pxT1"])
        m.pmm = [ps(f"pmm{i}", [128, 512]) for i in range(npm)]
        m.wcount = 0
        m.fe_count = 0
        return m

    def load_wchunk(m, src_ap, src_tok):
        i = m.wcount % m.nw
        m.wcount += 1
        DMA("sp", m.wbuf[i][:], src_ap, reads=[src_tok], writes=[f"wbuf{i}"])
        return i

    def front_end(m, src_dram, src_sb, src_tok, L, gT, gT_tok, hbuf, col0, htok, defer=False):
        b = m.fe_count % 2
        m.fe_count += 1
        xt, xn, ss, rstd, pxT, hT = m.xt, m.xn, m.ss, m.rstd, m.pxT, m.hT
        if src_dram is not None:
            DMA("sp", xt[b][0:L], src_dram, writes=[f"xt{b}"])
            src = xt[b]
            stok = f"xt{b}"
        else:
            src = src_sb
            stok = src_tok
        def head():
            MEMSET("dve", ss[b][0:L], 0.0, [f"ss{b}"])
            ACTV(xn[b][0:L], src[0:L], AF.Square, [stok, f"ss{b}"], [f"xn{b}", f"ss{b}"], accum_out=ss[b][0:L, 0:1])
            ACTV(rstd[b][0:L], ss[b][0:L], AF.Sqrt, [f"ss{b}"], [f"rstd{b}"], bias=EPS, scale=1.0 / D)
            RECIP(rstd[b][0:L], rstd[b][0:L], [f"rstd{b}"], [f"rstd{b}"])
            ACTV(xn[b][0:L], src[0:L], AF.Copy, [stok, f"rstd{b}"], [f"xn{b}"], scale=rstd[b][0:L, 0:1])

        if defer != "3":
            head()

        def tail():
            for kt in range(KT):
                TR(pxT[:, kt, 0:L], xn[b][0:L, kt * 128:(kt + 1) * 128], ident_b[0:L, 0:L], [f"xn{b}", "ident_b"], [f"pxT{kt // 8}"])
            for half in range(2):
                TT("dve", hT[hbuf][:, half * 8:(half + 1) * 8, col0:col0 + L], pxT[:, half * 8:(half + 1) * 8, 0:L],
                   gT[:, half * 8:(half + 1) * 8].unsqueeze(2).to_broadcast([128, 8, L]), ALU.mult,
                   [f"pxT{half}", gT_tok], [htok])

        if defer == "3":
            return head, tail
        if defer:
            return tail
        tail()
        return None

    if "B" in phases:
        pes = ExitStack()
        stk.append(pes)
        m = linear_machinery(npm=3)
        hT, wbuf, pmm = m.hT, m.wbuf, m.pmm
        ptr_k = ps("ptr_k", [128, 4, 128], BF16)
        ptr_u = ps("ptr_u", [128, 8, 128], BF16)
        pf4 = ps("pf4", [128, 4, NH])
        ft4 = sb("ft4", [128, 4, NH])
        sqs = [sb(f"sqs{i}", [128, 512]) for i in range(2)]
        ssh = [sb(f"ssh{i}", [128, 4]) for i in range(2)]
        knf = [sb(f"knf{i}", [128, 512]) for i in range(2)]
        kraw = [sb(f"kraw{i}", [128, 512]) for i in range(2)]
        kout = [sb(f"kout{i}", [128, 512]) for i in range(2)]
        kbf = [sb(f"kbf{i}", [128, 512], BF16) for i in range(2)]
        vout = [sb(f"vout{i}", [128, 512]) for i in range(2)]
        kT_grp = [sb(f"kT_grp{i}", [128, NH, 512], BF16) for i in range(2)]
        qT_grp = [sb("qT_grp0", [128, NH, 512], BF16)] * 2
        v_grp = [sb(f"v_grp{i}", [128, NH, 4, DH], BF16) for i in range(2)]
        u_bf = [sb(f"u_bf{i}", [128, DSSM], BF16) for i in range(4)]
        uT_grp = [sb("uT_grp0", [128, 8, 512], BF16)] * 2
        ft = sb("ft", [128, NH])
        it = [0]
        tails = []

        def fe_tile(grp, tt, defer=False):
            is_smp = grp == 16
            L = 32 if is_smp else 128
            src = x_smp[tt * 32:(tt + 1) * 32, :] if is_smp else x_ctx[(grp * 4 + tt) * 128:(grp * 4 + tt + 1) * 128, :]
            gi_ = grp_list.index(grp)
            return front_end(m, src, None, None, L, gmixT, "gmixT", gi_ % 2, tt * L, f"hT{gi_ % 2}_{tt}", defer=defer)

        grp_list = list(range(17)) if dbg_groups is None else list(dbg_groups)
        for tt in range(4):
            if grp_list:
                fe_tile(grp_list[0], tt)

        def chunks_of(grp):
            is_smp_ = grp == 16
            own_ = grp >= 12 and not is_smp_
            ch_ = ["u0", "u1"] + (["q0", "q1"] if (own_ or is_smp_) else []) + ["k0", "k1", "v0", "v1", "f"]
            if dbg_chunks is not None:
                ch_ = [c_ for c_ in dbg_chunks if c_ in ch_]
            return ch_

        flat = [(gi_, ci_, cn_) for gi_, g_ in enumerate(grp_list) for ci_, cn_ in enumerate(chunks_of(g_))]
        loads = [(gi_, ci_, cn_) for (gi_, ci_, cn_) in flat if cn_ != "f"]
        sched_next = {}
        for k_ in range(len(loads) - 1):
            sched_next[(loads[k_][0], loads[k_][1])] = loads[k_ + 1][2]
        wq = []
        fe_pending = [None]
        if loads:
            wq.append(load_wchunk(m, scr_win[WIN_CH[loads[0][2]]], f"scr_win{WIN_CH[loads[0][2]]}"))
        for gi, grp in enumerate(grp_list):
            is_smp = grp == 16
            own = grp >= 12 and not is_smp
            L = 32 if is_smp else 128
            gb = gi % 2
            chunks = ["u0", "u1"] + (["q0", "q1"] if (own or is_smp) else []) + ["k0", "k1", "v0", "v1", "f"]
            if dbg_chunks is not None:
                chunks = [c_ for c_ in dbg_chunks if c_ in chunks]
            for cidx, cn in enumerate(chunks):
                if "E" in phases and gi >= 1:
                    convert_E_weights(1)
                fe_head = None
                if cidx < 4 and gi + 1 < len(grp_list):
                    fe_head, fe_tail_new = fe_tile(grp_list[gi + 1], cidx, defer="3")
                if cn != "f":
                    wi = wq.pop(0)
                nxt = sched_next.get((gi, cidx))
                if nxt is not None:
                    wq.append(load_wchunk(m, scr_win[WIN_CH[nxt]], f"scr_win{WIN_CH[nxt]}"))
                for tt in range(4):
                    T = grp * 4 + tt
                    htok = f"hT{gb}_{tt}"
                    j = it[0] % 2
                    jp = it[0] % 3
                    it[0] += 1
                    if cn == "f":
                        for kt in range(KT):
                            MM(pf4[0:L, tt, :], hT[gb][:, kt, tt * L:(tt + 1) * L], wf_b[:, kt, :], kt == 0, kt == KT - 1, [htok, "wf_b"], ["pf4"])
                        if tt == 3:
                            TT("dve", ft4[0:L], pf4[0:L], bf_bc[0:L].unsqueeze(1).to_broadcast([L, 4, NH]), ALU.add, ["pf4", "bf_bc"], ["ft4"])
                            ACTV(ft4[0:L], ft4[0:L], AF.Exp, ["ft4"], ["ft4"], scale=-1.0)
                            ACTV(ft4[0:L], ft4[0:L], AF.Ln, ["ft4"], ["ft4"], bias=1.0)
                            if is_smp:
                                TS("dve", logf_smp[:, :, :], ft4[0:32], -1.0, None, ALU.mult, None, ["ft4"], ["logf_smp"])
                            else:
                                TS("dve", logf_all[:, grp * 4:grp * 4 + 4, :], ft4[:], -1.0, None, ALU.mult, None, ["ft4"],
                                   [f"logf{grp * 4 + t_}" for t_ in range(4)])
                        continue
                    pm = pmm[jp]
                    ptok = f"pmm{jp}"
                    for kt in range(KT):
                        MM(pm[0:L, :], hT[gb][:, kt, tt * L:(tt + 1) * L], wbuf[wi][:, kt, :], kt == 0, kt == KT - 1,
                           [htok, f"wbuf{wi}"], [ptok])
                    while len(tails) >= 2:
                        tails.pop(0)()
                    half = int(cn[1])
                    hs = slice(half * 512, (half + 1) * 512)
                    if cn[0] == "u":
                        CP("act", u_bf[tt][0:L, hs], pm[0:L, :], [ptok], [f"u_bf{tt}_{half}"])
                        if half == 1 and not (own or is_smp):
                            DMA("sp", u_scr[T], u_bf[tt][:], reads=[f"u_bf{tt}_0", f"u_bf{tt}_1"], writes=["u_scr"])
                        if half == 1 and (own or is_smp):
                            def utail(tt=tt, L=L, gb=gb):
                                for f8 in range(8):
                                    TR(ptr_u[:, f8, 0:L], u_bf[tt][0:L, f8 * 128:(f8 + 1) * 128], ident_b[0:L, 0:L],
                                       [f"u_bf{tt}_0", f"u_bf{tt}_1", "ident_b"], ["ptr_u"])
                                CP("act", uT_grp[gb][:, :, tt * L:(tt + 1) * L], ptr_u[:, :, 0:L], ["ptr_u"], ["uT_grp0"])
                            tails.append(utail)
                    elif cn[0] in "qk":
                        gbc, gtok = (gq_bc, "gq_bc") if cn[0] == "q" else (gk_bc, "gk_bc")
                        ACTV(sqs[j][0:L], pm[0:L, :], AF.Square, [ptok], [f"sqs{j}"])
                        CP("dve", kraw[j][0:L], pm[0:L, :], [ptok], [f"kraw{j}"])
                        RED(ssh[j][0:L], h4(sqs[j][0:L]), [f"sqs{j}"], [f"ssh{j}"])
                        ACTV(ssh[j][0:L], ssh[j][0:L], AF.Sqrt, [f"ssh{j}"], [f"ssh{j}"], bias=EPS, scale=1.0 / DH)
                        RECIP(ssh[j][0:L], ssh[j][0:L], [f"ssh{j}"], [f"ssh{j}"])
                        TT("dve", h4(knf[j][0:L]), h4(kraw[j][0:L]), ssh[j][0:L].unsqueeze(2).to_broadcast([L, 4, DH]), ALU.mult,
                           [f"kraw{j}", f"ssh{j}"], [f"knf{j}"])
                        TT("pool", h4(kout[j][0:L]), h4(knf[j][0:L]), gbc[0:L].unsqueeze(1).to_broadcast([L, 4, DH]), ALU.mult,
                           [f"knf{j}", gtok], [f"kout{j}"])
                        CP("act", kbf[j][0:L], kout[j][0:L], [f"kout{j}"], [f"kbf{j}"])
                        if is_smp:
                            dstT, dtok = (kT_s, "kT_s") if cn[0] == "k" else (qT_s, "qT_s")
                        else:
                            dstT, dtok = (kT_grp[gb], f"kT_grp{gb}") if cn[0] == "k" else (qT_grp[gb], "qT_grp0")

                        def ktail(j=j, L=L, dstT=dstT, dtok=dtok, half=half, tt=tt):
                            for hh in range(4):
                                TR(ptr_k[:, hh, 0:L], kbf[j][0:L, hh * 128:(hh + 1) * 128], ident_b[0:L, 0:L], [f"kbf{j}", "ident_b"], ["ptr_k"])
                            CP("dve", dstT[:, half * 4:(half + 1) * 4, tt * L:(tt + 1) * L], ptr_k[:, :, 0:L], ["ptr_k"], [dtok])
                        tails.append(ktail)
                        if cn[0] == "k" and (own or is_smp):
                            if is_smp:
                                dst = k_smp[tt * 32:(tt + 1) * 32, hs]
                            else:
                                dst = k_own[(T - NPT) * 128:(T - NPT + 1) * 128, hs]
                            DMA("sp", dst, kout[j][0:L], reads=[f"kout{j}"], out_store=True)
                    else:
                        CP("act", vout[j][0:L], pm[0:L, :], [ptok], [f"vout{j}"])
                        if is_smp:
                            CP("pool", v_s[:, half * 4:(half + 1) * 4, tt, :], h4(vout[j][0:32, :]), [f"vout{j}"], ["v_s"])
                        else:
                            CP("pool", v_grp[gb][:, half * 4:(half + 1) * 4, tt, :], h4(vout[j][:, :]), [f"vout{j}"], [f"v_grp{gb}"])
                        if own or is_smp:
                            if is_smp:
                                dst = v_smp[tt * 32:(tt + 1) * 32, hs]
                            else:
                                dst = v_own[(T - NPT) * 128:(T - NPT + 1) * 128, hs]
                            DMA("sp", dst, vout[j][0:L], reads=[f"vout{j}"], out_store=True)
                if fe_pending[0] is not None:
                    fe_pending[0]()
                    fe_pending[0] = None
                if fe_head is not None:
                    fe_head()
                    fe_pending[0] = fe_tail_new
            if fe_pending[0] is not None:
                fe_pending[0]()
                fe_pending[0] = None
            while tails:
                tails.pop(0)()
            if not is_smp:
                DMA("sp", kT_scr[:, :, grp * 512:(grp + 1) * 512].rearrange("h p n -> p h n"), kT_grp[gb][:],
                    reads=[f"kT_grp{gb}"], writes=["kT_scr"])
                DMA("sp", v_scr[:, :, grp * 4:(grp + 1) * 4, :].rearrange("h p t d -> p h (t d)"),
                    v_grp[gb][:].rearrange("p h t d -> p h (t d)"), reads=[f"v_grp{gb}"], writes=["v_scr"])
                if own:
                    DMA("sp", qT_scr[:, :, (grp - 12) * 512:(grp - 11) * 512].rearrange("h p n -> p h n"), qT_grp[gb][:],
                        reads=["qT_grp0"], writes=["qT_scr"])
                    DMA("sp", uT_scr[:, :, (grp - 12) * 512:(grp - 11) * 512].rearrange("f p n -> p f n"), uT_grp[gb][:],
                        reads=["uT_grp0"], writes=["uT_scr"])
            else:
                DMA("sp", uT_scr[:, :, NOWN:NOWN + NSMP].rearrange("f p n -> p f n"), uT_grp[gb][:, :, 0:NSMP],
                    reads=["uT_grp0"], writes=["uT_scr"])
        DMA("sp", lf_own.rearrange("(t p) h -> p t h", p=128), logf_all[:, NPT:NT, :],
            reads=[f"logf{T}" for T in range(NPT, NT)], out_store=True, nonc=True)
        DMA("sp", lf_smp.rearrange("(s p) h -> p s h", p=32), logf_smp[:], reads=["logf_smp"], out_store=True, nonc=True)
        P.flush()
        stk.pop()
        pes.close()


    rstd_ssm = sb("rstd_ssm", [128, 20])
    rstd_att = sb("rstd_att", [128, 20])
    if "C" in phases:
        ces = ExitStack()
        stk.append(ces)
        TWO_PI = float(2.0 * np.pi)
        NC = G * PST
        W1re = sb("W1re", [128, NC])
        W1im = sb("W1im", [128, NC])
        W2re = sb("W2re", [128, 32, 128])
        W2im = sb("W2im", [128, 32, 128])
        WEre = sb("WEre", [128, NC], BF16)
        WEim = sb("WEim", [128, NC], BF16)
        Bblk = sb("Bblk", [128, 8, 1024], BF16)
        Cre_b = sb("Cre_b", [128, 8, 4, 128], BF16)
        Cimn_b = sb("Cimn_b", [128, 8, 4, 128], BF16)
        Bpre = sb("Bpre", [128, 32, 32])
        Bpim = sb("Bpim", [128, 32, 32])
        d_t = sb("d_t", [128, 8])
        gssm_t = sb("gssm_t", [128, 8])
        hpre = [sb("hpre_re", [128, 32]), sb("hpre_im", [128, 32])]
        iota1 = sb("iota1", [128, 128])
        sp1 = sb("sp1", [128, 1])
        nsp1 = sb("nsp1", [128, 1])
        nE = sb("nE", [128, 1])
        DMA("sp", iota1[:], c_iota1, writes=["iota1"])
        DMA("sp", sp1[:], c_sp1, writes=["sp1"])
        DMA("sp", d_t[:], ssm_d.rearrange("(fb q) -> q fb", q=128), writes=["d_t"], nonc=True)
        DMA("sp", gssm_t[:], g_out_ssm.rearrange("(fb q) -> q fb", q=128), writes=["gssm_t"], nonc=True)
        TS("dve", nsp1[:], sp1[:], -1.0, None, ALU.mult, None, ["sp1"], ["nsp1"])
        TS("dve", nE[:], sp1[:], -1.0, 128.0, ALU.mult, ALU.add, ["sp1"], ["nE"])

        ses = ExitStack()
        stk.append(ses)
        QN = 1024
        lam_q = sb("lam_q", [128, QN])
        th_q = sb("th_q", [128, QN])
        are_q = sb("are_q", [128, QN])
        aim_q = sb("aim_q", [128, QN])
        fr_q = sb("fr_q", [128, QN])
        fi_q = sb("fi_q", [128, QN])
        tA = sb("tA", [128, QN])
        tB = sb("tB", [128, QN])
        tC = sb("tC", [128, QN])
        tI = sb("tI", [128, QN], I32)
        raw1 = sb("raw1", [128, QN])
        raw2 = sb("raw2", [128, QN])
        step_bc = sb("step_bc", [128, G])
        lam_st = sb("lam_st", [128, 32])
        th_st = sb("th_st", [128, 32])
        are_st = sb("are_st", [128, 32])
        aim_st = sb("aim_st", [128, 32])
        step_st = sb("step_st", [128, 32])

        def trig2(out_c, ctok, out_s, stok_, z, ztok):
            TS("dve", tB[:], z, 1.0, TWO_PI, ALU.mult, ALU.add, [ztok], ["tB"])
            TS("dve", tI[:], tB[:], 1.0 / TWO_PI, None, ALU.mult, None, ["tB"], ["tI"])
            CP("dve", tC[:], tI[:], ["tI"], ["tC"])
            STT("dve", tB[:], tC[:], -TWO_PI, tB[:], ALU.mult, ALU.add, ["tC", "tB"], ["tB"])
            TS("dve", tC[:], tB[:], float(np.pi), -TWO_PI, ALU.is_gt, ALU.mult, ["tB"], ["tC"])
            TT("dve", tB[:], tB[:], tC[:], ALU.add, ["tB", "tC"], ["tB"])
            ACTV(out_s, tB[:], AF.Sin, ["tB"], [stok_])
            STT("dve", tC[:], tB[:], -1.0, tB[:], ALU.mult, ALU.max, ["tB"], ["tC"])
            ACTV(out_c, tC[:], AF.Sin, ["tC"], [ctok], bias=float(np.pi / 2), scale=-1.0)

        DMA("sp", step_bc[:], log_step.broadcast_to([128, G]), writes=["step_bc"])
        ACTV(step_bc[:], step_bc[:], AF.Exp, ["step_bc"], ["step_bc"])
        DMA("sp", are_st[:], a_re.rearrange("o (j q) -> q (o j)", q=128), writes=["are_st"], nonc=True)
        DMA("sp", aim_st[:], a_im.rearrange("o (j q) -> q (o j)", q=128), writes=["aim_st"], nonc=True)
        ls3 = log_step.rearrange("o (j e) -> o j e", e=2)
        DMA("sp", step_st[0:64, :], ls3[:, :, 0].broadcast_to([64, 32]), writes=["step_st"], nonc=True)
        DMA("sp", step_st[64:128, :], ls3[:, :, 1].broadcast_to([64, 32]), writes=["step_st"], nonc=True)
        ACTV(step_st[:], step_st[:], AF.Exp, ["step_st"], ["step_st"])
        TT("dve", lam_st[:], are_st[:], step_st[:], ALU.mult, ["are_st", "step_st"], ["lam_st"])
        TT("dve", th_st[:], aim_st[:], step_st[:], ALU.mult, ["aim_st", "step_st"], ["th_st"])

        for qd in range(4):
            cq = slice(qd * QN, (qd + 1) * QN)
            gq = slice(qd * 16, (qd + 1) * 16)
            g3 = lambda ap: ap.rearrange("p (g q) -> p g q", g=16)
            stepb3 = step_bc[:, gq].unsqueeze(2).to_broadcast([128, 16, PST])
            DMA("sp", are_q[:], a_re[:, cq].broadcast_to([128, QN]), writes=["are_q"])
            DMA("sp", aim_q[:], a_im[:, cq].broadcast_to([128, QN]), writes=["aim_q"])
            TT("dve", g3(lam_q[:]), g3(are_q[:]), stepb3, ALU.mult, ["are_q", "step_bc"], ["lam_q"])
            TT("dve", g3(th_q[:]), g3(aim_q[:]), stepb3, ALU.mult, ["aim_q", "step_bc"], ["th_q"])
            trig2(fr_q[:], "fr_q", fi_q[:], "fi_q", th_q[:], "th_q")
            ACTV(tA[:], lam_q[:], AF.Exp, ["lam_q"], ["tA"])
            TT("dve", fr_q[:], fr_q[:], tA[:], ALU.mult, ["fr_q", "tA"], ["fr_q"])
            TT("dve", fi_q[:], fi_q[:], tA[:], ALU.mult, ["fi_q", "tA"], ["fi_q"])
            TS("dve", fr_q[:], fr_q[:], -1.0, None, ALU.add, None, ["fr_q"], ["fr_q"])
            TT("dve", tA[:], are_q[:], are_q[:], ALU.mult, ["are_q"], ["tA"])
            TT("dve", tB[:], aim_q[:], aim_q[:], ALU.mult, ["aim_q"], ["tB"])
            TT("dve", tA[:], tA[:], tB[:], ALU.add, ["tA", "tB"], ["tA"])
            RECIP(tA[:], tA[:], ["tA"], ["tA"])
            TT("dve", tB[:], fr_q[:], are_q[:], ALU.mult, ["fr_q", "are_q"], ["tB"])
            TT("dve", tC[:], fi_q[:], aim_q[:], ALU.mult, ["fi_q", "aim_q"], ["tC"])
            TT("dve", tB[:], tB[:], tC[:], ALU.add, ["tB", "tC"], ["tB"])
            TT("dve", tC[:], fi_q[:], are_q[:], ALU.mult, ["fi_q", "are_q"], ["tC"])
            TT("dve", fi_q[:], fr_q[:], aim_q[:], ALU.mult, ["fr_q", "aim_q"], ["fi_q"])
            TT("dve", fi_q[:], tC[:], fi_q[:], ALU.subtract, ["tC", "fi_q"], ["fi_q"])
            TT("dve", fr_q[:], tB[:], tA[:], ALU.mult, ["tB", "tA"], ["fr_q"])
            TT("dve", fi_q[:], fi_q[:], tA[:], ALU.mult, ["fi_q", "tA"], ["fi_q"])
            fbq = slice(2 * qd, 2 * qd + 2)
            v2 = lambda ap: ap.rearrange("p (a b) -> p a b", a=2)
            DMA("sp", v2(raw1[:]), bblk_re[:, fbq, :], writes=["raw1"])
            DMA("sp", v2(raw2[:]), bblk_im[:, fbq, :], writes=["raw2"])
            TT("dve", tA[:], raw1[:], fr_q[:], ALU.mult, ["raw1", "fr_q"], ["tA"])
            TT("dve", tB[:], raw2[:], fi_q[:], ALU.mult, ["raw2", "fi_q"], ["tB"])
            TT("dve", Bblk[:, fbq, 0:512], v2(tA[:]), v2(tB[:]), ALU.subtract, ["tA", "tB"], ["Bblk"])
            TT("dve", tA[:], raw2[:], fr_q[:], ALU.mult, ["raw2", "fr_q"], ["tA"])
            TT("dve", tB[:], raw1[:], fi_q[:], ALU.mult, ["raw1", "fi_q"], ["tB"])
            TT("dve", Bblk[:, fbq, 512:1024], v2(tA[:]), v2(tB[:]), ALU.add, ["tA", "tB"], ["Bblk"])
            DMA("sp", raw1[:], cblk_re[:, fbq].rearrange("p a b c -> p (a b c)"), reads=["Bblk"], writes=["raw1"])
            DMA("sp", raw2[:], cblk_im[:, fbq].rearrange("p a b c -> p (a b c)"), reads=["Bblk"], writes=["raw2"])
            CP("pool", Cre_b[:, fbq].rearrange("p a b c -> p (a b c)"), raw1[:], ["raw1"], ["Cre_b"])
            TS("dve", Cimn_b[:, fbq].rearrange("p a b c -> p (a b c)"), raw2[:], -1.0, None, ALU.mult, None, ["raw2"], ["Cimn_b"])
            TS("dve", tA[:], th_q[:], sp1[:, 0:1], None, ALU.mult, None, ["th_q", "sp1"], ["tA"])
            trig2(W1re[:, cq], "W1re", W1im[:, cq], "W1im", tA[:], "tA")
            ACTV(tA[:], lam_q[:], AF.Exp, ["lam_q", "nsp1"], ["tA"], scale=nsp1[:, 0:1])
            TT("dve", W1re[:, cq], W1re[:, cq], tA[:], ALU.mult, ["W1re", "tA"], ["W1re"])
            STT("dve", W1im[:, cq], W1im[:, cq], -1.0, tA[:], ALU.mult, ALU.mult, ["W1im", "tA"], ["W1im"])
            TS("dve", tA[:], th_q[:], nE[:, 0:1], None, ALU.mult, None, ["th_q", "nE"], ["tA"])
            trig2(fr_q[:], "fr_q", fi_q[:], "fi_q", tA[:], "tA")
            ACTV(tA[:], lam_q[:], AF.Exp, ["lam_q", "nE"], ["tA"], scale=nE[:, 0:1])
            TT("dve", WEre[:, cq], fr_q[:], tA[:], ALU.mult, ["fr_q", "tA"], ["WEre"])
            TT("dve", WEim[:, cq], fi_q[:], tA[:], ALU.mult, ["fi_q", "tA"], ["WEim"])
            jq_ = slice(qd * 8, (qd + 1) * 8)
            j3 = lambda ap: ap.rearrange("p (j t) -> p j t", j=8)
            iob = iota1[:].unsqueeze(1).to_broadcast([128, 8, 128])
            w2r = W2re[:, jq_, :].rearrange("p j t -> p (j t)")
            w2i = W2im[:, jq_, :].rearrange("p j t -> p (j t)")
            TT("dve", j3(tA[:]), iob, th_st[:, jq_].unsqueeze(2).to_broadcast([128, 8, 128]), ALU.mult, ["iota1", "th_st"], ["tA"])
            trig2(w2r, "W2re", w2i, "W2im", tA[:], "tA")
            TT("dve", j3(tA[:]), iob, lam_st[:, jq_].unsqueeze(2).to_broadcast([128, 8, 128]), ALU.mult, ["iota1", "lam_st"], ["tA"])
            ACTV(tA[:], tA[:], AF.Exp, ["tA"], ["tA"])
            TT("dve", w2r, w2r, tA[:], ALU.mult, ["W2re", "tA"], ["W2re"])
            TT("dve", w2i, w2i, tA[:], ALU.mult, ["W2im", "tA"], ["W2im"])
        fs = [sb(f"fs{i}", [128, 32]) for i in range(6)]
        nr_s, ni_s, den_s, t_s, fr_s, fi_s = fs
        TS("dve", nr_s[:], W2re[:, :, 0], -1.0, None, ALU.add, None, ["W2re"], ["nr_s"])
        CP("dve", ni_s[:], W2im[:, :, 0], ["W2im"], ["ni_s"])
        TT("dve", den_s[:], are_st[:], are_st[:], ALU.mult, ["are_st"], ["den_s"])
        TT("dve", t_s[:], aim_st[:], aim_st[:], ALU.mult, ["aim_st"], ["t_s"])
        TT("dve", den_s[:], den_s[:], t_s[:], ALU.add, ["den_s", "t_s"], ["den_s"])
        RECIP(den_s[:], den_s[:], ["den_s"], ["den_s"])
        TT("dve", fr_s[:], nr_s[:], are_st[:], ALU.mult, ["nr_s", "are_st"], ["fr_s"])
        TT("dve", t_s[:], ni_s[:], aim_st[:], ALU.mult, ["ni_s", "aim_st"], ["t_s"])
        TT("dve", fr_s[:], fr_s[:], t_s[:], ALU.add, ["fr_s", "t_s"], ["fr_s"])
        TT("dve", fr_s[:], fr_s[:], den_s[:], ALU.mult, ["fr_s", "den_s"], ["fr_s"])
        TT("dve", fi_s[:], ni_s[:], are_st[:], ALU.mult, ["ni_s", "are_st"], ["fi_s"])
        TT("dve", t_s[:], nr_s[:], aim_st[:], ALU.mult, ["nr_s", "aim_st"], ["t_s"])
        TT("dve", fi_s[:], fi_s[:], t_s[:], ALU.subtract, ["fi_s", "t_s"], ["fi_s"])
        TT("dve", fi_s[:], fi_s[:], den_s[:], ALU.mult, ["fi_s", "den_s"], ["fi_s"])
        braw_re = sb("braw_re", [128, 32, 32])
        braw_im = sb("braw_im", [128, 32, 32])
        bt1 = sb("bt1", [128, 32, 32])
        bt2 = sb("bt2", [128, 32, 32])
        DMA("sp", braw_re[:], bpad_re, writes=["braw_re"])
        DMA("sp", braw_im[:], bpad_im, writes=["braw_im"])
        frb = fr_s[:].unsqueeze(2).to_broadcast([128, 32, 32])
        fib = fi_s[:].unsqueeze(2).to_broadcast([128, 32, 32])
        TT("dve", bt1[:], braw_re[:], frb, ALU.mult, ["braw_re", "fr_s"], ["bt1"])
        TT("dve", bt2[:], braw_im[:], fib, ALU.mult, ["braw_im", "fi_s"], ["bt2"])
        TT("dve", Bpre[:], bt1[:], bt2[:], ALU.subtract, ["bt1", "bt2"], ["Bpre"])
        TT("dve", bt1[:], braw_im[:], frb, ALU.mult, ["braw_im", "fr_s"], ["bt1"])
        TT("dve", bt2[:], braw_re[:], fib, ALU.mult, ["braw_re", "fi_s"], ["bt2"])
        TT("dve", Bpim[:], bt1[:], bt2[:], ALU.add, ["bt1", "bt2"], ["Bpim"])
        MEMSET("dve", hpre[0][:], 0.0, ["hpre0"])
        MEMSET("dve", hpre[1][:], 0.0, ["hpre1"])
        P.flush()
        stk.pop()
        ses.close()

        e0 = ExitStack()
        stk.append(e0)
        pE = [ps(f"pE{i}", [128, 8, 2, 32]) for i in range(2)]
        ut = [sb(f"ut{i}", [128, DSSM], BF16) for i in range(2)]
        em2 = [[sb(f"em{q}_{i}", [128, 8, 32]) for i in range(6)] for q in range(2)]
        Ere = sb("Ere", [128, 32])
        Eim = sb("Eim", [128, 32])
        hs = [sb(f"hs{i}", [128, 32]) for i in range(4)]
        A128re = W2re[:, :, 127]
        A128im = W2im[:, :, 127]
        n_pre = NPT if dbg_groups is None else 0
        Ere2 = [Ere, sb("Ere_b2", [128, 32])]
        Eim2 = [Eim, sb("Eim_b2", [128, 32])]
        NR = n_pre * 4

        def e1(r):
            T, jq = r // 4, r % 4
            ub = T % 2
            if jq == 0:
                DMA("sp", ut[ub][:], u_scr[T], writes=[f"ut{ub}"])
            pe_ = pE[r % 2]
            ptk = f"pE{r % 2}"
            for jl in range(8):
                j = jq * 8 + jl
                MM(pe_[:, jl, 0, :], WEre[:, j * 128:(j + 1) * 128], ut[ub][:, j * 32:(j + 1) * 32], True, True, ["WEre", f"ut{ub}"], [ptk])
                MM(pe_[:, jl, 1, :], WEim[:, j * 128:(j + 1) * 128], ut[ub][:, j * 32:(j + 1) * 32], True, True, ["WEim", f"ut{ub}"], [ptk])

        def e2(r):
            T, jq = r // 4, r % 4
            pe_ = pE[r % 2]
            ptk = f"pE{r % 2}"
            js = slice(jq * 8, (jq + 1) * 8)
            em = em2[r % 2]
            eq = r % 2
            TT("dve", em[0][:], pe_[:, :, 0, :], Bpre[:, js, :], ALU.mult, [ptk, "Bpre"], [f"em{eq}0"])
            TT("dve", em[1][:], pe_[:, :, 1, :], Bpim[:, js, :], ALU.mult, [ptk, "Bpim"], [f"em{eq}1"])
            TT("dve", em[2][:], pe_[:, :, 0, :], Bpim[:, js, :], ALU.mult, [ptk, "Bpim"], [f"em{eq}2"])
            TT("dve", em[3][:], pe_[:, :, 1, :], Bpre[:, js, :], ALU.mult, [ptk, "Bpre"], [f"em{eq}3"])

        def e3(r):
            em = em2[r % 2]
            eq = r % 2
            TT("pool", em[4][:], em[0][:], em[1][:], ALU.subtract, [f"em{eq}0", f"em{eq}1"], [f"em{eq}4"])
            TT("pool", em[5][:], em[2][:], em[3][:], ALU.add, [f"em{eq}2", f"em{eq}3"], [f"em{eq}5"])

        def e4(r):
            T, jq = r // 4, r % 4
            js = slice(jq * 8, (jq + 1) * 8)
            em = em2[r % 2]
            eq = r % 2
            tp_ = T % 2
            RED(Ere2[tp_][:, js], em[4][:], [f"em{eq}4"], [f"Ere{tp_}_{jq}"])
            RED(Eim2[tp_][:, js], em[5][:], [f"em{eq}5"], [f"Eim{tp_}_{jq}"])
            if jq != 3:
                return
            TT("dve", hs[0][:], A128re, hpre[0][:], ALU.mult, ["hpre0"], ["hs0"])
            TT("dve", hs[1][:], A128im, hpre[1][:], ALU.mult, ["hpre1"], ["hs1"])
            TT("dve", hs[2][:], A128re, hpre[1][:], ALU.mult, ["hpre1"], ["hs2"])
            TT("dve", hs[3][:], A128im, hpre[0][:], ALU.mult, ["hpre0"], ["hs3"])
            TT("dve", hs[0][:], hs[0][:], hs[1][:], ALU.subtract, ["hs0", "hs1"], ["hs0"])
            TT("dve", hs[2][:], hs[2][:], hs[3][:], ALU.add, ["hs2", "hs3"], ["hs2"])
            TT("dve", hpre[0][:], hs[0][:], Ere2[tp_][:], ALU.add, ["hs0"] + [f"Ere{tp_}_{q_}" for q_ in range(4)], ["hpre0"])
            TT("dve", hpre[1][:], hs[2][:], Eim2[tp_][:], ALU.add, ["hs2"] + [f"Eim{tp_}_{q_}" for q_ in range(4)], ["hpre1"])

        for step in range(NR + 3):
            if step < NR:
                e1(step)
            if 0 <= step - 1 < NR:
                e2(step - 1)
            if 0 <= step - 2 < NR:
                e3(step - 2)
            if 0 <= step - 3 < NR:
                e4(step - 3)
        P.flush()
        stk.pop()
        e0.close()

        c1 = ExitStack()
        stk.append(c1)
        wglu_b = sb("wglu_b", [128, 8, DSSM], BF16)
        for kt in range(8):
            DMA("pool", wglu_b[:, kt, :], w_glu[kt * 128:(kt + 1) * 128, :], writes=["wglu_b"])
        pbu = [ps(f"pbu{i}", [128, 512]) for i in range(2)]
        pST = [ps(f"pST{i}", [128, 4, 128]) for i in range(2)]
        py = ps("py", [128, 128])
        pglu = ps("pglu", [128, 4, 128])
        prs = ps("prs", [128, 1])
        uT_t = [sb(f"uT_t{i}", [128, 8, 128], BF16) for i in range(2)]
        tt_ = [[sb(f"ct{q}_{i}", [128, 256]) for i in range(4)] for q in range(2)]
        Xb = [sb(f"Xb{q}", [128, 512], BF16) for q in range(2)]
        Ab = [sb(f"Ab{q}", [128, 4, 128]) for q in range(2)]
        pp = [[sb(f"pp{q}_{i}", [128, 2, 128]) for i in range(4)] for q in range(2)]
        Hb = [sb(f"Hb{q}", [128, 4, 128], BF16) for q in range(2)]
        yT = sb("yT", [128, 8, 128])
        g1 = sb("g1", [128, 8, 128])
        gyT = sb("gyT", [128, 8, 128])
        gyT_b = sb("gyT_b", [128, 8, 128], BF16)
        so = sb("so", [128, 4, 128])
        sq = sb("sq", [128, 8, 128], BF16)
        mixT_t = sb("mixT_t", [128, 8, 128], BF16)
        hc = [[sb(f"hc{a}{b}", [128, 32]) for b in range(2)] for a in range(2)]
        rs_t = sb("rs_t", [128, 1])

        if dbg_groups is None:
            tiles = [("own", n) for n in range(16)] + [("smp", s_) for s_ in range(4)]
        else:
            tiles = [("own", n) for n in range(1)] + [("smp", s_) for s_ in range(1)]
        yT2 = [yT, sb("yT_b2", [128, 8, 128])]
        units = []
        for ti, (kind, n) in enumerate(tiles):
            for hfb in range(16):
                units.append((ti, kind, n, hfb))

        def uinfo(u):
            ti, kind, n, hfb = units[u]
            L = 128 if kind == "own" else 32
            c0 = n * 128 if kind == "own" else NOWN + 32 * n
            tcol = n if kind == "own" else 16 + n
            return ti, kind, n, hfb, L, c0, tcol, ti % 2, ti % 2, hc[ti % 2], hc[1 - ti % 2], hfb // 2, hfb % 2, u % 2

        def stage1(u):
            ti, kind, n, hfb, L, c0, tcol, ub, pi, hin, hout, fb, hh, q = uinfo(u)
            if hfb == 0:
                DMA("sp", uT_t[ub][:, :, 0:L], uT_scr[:, :, c0:c0 + L].rearrange("f p n -> p f n"), writes=[f"uT_t{ub}"], nonc=(L < 128))
            cre = slice(fb * 512 + hh * 256, fb * 512 + hh * 256 + 256)
            MM(pbu[q][0:L, 0:256], uT_t[ub][:, fb, 0:L], Bblk[:, fb, hh * 256:hh * 256 + 256], True, True, [f"uT_t{ub}", "Bblk"], [f"pbu{q}"])
            MM(pbu[q][0:L, 256:512], uT_t[ub][:, fb, 0:L], Bblk[:, fb, 512 + hh * 256:512 + hh * 256 + 256], True, True,
               [f"uT_t{ub}", "Bblk"], [f"pbu{q}"])
            t_ = tt_[q]
            TT("dve", t_[0][0:L], pbu[q][0:L, 0:256], W1re[0:L, cre], ALU.mult, [f"pbu{q}", "W1re"], [f"ct{q}0"])
            TT("dve", t_[1][0:L], pbu[q][0:L, 256:512], W1im[0:L, cre], ALU.mult, [f"pbu{q}", "W1im"], [f"ct{q}1"])
            TT("dve", t_[2][0:L], pbu[q][0:L, 0:256], W1im[0:L, cre], ALU.mult, [f"pbu{q}", "W1im"], [f"ct{q}2"])
            TT("dve", t_[3][0:L], pbu[q][0:L, 256:512], W1re[0:L, cre], ALU.mult, [f"pbu{q}", "W1re"], [f"ct{q}3"])
            TT("pool", Xb[q][0:L, 0:256], t_[0][0:L], t_[1][0:L], ALU.subtract, [f"ct{q}0", f"ct{q}1"], [f"Xre{q}"])
            TT("pool", Xb[q][0:L, 256:512], t_[2][0:L], t_[3][0:L], ALU.add, [f"ct{q}2", f"ct{q}3"], [f"Xim{q}"])

        def stage2(u):
            ti, kind, n, hfb, L, c0, tcol, ub, pi, hin, hout, fb, hh, q = uinfo(u)
            if hfb == 0:
                if kind == "own" and n == 0:
                    CP("dve", hin[0][:], hpre[0][:], ["hpre0"], [f"hc{pi}0_{f_}" for f_ in range(8)])
                    CP("dve", hin[1][:], hpre[1][:], ["hpre1"], [f"hc{pi}1_{f_}" for f_ in range(8)])
                if kind == "smp":
                    DMA("sp", hin[0][:], sre[n].rearrange("(j q) -> q j", q=128), writes=[f"hc{pi}0_{f_}" for f_ in range(8)], nonc=True)
                    DMA("sp", hin[1][:], sim[n].rearrange("(j q) -> q j", q=128), writes=[f"hc{pi}1_{f_}" for f_ in range(8)], nonc=True)
            for k2 in range(2):
                MM(pST[q][:, k2, 0:L], Xb[q][0:L, k2 * 128:(k2 + 1) * 128], triT_b[0:L, 0:L], True, True, [f"Xre{q}", "triT_b"], [f"pST{q}"])
            for k2 in range(2):
                MM(pST[q][:, 2 + k2, 0:L], Xb[q][0:L, 256 + k2 * 128:256 + (k2 + 1) * 128], triT_b[0:L, 0:L], True, True,
                   [f"Xim{q}", "triT_b"], [f"pST{q}"])
            j0 = fb * 4 + hh * 2
            for k2 in range(2):
                ACTV(Ab[q][:, k2, 0:L], pST[q][:, k2, 0:L], AF.Identity, [f"pST{q}", f"hc{pi}0_{fb}"], [f"Are{q}"], bias=hin[0][:, j0 + k2:j0 + k2 + 1])
            for k2 in range(2):
                ACTV(Ab[q][:, 2 + k2, 0:L], pST[q][:, 2 + k2, 0:L], AF.Identity, [f"pST{q}", f"hc{pi}1_{fb}"], [f"Aim{q}"],
                     bias=hin[1][:, j0 + k2:j0 + k2 + 1])
            js = slice(j0, j0 + 2)
            p_ = pp[q]
            TT("dve", p_[0][:, :, 0:L], Ab[q][:, 0:2, 0:L], W2re[:, js, 0:L], ALU.mult, [f"Are{q}", "W2re"], [f"pp{q}0"])
            TT("dve", p_[1][:, :, 0:L], Ab[q][:, 2:4, 0:L], W2im[:, js, 0:L], ALU.mult, [f"Aim{q}", "W2im"], [f"pp{q}1"])
            TT("dve", p_[2][:, :, 0:L], Ab[q][:, 0:2, 0:L], W2im[:, js, 0:L], ALU.mult, [f"Are{q}", "W2im"], [f"pp{q}2"])
            TT("dve", p_[3][:, :, 0:L], Ab[q][:, 2:4, 0:L], W2re[:, js, 0:L], ALU.mult, [f"Aim{q}", "W2re"], [f"pp{q}3"])
            TT("pool", Hb[q][:, 0:2, 0:L], p_[0][:, :, 0:L], p_[1][:, :, 0:L], ALU.subtract, [f"pp{q}0", f"pp{q}1"], [f"Hre{q}"])
            TT("pool", Hb[q][:, 2:4, 0:L], p_[2][:, :, 0:L], p_[3][:, :, 0:L], ALU.add, [f"pp{q}2", f"pp{q}3"], [f"Him{q}"])
            TT("pool", hout[0][:, js], p_[0][:, :, L - 1], p_[1][:, :, L - 1], ALU.subtract, [f"pp{q}0", f"pp{q}1"], [f"hc{1 - pi}0_{fb}"])
            TT("pool", hout[1][:, js], p_[2][:, :, L - 1], p_[3][:, :, L - 1], ALU.add, [f"pp{q}2", f"pp{q}3"], [f"hc{1 - pi}1_{fb}"])
            if hfb == 15:
                hot = [f"hc{1 - pi}0_{f_}" for f_ in range(8)]
                hot1 = [f"hc{1 - pi}1_{f_}" for f_ in range(8)]
                if kind == "own" and n == 15:
                    DMA("sp", hre_own.rearrange("(j q) -> q j", q=128), hout[0][:], reads=hot, out_store=True, nonc=True)
                    DMA("sp", him_own.rearrange("(j q) -> q j", q=128), hout[1][:], reads=hot1, out_store=True, nonc=True)
                if kind == "smp":
                    DMA("sp", hre_smp[n].rearrange("(j q) -> q j", q=128), hout[0][:], reads=hot, out_store=True, nonc=True)
                    DMA("sp", him_smp[n].rearrange("(j q) -> q j", q=128), hout[1][:], reads=hot1, out_store=True, nonc=True)

        def stage3(u):
            ti, kind, n, hfb, L, c0, tcol, ub, pi, hin, hout, fb, hh, q = uinfo(u)
            yT_ = yT2[ti % 2]
            yp = ti % 2
            for k2 in range(2):
                MM(py[:, 0:L], Cre_b[:, fb, hh * 2 + k2, :], Hb[q][:, k2, 0:L], hh == 0 and k2 == 0, False, ["Cre_b", f"Hre{q}"], ["py"])
            for k2 in range(2):
                MM(py[:, 0:L], Cimn_b[:, fb, hh * 2 + k2, :], Hb[q][:, 2 + k2, 0:L], False, hh == 1 and k2 == 1, ["Cimn_b", f"Him{q}"], ["py"])
            if hh == 1:
                STT("dve", yT_[:, fb, 0:L], uT_t[ub][:, fb, 0:L], d_t[:, fb:fb + 1], py[:, 0:L], ALU.mult, ALU.add,
                    [f"uT_t{ub}", "d_t", "py"], [f"yT{yp}_{fb}"])
            if hfb != 15:
                return
            ytoks = [f"yT{yp}_{f_}" for f_ in range(8)]
            TT("pool", g1[:, :, 0:L], yT_[:, :, 0:L], yT_[:, :, 0:L], ALU.mult, ytoks, ["g1"])
            TS("dve", g1[:, :, 0:L], g1[:, :, 0:L], 0.044715, 1.0, ALU.mult, ALU.add, ["g1"], ["g1"])
            TT("pool", g1[:, :, 0:L], g1[:, :, 0:L], yT_[:, :, 0:L], ALU.mult, ["g1"] + ytoks, ["g1"])
            ACTV(g1[:, :, 0:L], g1[:, :, 0:L], AF.Sigmoid, ["g1"], ["g1"], scale=1.5957691216057308)
            TT("dve", gyT[:, :, 0:L], yT_[:, :, 0:L], g1[:, :, 0:L], ALU.mult, ytoks + ["g1"], ["gyT"])
            CP("pool", gyT_b[:, :, 0:L], gyT[:, :, 0:L], ["gyT"], ["gyT_b"])
            for half in range(2):
                for fo4 in range(4):
                    fo = half * 4 + fo4
                    for fi_ in range(8):
                        MM(pglu[:, fo4, 0:L], wglu_b[:, fi_, fo * 128:(fo + 1) * 128], gyT_b[:, fi_, 0:L], fi_ == 0, fi_ == 7,
                           ["wglu_b", "gyT_b"], ["pglu"])
                fs_ = slice(half * 4, half * 4 + 4)
                ACTV(so[:, :, 0:L], pglu[:, :, 0:L], AF.Sigmoid, ["pglu"], ["so"])
                TT("dve", so[:, :, 0:L], so[:, :, 0:L], gyT[:, fs_, 0:L], ALU.mult, ["so", "gyT"], ["so"])
                sqc = 0 if kind == "own" else 32 * n
                ACTV(sq[:, fs_, sqc:sqc + L], so[:, :, 0:L], AF.Square, ["so"], [f"sq{half}"])
                TT("pool", mixT_t[:, fs_, 0:L], so[:, :, 0:L], gssm_t[:, fs_].unsqueeze(2).to_broadcast([128, 4, L]), ALU.mult,
                   ["so", "gssm_t"], [f"mixT_t{half}"])
            if kind == "own" or n == 3 or dbg_groups is not None:
                LL = 128 if (kind == "own" or dbg_groups is None) else 32
                rc = tcol if kind == "own" else 16
                for fo in range(8):
                    MM(prs[0:LL, :], sq[:, fo, 0:LL], ones_b[:, 0:1], fo == 0, fo == 7, ["sq0", "sq1", "ones_b"], ["prs"])
                ACTV(rs_t[0:LL], prs[0:LL, :], AF.Sqrt, ["prs"], ["rs_t"], bias=EPS, scale=1.0 / DSSM)
                RECIP(rstd_ssm[0:LL, rc:rc + 1], rs_t[0:LL], ["rs_t"], [f"rstd_ssm{rc}"])
            DMA("sp", mixT_scr[0:8, :, c0:c0 + L].rearrange("f p n -> p f n"), mixT_t[:, :, 0:L], reads=["mixT_t0", "mixT_t1"],
                writes=["mixT_scr_s"], nonc=(L < 128))

        NU = len(units)
        for step in range(NU + 2):
            if step < NU:
                stage1(step)
            if 0 <= step - 1 < NU:
                stage2(step - 1)
            if 0 <= step - 2 < NU:
                stage3(step - 2)
        P.flush()
        stk.pop()
        c1.close()
        stk.pop()
        ces.close()


    SCALE = float(DH ** -0.5)
    rss_att = sb("rss_att", [128, 20])
    if "D" in phases:
        dd = ExitStack()
        stk.append(dd)
        gatt_t = sb("gatt_t", [128, 8])
        DMA("sp", gatt_t[:], g_out_attn.rearrange("(fb q) -> q fb", q=128), writes=["gatt_t"], nonc=True)
        MEMSET("dve", rss_att[:], 0.0, ["rss_att"])
        biasT = sb("biasT", [128, 4, NH, NT])
        bias_s = sb("bias_s", [128, 4, NH, 16])
        bias_n = sb("bias_n", [32, 4, NH])
        d0 = ExitStack()
        stk.append(d0)
        c_all = sb("c_all", [128, NT, NH])
        tot = sb("tot", [128, NT, NH])
        carry = sb("carry", [128, NT, NH])
        cref = sb("cref", [128, 4, NH])
        pc = ps("pc", [128, 512])
        pt = ps("pt", [128, 512])
        lf2 = logf_all[:].rearrange("p t h -> p (t h)")
        MM(pc[:], triT_f[:], lf2, True, True, ["triT_f"], ["pc"])
        MM(pt[:], ones_f[:], lf2, True, True, ["ones_f"], ["pt"])
        CP("dve", tot[:].rearrange("p t h -> p (t h)"), pt[:], ["pt"], ["tot"])
        MEMSET("dve", carry[:, 0, :], 0.0, ["carry"])
        for T in range(1, NT):
            TT("dve", carry[:, T, :], carry[:, T - 1, :], tot[:, T - 1, :], ALU.add, ["carry", "tot"], ["carry"])
        TT("dve", c_all[:].rearrange("p t h -> p (t h)"), pc[:], carry[:].rearrange("p t h -> p (t h)"), ALU.add, ["pc", "carry"], ["c_all"])
        for qg in range(4):
            T = NPT + 4 * qg + 3
            TT("dve", cref[:, qg, :], carry[:, T, :], tot[:, T, :], ALU.add, ["carry", "tot"], ["cref"])
        for qg in range(4):
            for h in range(NH):
                TS("dve", biasT[:, qg, h, :], c_all[:, :, h], cref[:, qg, h:h + 1], -1.0, ALU.subtract, ALU.mult, ["c_all", "cref"], ["biasT"])
                TT("dve", biasT[:, qg, h, :], biasT[:, qg, h, :], kmask[:], ALU.add, ["biasT", "kmask"], ["biasT"])
        clf_sb = sb("clf_sb", [128, 4, 16, NH])
        cw_s = sb("cw_s", [128, 4, 16, NH])
        tot_s = sb("tot_s", [128, 4, 16, NH])
        car_s = sb("car_s", [128, 4, 16, NH])
        totn = sb("totn", [128, 4, NH])
        total = sb("total", [128, 4, NH])
        for s_ in range(4):
            DMA("sp", clf_sb[:, s_], clf[s_].rearrange("(t p) h -> p t h", p=128), writes=["clf_sb"], nonc=True)
        cl2 = clf_sb[:].rearrange("p s t h -> p (s t h)")
        MM(pc[:], triT_f[:], cl2, True, True, ["triT_f", "clf_sb"], ["pc"])
        MM(pt[:], ones_f[:], cl2, True, True, ["ones_f", "clf_sb"], ["pt"])
        CP("dve", cw_s[:].rearrange("p s t h -> p (s t h)"), pc[:], ["pc"], ["cw_s"])
        CP("dve", tot_s[:].rearrange("p s t h -> p (s t h)"), pt[:], ["pt"], ["tot_s"])
        MEMSET("dve", car_s[:, :, 0, :], 0.0, ["car_s"])
        for T in range(1, 16):
            TT("dve", car_s[:, :, T, :], car_s[:, :, T - 1, :], tot_s[:, :, T - 1, :], ALU.add, ["car_s", "tot_s"], ["car_s"])
        ln2 = logf_smp[:].rearrange("p s h -> p (s h)")
        MM(pc[0:32, 0:32], triT_f[0:32, 0:32], ln2, True, True, ["triT_f", "logf_smp"], ["pc"])
        MM(pt[:, 0:32], ones_f[0:32, :], ln2, True, True, ["ones_f", "logf_smp"], ["pt"])
        CP("dve", totn[:].rearrange("p s h -> p (s h)"), pt[:, 0:32], ["pt"], ["totn"])
        TT("dve", total[:], car_s[:, :, 15, :], tot_s[:, :, 15, :], ALU.add, ["car_s", "tot_s"], ["total"])
        TT("dve", total[:], total[:], totn[:], ALU.add, ["total", "totn"], ["total"])
        TT("dve", cw_s[:], cw_s[:], car_s[:], ALU.add, ["cw_s", "car_s"], ["cw_s"])
        for T in range(16):
            TT("dve", bias_s[:, :, :, T], total[:], cw_s[:, :, T, :], ALU.subtract, ["cw_s", "total"], ["bias_s"])
        TT("dve", bias_n[:].rearrange("p s h -> p (s h)"), totn[0:32].rearrange("p s h -> p (s h)"), pc[0:32, 0:32], ALU.subtract,
           ["pc", "totn"], ["bias_n"])
        P.flush()
        stk.pop()
        d0.close()

        d1 = ExitStack()
        stk.append(d1)
        KT_h = [sb(f"KT_h{i}", [128, NCTX], BF16) for i in range(2)]
        V_h = [sb(f"V_h{i}", [128, NT, DH], BF16) for i in range(2)]
        QT_h = [sb(f"QT_h{i}", [128, NOWN], BF16) for i in range(2)]
        Pt = [sb(f"Pt{i}", [128, 512], BF16) for i in range(3)]
        pS = [ps(f"pS{i}", [128, 512]) for i in range(3)]
        pO = [ps(f"pO{i}", [128, 512]) for i in range(2)]
        pD = [ps(f"pD{i}", [128, 512]) for i in range(2)]
        prs2 = ps("prs2", [128, 4])
        rden = [sb(f"rden{i}", [128, 512]) for i in range(2)]
        Psum = [sb(f"Psum{i}", [128, 512], BF16) for i in range(2)]
        at = [sb(f"at{i}", [128, 512]) for i in range(2)]
        sqa = [sb(f"sqa{i}", [128, 512], BF16) for i in range(2)]
        mixA = [sb(f"mixA{i}", [128, 512], BF16) for i in range(2)]
        heads = range(NH) if dbg_groups is None else range(1)
        qgs = range(4) if dbg_groups is None else range(1)
        sc = 0
        for hi, h in enumerate(heads):
            hb = hi % 2
            DMA("sp", KT_h[hb][:], kT_scr[h], writes=[f"KT_h{hb}"])
            DMA("sp", V_h[hb][:].rearrange("p t d -> p (t d)"), v_scr[h].rearrange("p t d -> p (t d)"), writes=[f"V_h{hb}"])
            DMA("sp", QT_h[hb][:], qT_scr[h], writes=[f"QT_h{hb}"])
            for qg in qgs:
                q0 = qg * 512
                ob = (hi * 4 + qg) % 2
                tl = [(T, 0) for T in range(NPT + 4 * qg)] + [(NPT + 4 * qg + m_, 128 * m_) for m_ in range(4)]
                n_t = len(tl)
                n_full = NPT + 4 * qg

                def emit_S(i):
                    T, off = tl[i]
                    b_ = (sc + i) % 3
                    MM(pS[b_][:, off:512], KT_h[hb][:, T * 128:(T + 1) * 128], QT_h[hb][:, q0 + off:q0 + 512], True, True,
                       [f"KT_h{hb}", f"QT_h{hb}"], [f"pS{b_}"])

                emit_S(0)
                emit_S(1)
                for i in range(n_t):
                    T, off = tl[i]
                    b_ = (sc + i) % 3
                    ACTV(Pt[b_][:, off:512], pS[b_][:, off:512], AF.Exp, [f"pS{b_}", "biasT"], [f"Pt{b_}"],
                         bias=biasT[:, qg, h, T:T + 1], scale=SCALE)
                    if T >= NPT + 4 * qg:
                        TT("pool", Pt[b_][:, off:off + 128], Pt[b_][:, off:off + 128], triT_b[:], ALU.mult, [f"Pt{b_}", "triT_b"], [f"Pt{b_}"])
                    if i + 2 < n_t:
                        emit_S(i + 2)
                    MM(pO[ob][:, off:512], V_h[hb][:, T, :], Pt[b_][:, off:512], i == 0, i == n_t - 1, [f"V_h{hb}", f"Pt{b_}"], [f"pO{ob}"])
                    if i < n_full:
                        if i % 2 == 1:
                            pb_ = (i // 2) % 2
                            bp_ = (sc + i - 1) % 3
                            TT("dve", Psum[pb_][:], Pt[bp_][:], Pt[b_][:], ALU.add, [f"Pt{bp_}", f"Pt{b_}"], [f"Psum{pb_}"])
                            MM(pD[ob][:, :], ones_b[:], Psum[pb_][:], i == 1, False, ["ones_b", f"Psum{pb_}"], [f"pD{ob}"])
                    else:
                        MM(pD[ob][:, off:512], ones_b[:], Pt[b_][:, off:512], False, i == n_t - 1, ["ones_b", f"Pt{b_}"], [f"pD{ob}"])
                sc += n_t
                RECIP(rden[ob][:], pD[ob][:], [f"pD{ob}"], [f"rden{ob}"])
                TT("dve", at[ob][:], pO[ob][:], rden[ob][:], ALU.mult, [f"pO{ob}", f"rden{ob}"], [f"at{ob}"])
                ACTV(sqa[ob][:], at[ob][:], AF.Square, [f"at{ob}"], [f"sqa{ob}"])
                TS("dve", mixA[ob][:], at[ob][:], gatt_t[:, h:h + 1], None, ALU.mult, None, [f"at{ob}", "gatt_t"], [f"mixA{ob}"])
                DMA("sp", mixT_scr[8 + h, :, q0:q0 + 512], mixA[ob][:], reads=[f"mixA{ob}"], writes=["mixT_scr_a"])
                for t4 in range(4):
                    MM(prs2[:, t4:t4 + 1], sqa[ob][:, t4 * 128:(t4 + 1) * 128], ones_b[:, 0:1], True, True, [f"sqa{ob}", "ones_b"], ["prs2"])
                TT("dve", rss_att[:, qg * 4:qg * 4 + 4], rss_att[:, qg * 4:qg * 4 + 4], prs2[:, 0:4], ALU.add, ["rss_att", "prs2"], ["rss_att"])
        P.flush()
        stk.pop()
        d1.close()

        d2 = ExitStack()
        stk.append(d2)
        kc_f = [sb(f"kc_f{i}", [128, 16, DH]) for i in range(2)]
        vc_f = [sb(f"vc_f{i}", [128, 16, DH]) for i in range(2)]
        kc_b = sb("kc_b", [128, 16, DH], BF16)
        vc_b = [sb(f"vc_b{i}", [128, 16, DH], BF16) for i in range(2)]
        KTc = [sb(f"KTc{i}", [128, 16, 128], BF16) for i in range(2)]
        ptK = [ps(f"ptK{i}", [128, 8, 128], BF16) for i in range(2)]
        pSs = ps("pSs", [128, 16, 32])
        pSn = ps("pSn", [32, 32])
        pOs = ps("pOs", [128, 32])
        pDs = ps("pDs", [128, 32])
        tmpS = sb("tmpS", [128, 16, 32])
        Pts = sb("Pts", [128, 16, 32], BF16)
        Ptn = sb("Ptn", [32, 32], BF16)
        rdn = sb("rdn", [128, 32])
        ats = sb("ats", [128, 32])
        sqs_all = sb("sqs_all", [128, NH, NSMP], BF16)
        prs4 = ps("prs4", [128, 1])
        mixS = sb("mixS", [128, NH, NSMP], BF16)
        ck4 = ck.rearrange("s (t p) h d -> s p t h d", p=128)
        cv4 = cv.rearrange("s (t p) h d -> s p t h d", p=128)
        sh_list = [(s_, h) for s_ in range(4) for h in range(NH)] if dbg_groups is None else [(0, 0)]
        def smp_A(ii):
            s_, h = sh_list[ii]
            b2 = ii % 2
            DMA("sp", kc_f[b2][:], ck4[s_, :, :, h, :], writes=[f"kc_f{b2}"])
            DMA("sp", vc_f[b2][:], cv4[s_, :, :, h, :], writes=[f"vc_f{b2}"])
            CP("dve", kc_b[:], kc_f[b2][:], [f"kc_f{b2}"], ["kc_b"])
            CP("act", vc_b[b2][:], vc_f[b2][:], [f"vc_f{b2}"], [f"vc_b{b2}"])
            for T in range(16):
                TR(ptK[T // 8][:, T % 8, :], kc_b[:, T, :], ident_b[:], ["kc_b", "ident_b"], [f"ptK{T // 8}"])
            CP("act", KTc[b2][:, 0:8, :], ptK[0][:], ["ptK0"], [f"KTc{b2}"])
            CP("dve", KTc[b2][:, 8:16, :], ptK[1][:], ["ptK1"], [f"KTc{b2}"])

        smp_A(0)
        for ii, (s_, h) in enumerate(sh_list):
            b2 = ii % 2
            if ii + 1 < len(sh_list):
                smp_A(ii + 1)
            qs = slice(32 * s_, 32 * s_ + 32)
            for T in range(16):
                MM(pSs[:, T, :], KTc[b2][:, T, :], qT_s[:, h, qs], True, True, [f"KTc{b2}", "qT_s"], ["pSs"])
            MM(pSn[:, :], kT_s[:, h, qs], qT_s[:, h, qs], True, True, ["kT_s", "qT_s"], ["pSn"])
            STT("dve", tmpS[:], pSs[:], SCALE, bias_s[:, s_, h, :].unsqueeze(2).to_broadcast([128, 16, 32]), ALU.mult, ALU.add,
                ["pSs", "bias_s"], ["tmpS"])
            ACTV(Pts[:], tmpS[:], AF.Exp, ["tmpS"], ["Pts"])
            ACTV(Ptn[:], pSn[:], AF.Exp, ["pSn", "bias_n"], ["Ptn"], bias=bias_n[:, s_, h:h + 1], scale=SCALE)
            TT("pool", Ptn[:], Ptn[:], triT_b[0:32, 0:32], ALU.mult, ["Ptn", "triT_b"], ["Ptn"])
            for T in range(16):
                MM(pOs[:], vc_b[b2][:, T, :], Pts[:, T, :], T == 0, False, [f"vc_b{b2}", "Pts"], ["pOs"])
            MM(pOs[:], v_s[:, h, s_, :], Ptn[:], False, True, ["v_s", "Ptn"], ["pOs"])
            for T in range(16):
                MM(pDs[:], ones_b[:], Pts[:, T, :], T == 0, False, ["ones_b", "Pts"], ["pDs"])
            MM(pDs[:], ones_b[0:32, :], Ptn[:], False, True, ["ones_b", "Ptn"], ["pDs"])
            RECIP(rdn[:], pDs[:], ["pDs"], ["rdn"])
            TT("dve", ats[:], pOs[:], rdn[:], ALU.mult, ["pOs", "rdn"], ["ats"])
            ACTV(sqs_all[:, h, qs], ats[:], AF.Square, ["ats"], ["sqs_all"])
            TS("dve", mixS[:, h, qs], ats[:], gatt_t[:, h:h + 1], None, ALU.mult, None, ["ats", "gatt_t"], ["mixS"])
        DMA("sp", mixT_scr[8:16, :, NOWN:NOWN + NSMP].rearrange("f p n -> p f n"), mixS[:], reads=["mixS"], writes=["mixT_scr_a"])
        if dbg_groups is None:
            for h in range(NH):
                MM(prs4[:, :], sqs_all[:, h, :], ones_b[:, 0:1], h == 0, h == NH - 1, ["sqs_all", "ones_b"], ["prs4"])
            CP("dve", rss_att[:, 16:17], prs4[:, :], ["prs4"], ["rss_att"])
        ACTV(rstd_att[:], rss_att[:], AF.Sqrt, ["rss_att"], ["rstd_att"], bias=EPS, scale=1.0 / DSSM)
        RECIP(rstd_att[:], rstd_att[:], ["rstd_att"], ["rstd_att"])
        P.flush()
        stk.pop()
        d2.close()
        stk.pop()
        dd.close()


    if "E" in phases:
        if conv_jobs:
            convert_E_weights()
            P.flush()
        ee = ExitStack()
        stk.append(ee)
        m = linear_machinery(nw=2, nh=1, with_xt=False)
        hT, wbuf, pmm = m.hT, m.wbuf, m.pmm
        pq = [ps(f"pq{i}", [128, 512]) for i in range(4)]
        x1 = sb("x1", [128, 4, D])
        aT = sb("aT", [128, 64, 512], BF16)
        mixT = sb("mixT", [128, 16, 512], BF16)
        rr = [sb(f"rr{i}", [128, 512]) for i in range(2)]
        yb = [sb(f"yb{i}", [128, 512]) for i in range(2)]
        if dbg_groups is None:
            egroups = [("own", g_) for g_ in range(4)] + [("smp", 0)]
        else:
            egroups = [("own", 0), ("smp", 0)]
        cnt = 0
        eloads = []
        for _ in egroups:
            eloads += [(scr_wout[c], f"scr_wout{c}") for c in range(4)]
            eloads += [(scr_wup[c], f"scr_wup{c}") for c in range(16)]
            eloads += [(scr_wdn[d_, f_], f"scr_wdn{d_}_{f_}") for d_ in range(4) for f_ in range(4)]
        ewq = []
        epos = [0]

        def e_next_chunk():
            if not ewq:
                ewq.append(load_wchunk(m, *eloads[epos[0]]))
                epos[0] += 1
            wi_ = ewq.pop(0)
            if epos[0] < len(eloads):
                ewq.append(load_wchunk(m, *eloads[epos[0]]))
                epos[0] += 1
            return wi_

        for kind, g_ in egroups:
            L = 128
            ntl = 4 if kind == "own" else 1
            ncol = ntl * L
            c0 = g_ * 512 if kind == "own" else NOWN
            DMA("sp", mixT[:, :, 0:ncol], mixT_scr[:, :, c0:c0 + ncol].rearrange("f p n -> p f n"), writes=["mixT"], nonc=(ncol < 512))
            for tt in range(ntl):
                src = x_ctx[NPRE + c0 + tt * 128:NPRE + c0 + (tt + 1) * 128, :] if kind == "own" else x_smp[0:128, :]
                DMA("sp", x1[0:L, tt, :], src, writes=[f"x1_{tt}"])
            for c in range(4):
                wi = e_next_chunk()
                cs = slice(c * 512, (c + 1) * 512)
                for tt in range(ntl):
                    tcol = (g_ * 4 + tt) if kind == "own" else 16
                    pa, pb = pq[(cnt % 2) * 2], pq[(cnt % 2) * 2 + 1]
                    ta, tb = f"pq{(cnt % 2) * 2}", f"pq{(cnt % 2) * 2 + 1}"
                    cnt += 1
                    for ft in range(8):
                        MM(pa[0:L, :], mixT[:, ft, tt * L:(tt + 1) * L], wbuf[wi][:, ft, :], ft == 0, ft == 7, ["mixT", f"wbuf{wi}"], [ta])
                    for ft in range(8, 16):
                        MM(pb[0:L, :], mixT[:, ft, tt * L:(tt + 1) * L], wbuf[wi][:, ft, :], ft == 8, ft == 15, ["mixT", f"wbuf{wi}"], [tb])
                    STT("dve", x1[0:L, tt, cs], pa[0:L, :], rstd_ssm[0:L, tcol:tcol + 1], x1[0:L, tt, cs], ALU.mult, ALU.add,
                        [ta, f"x1_{tt}"], [f"x1_{tt}"])
                    STT("dve", x1[0:L, tt, cs], pb[0:L, :], rstd_att[0:L, tcol:tcol + 1], x1[0:L, tt, cs], ALU.mult, ALU.add,
                        [tb, f"x1_{tt}"], [f"x1_{tt}"])
            for tt in range(ntl):
                front_end(m, None, x1[:, tt, :], f"x1_{tt}", L, gmlpT, "gmlpT", 0, tt * L, f"hT0_{tt}")
            htoks = [f"hT0_{tt}" for tt in range(ntl)]
            for ffc in range(16):
                wi = e_next_chunk()
                for f4 in range(4):
                    fft = ffc * 4 + f4
                    j = fft % 2
                    for kt in range(KT):
                        MM(pmm[j][:, 0:ncol], wbuf[wi][:, kt, f4 * 128:(f4 + 1) * 128], hT[0][:, kt, 0:ncol], kt == 0, kt == KT - 1,
                           htoks + [f"wbuf{wi}"], [f"pmm{j}"])
                    ACTV(rr[j][:, 0:ncol], pmm[j][:, 0:ncol], AF.Relu, [f"pmm{j}"], [f"rr{j}"])
                    TT("pool" if fft % 4 == 3 else "dve", aT[:, fft, 0:ncol], rr[j][:, 0:ncol], rr[j][:, 0:ncol], ALU.mult, [f"rr{j}"], [f"aT{fft}"])
            for dmc in range(4):
                ds_ = slice(dmc * 512, (dmc + 1) * 512)
                for fq in range(4):
                    wi = e_next_chunk()
                    for tt in range(ntl):
                        for kt in range(KT):
                            fft = fq * 16 + kt
                            MM(pq[tt][0:L, :], aT[:, fft, tt * L:(tt + 1) * L], wbuf[wi][:, kt, :], fq == 0 and kt == 0, fq == 3 and kt == KT - 1,
                               [f"aT{fft}", f"wbuf{wi}"], [f"pq{tt}"])
                for tt in range(ntl):
                    j = tt % 2
                    TT("dve", yb[j][0:L], pq[tt][0:L, :], x1[0:L, tt, ds_], ALU.add, [f"pq{tt}", f"x1_{tt}"], [f"yb{j}"])
                    if kind == "own":
                        dst = y_own[c0 + tt * 128:c0 + (tt + 1) * 128, ds_]
                    else:
                        dst = y_smp[0:128, ds_]
                    DMA("sp", dst, yb[j][0:L], reads=[f"yb{j}"], out_store=True)
        P.flush()
        stk.pop()
        ee.close()

    if P.ops:
        P.flush()
    es.close()
    return nc, P


def _host_consts():
    ident = np.eye(128, dtype=np.float32)
    s = np.arange(128)
    triT = (s[:, None] <= s[None, :]).astype(np.float32)
    iota1 = np.broadcast_to((np.arange(128) + 1).astype(np.float32)[None, :], (128, 128)).copy()
    sp1 = (np.arange(128) + 1).astype(np.float32)[:, None].copy()
    return ident, triT, iota1, sp1


def _prep_inputs(inp):
    f32 = np.float32
    ident, triT, iota1, sp1 = _host_consts()
    b_re = np.asarray(inp["ssm_b_re"], f32)[0]
    b_im = np.asarray(inp["ssm_b_im"], f32)[0]
    c_re = np.asarray(inp["ssm_c_re"], f32)[0]
    c_im = np.asarray(inp["ssm_c_im"], f32)[0]

    def bblk(b):
        out = np.zeros((128, 8, 512), f32)
        for fb in range(8):
            for gl in range(8):
                g = fb * 8 + gl
                out[gl * 16:(gl + 1) * 16, fb, gl * 64:(gl + 1) * 64] = b[g].T
        return out

    def cblk(c):
        out = np.zeros((128, 8, 4, 128), f32)
        for fb in range(8):
            for k in range(4):
                for e in range(2):
                    g = fb * 8 + 2 * k + e
                    out[e * 64:(e + 1) * 64, fb, k, 16 * (2 * k + e):16 * (2 * k + e + 1)] = c[g].T
        return out

    def bpad(b):
        out = np.zeros((128, 32, 32), f32)
        for j in range(32):
            for e in range(2):
                out[e * 64:(e + 1) * 64, j, e * 16:(e + 1) * 16] = b[2 * j + e]
        return out

    shared = {
        "g_norm_mix": np.asarray(inp["g_norm_mix"], f32)[0],
        "w_in": np.asarray(inp["w_in"], f32)[0],
        "b_f": np.asarray(inp["b_f"], f32),
        "ssm_a_re": np.asarray(inp["ssm_a_re"], f32).reshape(1, G * PST),
        "ssm_a_im": np.asarray(inp["ssm_a_im"], f32).reshape(1, G * PST),
        "ssm_log_step": np.asarray(inp["ssm_log_step"], f32).reshape(1, G),
        "bblk_re": bblk(b_re), "bblk_im": bblk(b_im),
        "cblk_re": cblk(c_re), "cblk_im": cblk(c_im),
        "bpad_re": bpad(b_re), "bpad_im": bpad(b_im),
        "ssm_d": np.asarray(inp["ssm_d"], f32).reshape(G * 16),
        "w_glu": np.asarray(inp["w_glu"], f32)[0],
        "g_q": np.asarray(inp["g_q"], f32), "g_k": np.asarray(inp["g_k"], f32),
        "g_out_ssm": np.asarray(inp["g_out_ssm"], f32)[0],
        "g_out_attn": np.asarray(inp["g_out_attn"], f32)[0],
        "w_out": np.asarray(inp["w_out"], f32)[0],
        "g_norm_mlp": np.asarray(inp["g_norm_mlp"], f32)[0],
        "w_up": np.asarray(inp["w_up"], f32)[0],
        "w_down": np.asarray(inp["w_down"], f32)[0],
        "c_ident": ident, "c_triT": triT, "c_iota1": iota1, "c_sp1": sp1,
    }
    xp = np.asarray(inp["x_prompt"], f32)
    xs = np.asarray(inp["x_sample"], f32)
    cks = np.asarray(inp["cache_k"], f32)[0]
    cvs = np.asarray(inp["cache_v"], f32)[0]
    clfs = np.asarray(inp["cache_logf"], f32)[0]
    sres = np.asarray(inp["state_ssm_re"], f32)[0].reshape(32, G * PST)
    sims = np.asarray(inp["state_ssm_im"], f32)[0].reshape(32, G * PST)
    maps = []
    for c in range(8):
        b, j = c // 4, c % 4
        x_ctx = np.zeros((NCTX, D), f32)
        nv = (3 - j) * NOWN
        x_ctx[nv:] = xp[b, 0:(j + 1) * NOWN]
        km = np.zeros((NCTX,), f32)
        km[:nv] = -30000.0
        m = dict(shared)
        m.update({
            "x_ctx": x_ctx,
            "x_smp": np.ascontiguousarray(xs[4 * c:4 * c + 4].reshape(NSMP, D)),
            "ck": np.ascontiguousarray(cks[4 * c:4 * c + 4]),
            "cv": np.ascontiguousarray(cvs[4 * c:4 * c + 4]),
            "clf": np.ascontiguousarray(clfs[4 * c:4 * c + 4]),
            "sre": np.ascontiguousarray(sres[4 * c:4 * c + 4]),
            "sim": np.ascontiguousarray(sims[4 * c:4 * c + 4]),
            "c_kmask": np.ascontiguousarray(km.reshape(NT, 128).T),
        })
        maps.append(m)
    return maps


_CACHE = {}


def kernel(**inputs):
    if "nc" not in _CACHE:
        _CACHE["nc"] = build_program()[0]
    nc = _CACHE["nc"]
    maps = _prep_inputs(inputs)
    res = run_bass_kernel_spmd(nc, maps, core_ids=list(range(8)))
    R = res.results
    f32 = np.float32
    y_p = np.zeros((2, 8192, D), f32)
    k_p = np.zeros((1, 2, 8192, NH, DH), f32)
    v_p = np.zeros((1, 2, 8192, NH, DH), f32)
    lf_p = np.zeros((1, 2, 8192, NH), f32)
    hre_p = np.zeros((1, 2, G, PST), f32)
    him_p = np.zeros((1, 2, G, PST), f32)
    y_s = np.zeros((32, 32, D), f32)
    k_s = np.zeros((1, 32, 32, NH, DH), f32)
    v_s = np.zeros((1, 32, 32, NH, DH), f32)
    lf_s = np.zeros((1, 32, 32, NH), f32)
    hre_s = np.zeros((1, 32, G, PST), f32)
    him_s = np.zeros((1, 32, G, PST), f32)
    for c in range(8):
        b, j = c // 4, c % 4
        r = R[c]
        sl = slice(j * NOWN, (j + 1) * NOWN)
        y_p[b, sl] = r["y_own"]
        k_p[0, b, sl] = r["k_own"].reshape(NOWN, NH, DH)
        v_p[0, b, sl] = r["v_own"].reshape(NOWN, NH, DH)
        lf_p[0, b, sl] = r["lf_own"]
        if j == 3:
            hre_p[0, b] = r["hre_own"].reshape(G, PST)
            him_p[0, b] = r["him_own"].reshape(G, PST)
        s4 = slice(4 * c, 4 * c + 4)
        y_s[s4] = r["y_smp"].reshape(4, 32, D)
        k_s[0, s4] = r["k_smp"].reshape(4, 32, NH, DH)
        v_s[0, s4] = r["v_smp"].reshape(4, 32, NH, DH)
        lf_s[0, s4] = r["lf_smp"].reshape(4, 32, NH)
        hre_s[0, s4] = r["hre_smp"].reshape(4, G, PST)
        him_s[0, s4] = r["him_smp"].reshape(4, G, PST)
    return (y_p, y_s, k_p, v_p, lf_p, hre_p, him_p, k_s, v_s, lf_s, hre_s, him_s)
```

```python
import numpy as np
from contextlib import ExitStack
import concourse.bass as bass
import concourse.mybir as mybir
from concourse.bass_utils import run_bass_kernel_spmd

F32 = mybir.dt.float32
BF16 = mybir.dt.bfloat16
I32 = mybir.dt.int32
AF = mybir.ActivationFunctionType
ALU = mybir.AluOpType
AX = mybir.AxisListType

ENGINES = ("pe", "act", "dve", "pool", "sp")
WAR_GUARD = True
QS = ("sp", "pool", "act")


class Op:
    __slots__ = ("eng", "fn", "reads", "writes", "dma", "deps", "ticket", "has_dep",
                 "dsem", "dval", "prev_same_sem", "out_store")

    def __init__(self, eng, fn, reads, writes, dma=False, out_store=False):
        self.eng = eng
        self.fn = fn
        self.reads = tuple(reads)
        self.writes = tuple(writes)
        self.dma = dma
        self.deps = ()
        self.ticket = None
        self.has_dep = False
        self.dsem = None
        self.dval = None
        self.prev_same_sem = None
        self.out_store = out_store


class Prog:
    def __init__(self, nc, esems, dsems, n_dma_sems):
        self.nc = nc
        self.ops = []
        self.n = n_dma_sems
        self.esems = esems
        self.dsems = dsems
        self.cnt = {e: 0 for e in ENGINES}
        self.k = {q: 0 for q in QS}
        self.semcnt = [0] * (n_dma_sems * len(QS))
        self.nseg = 0
        self.tot_ops = 0
        self.psum_tokens = set()

    def add(self, eng, fn, reads=(), writes=(), dma=False, out_store=False):
        xs = [r + "#x" for r in reads if r in self.psum_tokens]
        if xs:
            reads = tuple(reads) + tuple(xs)
            writes = tuple(writes) + tuple(xs)
        self.ops.append(Op(eng, fn, reads, writes, dma, out_store))

    def pe(self, fn, reads=(), writes=()):
        self.add("pe", fn, reads, writes)

    def act(self, fn, reads=(), writes=()):
        self.add("act", fn, reads, writes)

    def dve(self, fn, reads=(), writes=()):
        self.add("dve", fn, reads, writes)

    def pool(self, fn, reads=(), writes=()):
        self.add("pool", fn, reads, writes)

    def dma(self, q, fn, reads=(), writes=(), out_store=False):
        self.add(q, fn, reads, writes, dma=True, out_store=out_store)

    def _resolve(self):
        last_w = {}
        readers = {}
        ops = self.ops
        eng_ops = {e: [] for e in ENGINES}
        eng_pos = {}
        for i, op in enumerate(ops):
            if not op.dma:
                eng_pos[i] = len(eng_ops[op.eng])
                eng_ops[op.eng].append(i)
        for i, op in enumerate(ops):
            deps = set()
            for r in op.reads:
                w = last_w.get(r)
                if w is not None:
                    deps.add(w)
            for w_ in op.writes:
                w = last_w.get(w_)
                if w is not None:
                    deps.add(w)
                rl = readers.get(w_, ())
                lastc = {}
                for rd in rl:
                    if ops[rd].dma:
                        deps.add(rd)
                    else:
                        lastc[ops[rd].eng] = rd
                for e_, rd in lastc.items():
                    if WAR_GUARD and e_ != op.eng and w_ in self.psum_tokens:
                        lst = eng_ops[e_]
                        pos = eng_pos[rd]
                        if pos + 1 < len(lst) and lst[pos + 1] < i:
                            rd = lst[pos + 1]
                    deps.add(rd)
            deps.discard(i)
            if op.eng == "pe" and not op.dma:
                deps = {d for d in deps if not (ops[d].eng == "pe" and not ops[d].dma)}
            op.deps = tuple(sorted(deps))
            for d in op.deps:
                ops[d].has_dep = True
            for r in op.reads:
                readers.setdefault(r, []).append(i)
            for w_ in op.writes:
                last_w[w_] = i
                readers[w_] = []
        last = {}
        for op in ops:
            if not op.dma:
                last[op.eng] = op
        for op in last.values():
            op.has_dep = True
        for op in ops:
            if not op.dma and op.has_dep:
                self.cnt[op.eng] += 1
                op.ticket = self.cnt[op.eng]
        n = self.n
        for op in ops:
            if op.dma:
                qi = QS.index(op.eng)
                s = qi * n + (self.k[op.eng] % n)
                self.k[op.eng] += 1
                op.prev_same_sem = self.semcnt[s]
                self.semcnt[s] += 16
                op.dsem = s
                op.dval = self.semcnt[s]

    def flush(self):
        self._resolve()
        ops = self.ops
        esems, dsems = self.esems, self.dsems
        per = {e: [] for e in ENGINES}
        for i, op in enumerate(ops):
            per[op.eng].append(i)
        fin_e = dict(self.cnt)
        fin_d = list(self.semcnt)
        seg = self.nseg

        def run(eng_name, eng):
            seen_e = {e: 0 for e in ENGINES}
            seen_d = [0] * len(dsems)
            for i in per[eng_name]:
                op = ops[i]
                for d in op.deps:
                    dop = ops[d]
                    if dop.dma:
                        if seen_d[dop.dsem] < dop.dval:
                            eng.wait_ge(dsems[dop.dsem], dop.dval)
                            seen_d[dop.dsem] = dop.dval
                    else:
                        if seen_e[dop.eng] < dop.ticket:
                            eng.wait_ge(esems[dop.eng], dop.ticket)
                            seen_e[dop.eng] = dop.ticket
                if op.dma:
                    if op.prev_same_sem > 0 and seen_d[op.dsem] < op.prev_same_sem:
                        eng.wait_ge(dsems[op.dsem], op.prev_same_sem)
                        seen_d[op.dsem] = op.prev_same_sem
                    ins = op.fn(eng)
                    ins.then_inc(dsems[op.dsem], 16)
                else:
                    ins = op.fn(eng)
                    if op.ticket is not None:
                        ins.then_inc(esems[op.eng], 1)
            for e2 in ("pe", "act", "dve", "pool"):
                if fin_e[e2] > 0 and seen_e[e2] < fin_e[e2]:
                    eng.wait_ge(esems[e2], fin_e[e2])
            for si, v in enumerate(fin_d):
                if v > 0 and seen_d[si] < v:
                    eng.wait_ge(dsems[si], v)

        with self.nc.Block(f"seg{seg}") as block:
            @block.sync
            def _(e):
                run("sp", e)

            @block.tensor
            def _(e):
                run("pe", e)

            @block.scalar
            def _(e):
                run("act", e)

            @block.vector
            def _(e):
                run("dve", e)

            @block.gpsimd
            def _(e):
                run("pool", e)
        self.tot_ops += len(ops)
        self.ops = []
        self.nseg += 1


D = 2048
KT = 16
DIN = 4104
NH = 8
DH = 128
DSSM = 1024
G = 64
PST = 64
DFF = 8192
NCTX = 8192
NPRE = 6144
NOWN = 2048
NT = 64
NPT = 48
NSMP = 128
PAST = 2048
EPS = 1e-6
NTOK = NOWN + NSMP

PHASES = ("B", "C", "D", "E")


def build_program(phases=PHASES, dbg_groups=None, dbg_chunks=None):
    nc = bass.Bass("TRN2", target_bir_lowering=False)
    es = ExitStack()

    def din(name, shape, dt=F32):
        return nc.dram_tensor(name, list(shape), dt, kind="ExternalInput").ap()

    def dout(name, shape, dt=F32):
        return nc.dram_tensor(name, list(shape), dt, kind="ExternalOutput").ap()

    def dscr(name, shape, dt=BF16):
        return nc.dram_tensor(name, list(shape), dt).ap()

    x_ctx = din("x_ctx", [NCTX, D])
    x_smp = din("x_smp", [NSMP, D])
    ck = din("ck", [4, PAST, NH, DH])
    cv = din("cv", [4, PAST, NH, DH])
    clf = din("clf", [4, PAST, NH])
    sre = din("sre", [4, G * PST])
    sim = din("sim", [4, G * PST])
    g_norm_mix = din("g_norm_mix", [D])
    w_in = din("w_in", [D, DIN])
    b_f = din("b_f", [1, NH])
    a_re = din("ssm_a_re", [1, G * PST])
    a_im = din("ssm_a_im", [1, G * PST])
    log_step = din("ssm_log_step", [1, G])
    bblk_re = din("bblk_re", [128, 8, 512])
    bblk_im = din("bblk_im", [128, 8, 512])
    cblk_re = din("cblk_re", [128, 8, 4, 128])
    cblk_im = din("cblk_im", [128, 8, 4, 128])
    bpad_re = din("bpad_re", [128, 32, 32])
    bpad_im = din("bpad_im", [128, 32, 32])
    ssm_d = din("ssm_d", [G * 16])
    w_glu = din("w_glu", [DSSM, DSSM])
    g_q = din("g_q", [1, DH])
    g_k = din("g_k", [1, DH])
    g_out_ssm = din("g_out_ssm", [DSSM])
    g_out_attn = din("g_out_attn", [DSSM])
    w_out = din("w_out", [D, D])
    g_norm_mlp = din("g_norm_mlp", [D])
    w_up = din("w_up", [D, DFF])
    w_down = din("w_down", [DFF, D])
    c_ident = din("c_ident", [128, 128])
    c_triT = din("c_triT", [128, 128])
    c_kmask = din("c_kmask", [128, NT])
    c_iota1 = din("c_iota1", [128, 128])
    c_sp1 = din("c_sp1", [128, 1])

    y_own = dout("y_own", [NOWN, D])
    y_smp = dout("y_smp", [NSMP, D])
    k_own = dout("k_own", [NOWN, NH * DH])
    v_own = dout("v_own", [NOWN, NH * DH])
    lf_own = dout("lf_own", [NOWN, NH])
    hre_own = dout("hre_own", [G * PST])
    him_own = dout("him_own", [G * PST])
    k_smp = dout("k_smp", [NSMP, NH * DH])
    v_smp = dout("v_smp", [NSMP, NH * DH])
    lf_smp = dout("lf_smp", [NSMP, NH])
    hre_smp = dout("hre_smp", [4, G * PST])
    him_smp = dout("him_smp", [4, G * PST])

    scr_win = dscr("scr_win", [8, 128, KT, 512])
    scr_wout = dscr("scr_wout", [4, 128, KT, 512])
    scr_wup = dscr("scr_wup", [16, 128, KT, 512])
    scr_wdn = dscr("scr_wdn", [4, 4, 128, KT, 512])
    kT_scr = dscr("kT_scr", [NH, 128, NCTX])
    v_scr = dscr("v_scr", [NH, 128, NT, DH])
    qT_scr = dscr("qT_scr", [NH, 128, NOWN])
    uT_scr = dscr("uT_scr", [8, 128, NTOK])
    u_scr = dscr("u_scr", [NPT, 128, DSSM])
    mixT_scr = dscr("mixT_scr", [16, 128, NTOK])

    stk = [es]

    uid = [0]

    def sb(name, shape, dt=F32):
        uid[0] += 1
        return stk[-1].enter_context(nc.sbuf_tensor(f"{name}_{uid[0]}", list(shape), dt))

    def ps(name, shape, dt=F32, toks=None):
        P.psum_tokens.update(toks if toks is not None else [name])
        uid[0] += 1
        return stk[-1].enter_context(nc.psum_tensor(f"{name}_{uid[0]}", list(shape), dt))

    NDS = 16
    sems = [es.enter_context(nc.semaphore(f"s{i}")) for i in range(4 + NDS * len(QS))]
    esems = {"pe": sems[0], "act": sems[1], "dve": sems[2], "pool": sems[3]}
    P = Prog(nc, esems, sems[4:], NDS)

    def DMA(q, out, in_, reads=(), writes=(), out_store=False, nonc=False):
        if nonc:
            def fn(e, out=out, in_=in_):
                with nc.allow_non_contiguous_dma(reason="small strided table"):
                    return e.dma_start(out=out, in_=in_)
        else:
            def fn(e, out=out, in_=in_):
                return e.dma_start(out=out, in_=in_)
        P.dma(q, fn, reads=reads, writes=writes, out_store=out_store)

    ident_f = sb("ident_f", [128, 128])
    ident_b = sb("ident_b", [128, 128], BF16)
    triT_f = sb("triT_f", [128, 128])
    triT_b = sb("triT_b", [128, 128], BF16)
    ones_b = sb("ones_b", [128, 128], BF16)
    ones_f = sb("ones_f", [128, 128])
    kmask = sb("kmask", [128, NT])
    gmixT = sb("gmixT", [128, KT])
    gmlpT = sb("gmlpT", [128, KT])
    gq_bc = sb("gq_bc", [128, DH])
    gk_bc = sb("gk_bc", [128, DH])
    bf_bc = sb("bf_bc", [128, NH])
    wf_b = sb("wf_b", [128, KT, NH], BF16)
    logf_all = sb("logf_all", [128, NT, NH])
    logf_smp = sb("logf_smp", [32, 4, NH])
    kT_s = sb("kT_s", [128, NH, NSMP], BF16)
    qT_s = sb("qT_s", [128, NH, NSMP], BF16)
    v_s = sb("v_s", [32, NH, 4, DH], BF16)

    DMA("sp", ident_f[:], c_ident, writes=["ident_f"])
    DMA("sp", triT_f[:], c_triT, writes=["triT_f"])
    DMA("sp", kmask[:], c_kmask, writes=["kmask"])
    DMA("sp", gmixT[:], g_norm_mix.rearrange("(kt p) -> p kt", p=128), writes=["gmixT"], nonc=True)
    DMA("sp", gmlpT[:], g_norm_mlp.rearrange("(kt p) -> p kt", p=128), writes=["gmlpT"], nonc=True)
    DMA("sp", gq_bc[:], g_q.broadcast_to([128, DH]), writes=["gq_bc"])
    DMA("sp", gk_bc[:], g_k.broadcast_to([128, DH]), writes=["gk_bc"])
    DMA("sp", bf_bc[:], b_f.broadcast_to([128, NH]), writes=["bf_bc"])
    P.dve(lambda e: e.tensor_copy(ident_b[:], ident_f[:]), reads=["ident_f"], writes=["ident_b"])
    P.dve(lambda e: e.tensor_copy(triT_b[:], triT_f[:]), reads=["triT_f"], writes=["triT_b"])
    P.dve(lambda e: e.memset(ones_b[:], 1.0), writes=["ones_b"])
    P.dve(lambda e: e.memset(ones_f[:], 1.0), writes=["ones_f"])

    def conv_chunk(dst, src_w, c0, tok):
        DMA("pool", dst, src_w[:, c0:c0 + 512].rearrange("(kt p) n -> p kt n", p=128), writes=[tok])

    WIN_CH = {"u0": 0, "u1": 1, "q0": 2, "q1": 3, "k0": 4, "k1": 5, "v0": 6, "v1": 7}
    DMA("pool", wf_b[:], w_in[:, 4096:4104].rearrange("(kt p) n -> p kt n", p=128), writes=["wf_b"], nonc=True)
    conv_jobs = []
    for ci in range(4):
        conv_jobs.append((scr_wout[ci], w_out[:, ci * 512:(ci + 1) * 512].rearrange("(kt p) n -> p kt n", p=128), f"scr_wout{ci}"))
    for ci in range(16):
        conv_jobs.append((scr_wup[ci], w_up[:, ci * 512:(ci + 1) * 512].rearrange("(kt p) n -> p kt n", p=128), f"scr_wup{ci}"))
    for dmc in range(4):
        for fq in range(4):
            conv_jobs.append((scr_wdn[dmc, fq],
                              w_down[fq * 2048:(fq + 1) * 2048, dmc * 512:(dmc + 1) * 512].rearrange("(kt p) n -> p kt n", p=128),
                              f"scr_wdn{dmc}_{fq}"))

    def convert_E_weights(n=None):
        k_ = 0
        while conv_jobs and (n is None or k_ < n):
            dst_, src_, tok_ = conv_jobs.pop(0)
            DMA("pool", dst_, src_, writes=[tok_])
            k_ += 1

    conv_done = [False]

    def MM(out, lhsT, rhs, start, stop, reads, writes):
        P.pe(lambda e: e.matmul(out, lhsT, rhs, start=start, stop=stop), reads, writes)

    def TR(out, in_, idn, reads, writes):
        P.pe(lambda e: e.transpose(out, in_, idn), reads, writes)

    def ACTV(out, in_, func, reads, writes, **kw):
        P.act(lambda e: e.activation(out, in_, func, **kw), reads, writes)

    def ENG(eng):
        return {"dve": P.dve, "pool": P.pool, "act": P.act}[eng]

    def TT(eng, out, in0, in1, op, reads, writes):
        ENG(eng)(lambda e: e.tensor_tensor(out, in0, in1, op), reads, writes)

    def TS(eng, out, in0, s1, s2, op0, op1, reads, writes):
        if op1 is None:
            ENG(eng)(lambda e: e.tensor_scalar(out, in0, s1, None, op0), reads, writes)
        else:
            ENG(eng)(lambda e: e.tensor_scalar(out, in0, s1, s2, op0, op1), reads, writes)

    def STT(eng, out, in0, scalar, in1, op0, op1, reads, writes):
        ENG(eng)(lambda e: e.scalar_tensor_tensor(out, in0, scalar, in1, op0, op1), reads, writes)

    def CP(eng, out, in_, reads, writes):
        if eng == "act":
            P.act(lambda e: e.copy(out, in_), reads, writes)
        else:
            ENG(eng)(lambda e: e.tensor_copy(out, in_), reads, writes)

    def RED(out, in_, reads, writes):
        P.dve(lambda e: e.tensor_reduce(out, in_, AX.X, ALU.add), reads, writes)

    def RECIP(out, in_, reads, writes):
        P.dve(lambda e: e.reciprocal(out, in_), reads, writes)

    def MEMSET(eng, ap, val, writes):
        ENG(eng)(lambda e: e.memset(ap, val), (), writes)

    def h4(ap):
        return ap.rearrange("p (h d) -> p h d", h=4)

    a0 = ExitStack()
    stk.append(a0)
    stg_f = [sb(f"stg_f{i}", [128, 2, 512]) for i in range(4)]
    stg_b = [sb(f"stg_b{i}", [128, KT, 512], BF16) for i in range(2)]
    sc_ = 0
    for n_, nm in enumerate(["u0", "u1", "k0", "k1", "v0", "v1", "q0", "q1"]):
        ci = WIN_CH[nm]
        bb = n_ % 2
        for k2 in range(KT // 2):
            fb_ = sc_ % 4
            sc_ += 1
            DMA("sp", stg_f[fb_][:], w_in[k2 * 256:(k2 + 1) * 256, ci * 512:(ci + 1) * 512].rearrange("(kt p) n -> p kt n", p=128),
                writes=[f"stg_f{fb_}"])
            CP("act" if k2 % 2 == 0 else "dve", stg_b[bb][:, 2 * k2:2 * k2 + 2, :], stg_f[fb_][:], [f"stg_f{fb_}"], [f"stg_b{bb}"])
        DMA("sp", scr_win[ci], stg_b[bb][:], reads=[f"stg_b{bb}"], writes=[f"scr_win{ci}"])
    P.flush()
    stk.pop()
    a0.close()

    class LM:
        pass

    def linear_machinery(nw=3, nh=2, with_xt=True, npm=2):
        m = LM()
        m.nw = nw
        m.wbuf = [sb(f"wbuf{i}", [128, KT, 512], BF16) for i in range(nw)]
        m.hT = [sb(f"hT{i}", [128, KT, 512], BF16) for i in range(nh)]
        m.xt = [sb(f"xt{i}", [128, D]) for i in range(2)] if with_xt else None
        m.xn = [sb(f"xn{i}", [128, D], BF16) for i in range(2)]
        m.ss = [sb(f"ss{i}", [128, 1]) for i in range(2)]
        m.rstd = [sb(f"rstd{i}", [128, 1]) for i in range(2)]
        m.pxT = ps("pxT", [128, KT, 128], BF16, toks=["pxT0", "pxT1"])
        m.pmm = [ps(f"pmm{i}", [128, 512]) for i in range(npm)]
        m.wcount = 0
        m.fe_count = 0
        return m

    def load_wchunk(m, src_ap, src_tok):
        i = m.wcount % m.nw
        m.wcount += 1
        DMA("sp", m.wbuf[i][:], src_ap, reads=[src_tok], writes=[f"wbuf{i}"])
        return i

    def front_end(m, src_dram, src_sb, src_tok, L, gT, gT_tok, hbuf, col0, htok, defer=False):
        b = m.fe_count % 2
        m.fe_count += 1
        xt, xn, ss, rstd, pxT, hT = m.xt, m.xn, m.ss, m.rstd, m.pxT, m.hT
        if src_dram is not None:
            DMA("sp", xt[b][0:L], src_dram, writes=[f"xt{b}"])
            src = xt[b]
            stok = f"xt{b}"
        else:
            src = src_sb
            stok = src_tok
        def head():
            MEMSET("dve", ss[b][0:L], 0.0, [f"ss{b}"])
            ACTV(xn[b][0:L], src[0:L], AF.Square, [stok, f"ss{b}"], [f"xn{b}", f"ss{b}"], accum_out=ss[b][0:L, 0:1])
            ACTV(rstd[b][0:L], ss[b][0:L], AF.Sqrt, [f"ss{b}"], [f"rstd{b}"], bias=EPS, scale=1.0 / D)
            RECIP(rstd[b][0:L], rstd[b][0:L], [f"rstd{b}"], [f"rstd{b}"])
            ACTV(xn[b][0:L], src[0:L], AF.Copy, [stok, f"rstd{b}"], [f"xn{b}"], scale=rstd[b][0:L, 0:1])

        if defer != "3":
            head()

        def tail():
            for kt in range(KT):
                TR(pxT[:, kt, 0:L], xn[b][0:L, kt * 128:(kt + 1) * 128], ident_b[0:L, 0:L], [f"xn{b}", "ident_b"], [f"pxT{kt // 8}"])
            for half in range(2):
                TT("dve", hT[hbuf][:, half * 8:(half + 1) * 8, col0:col0 + L], pxT[:, half * 8:(half + 1) * 8, 0:L],
                   gT[:, half * 8:(half + 1) * 8].unsqueeze(2).to_broadcast([128, 8, L]), ALU.mult,
                   [f"pxT{half}", gT_tok], [htok])

        if defer == "3":
            return head, tail
        if defer:
            return tail
        tail()
        return None

    if "B" in phases:
        pes = ExitStack()
        stk.append(pes)
        m = linear_machinery(npm=3)
        hT, wbuf, pmm = m.hT, m.wbuf, m.pmm
        ptr_k = ps("ptr_k", [128, 4, 128], BF16)
        ptr_u = ps("ptr_u", [128, 8, 128], BF16)
        pf4 = ps("pf4", [128, 4, NH])
        ft4 = sb("ft4", [128, 4, NH])
        sqs = [sb(f"sqs{i}", [128, 512]) for i in range(2)]
        ssh = [sb(f"ssh{i}", [128, 4]) for i in range(2)]
        knf = [sb(f"knf{i}", [128, 512]) for i in range(2)]
        kraw = [sb(f"kraw{i}", [128, 512]) for i in range(2)]
        kout = [sb(f"kout{i}", [128, 512]) for i in range(2)]
        kbf = [sb(f"kbf{i}", [128, 512], BF16) for i in range(2)]
        vout = [sb(f"vout{i}", [128, 512]) for i in range(2)]
        kT_grp = [sb(f"kT_grp{i}", [128, NH, 512], BF16) for i in range(2)]
        qT_grp = [sb("qT_grp0", [128, NH, 512], BF16)] * 2
        v_grp = [sb(f"v_grp{i}", [128, NH, 4, DH], BF16) for i in range(2)]
        u_bf = [sb(f"u_bf{i}", [128, DSSM], BF16) for i in range(4)]
        uT_grp = [sb("uT_grp0", [128, 8, 512], BF16)] * 2
        ft = sb("ft", [128, NH])
        it = [0]
        tails = []

        def fe_tile(grp, tt, defer=False):
            is_smp = grp == 16
            L = 32 if is_smp else 128
            src = x_smp[tt * 32:(tt + 1) * 32, :] if is_smp else x_ctx[(grp * 4 + tt) * 128:(grp * 4 + tt + 1) * 128, :]
            gi_ = grp_list.index(grp)
            return front_end(m, src, None, None, L, gmixT, "gmixT", gi_ % 2, tt * L, f"hT{gi_ % 2}_{tt}", defer=defer)

        grp_list = list(range(17)) if dbg_groups is None else list(dbg_groups)
        for tt in range(4):
            if grp_list:
                fe_tile(grp_list[0], tt)

        def chunks_of(grp):
            is_smp_ = grp == 16
            own_ = grp >= 12 and not is_smp_
            ch_ = ["u0", "u1"] + (["q0", "q1"] if (own_ or is_smp_) else []) + ["k0", "k1", "v0", "v1", "f"]
            if dbg_chunks is not None:
                ch_ = [c_ for c_ in dbg_chunks if c_ in ch_]
            return ch_

        flat = [(gi_, ci_, cn_) for gi_, g_ in enumerate(grp_list) for ci_, cn_ in enumerate(chunks_of(g_))]
        loads = [(gi_, ci_, cn_) for (gi_, ci_, cn_) in flat if cn_ != "f"]
        sched_next = {}
        for k_ in range(len(loads) - 1):
            sched_next[(loads[k_][0], loads[k_][1])] = loads[k_ + 1][2]
        wq = []
        fe_pending = [None]
        if loads:
            wq.append(load_wchunk(m, scr_win[WIN_CH[loads[0][2]]], f"scr_win{WIN_CH[loads[0][2]]}"))
        for gi, grp in enumerate(grp_list):
            is_smp = grp == 16
            own = grp >= 12 and not is_smp
            L = 32 if is_smp else 128
            gb = gi % 2
            chunks = ["u0", "u1"] + (["q0", "q1"] if (own or is_smp) else []) + ["k0", "k1", "v0", "v1", "f"]
            if dbg_chunks is not None:
                chunks = [c_ for c_ in dbg_chunks if c_ in chunks]
            for cidx, cn in enumerate(chunks):
                if "E" in phases and gi >= 1:
                    convert_E_weights(1)
                fe_head = None
                if cidx < 4 and gi + 1 < len(grp_list):
                    fe_head, fe_tail_new = fe_tile(grp_list[gi + 1], cidx, defer="3")
                if cn != "f":
                    wi = wq.pop(0)
                nxt = sched_next.get((gi, cidx))
                if nxt is not None:
                    wq.append(load_wchunk(m, scr_win[WIN_CH[nxt]], f"scr_win{WIN_CH[nxt]}"))
                for tt in range(4):
                    T = grp * 4 + tt
                    htok = f"hT{gb}_{tt}"
                    j = it[0] % 2
                    jp = it[0] % 3
                    it[0] += 1
                    if cn == "f":
                        for kt in range(KT):
                            MM(pf4[0:L, tt, :], hT[gb][:, kt, tt * L:(tt + 1) * L], wf_b[:, kt, :], kt == 0, kt == KT - 1, [htok, "wf_b"], ["pf4"])
                        if tt == 3:
                            TT("dve", ft4[0:L], pf4[0:L], bf_bc[0:L].unsqueeze(1).to_broadcast([L, 4, NH]), ALU.add, ["pf4", "bf_bc"], ["ft4"])
                            ACTV(ft4[0:L], ft4[0:L], AF.Exp, ["ft4"], ["ft4"], scale=-1.0)
                            ACTV(ft4[0:L], ft4[0:L], AF.Ln, ["ft4"], ["ft4"], bias=1.0)
                            if is_smp:
                                TS("dve", logf_smp[:, :, :], ft4[0:32], -1.0, None, ALU.mult, None, ["ft4"], ["logf_smp"])
                            else:
                                TS("dve", logf_all[:, grp * 4:grp * 4 + 4, :], ft4[:], -1.0, None, ALU.mult, None, ["ft4"],
                                   [f"logf{grp * 4 + t_}" for t_ in range(4)])
                        continue
                    pm = pmm[jp]
                    ptok = f"pmm{jp}"
                    for kt in range(KT):
                        MM(pm[0:L, :], hT[gb][:, kt, tt * L:(tt + 1) * L], wbuf[wi][:, kt, :], kt == 0, kt == KT - 1,
                           [htok, f"wbuf{wi}"], [ptok])
                    while len(tails) >= 2:
                        tails.pop(0)()
                    half = int(cn[1])
                    hs = slice(half * 512, (half + 1) * 512)
                    if cn[0] == "u":
                        CP("act", u_bf[tt][0:L, hs], pm[0:L, :], [ptok], [f"u_bf{tt}_{half}"])
                        if half == 1 and not (own or is_smp):
                            DMA("sp", u_scr[T], u_bf[tt][:], reads=[f"u_bf{tt}_0", f"u_bf{tt}_1"], writes=["u_scr"])
                        if half == 1 and (own or is_smp):
                            def utail(tt=tt, L=L, gb=gb):
                                for f8 in range(8):
                                    TR(ptr_u[:, f8, 0:L], u_bf[tt][0:L, f8 * 128:(f8 + 1) * 128], ident_b[0:L, 0:L],
                                       [f"u_bf{tt}_0", f"u_bf{tt}_1", "ident_b"], ["ptr_u"])
                                CP("act", uT_grp[gb][:, :, tt * L:(tt + 1) * L], ptr_u[:, :, 0:L], ["ptr_u"], ["uT_grp0"])
                            tails.append(utail)
                    elif cn[0] in "qk":
                        gbc, gtok = (gq_bc, "gq_bc") if cn[0] == "q" else (gk_bc, "gk_bc")
                        ACTV(sqs[j][0:L], pm[0:L, :], AF.Square, [ptok], [f"sqs{j}"])
                        CP("dve", kraw[j][0:L], pm[0:L, :], [ptok], [f"kraw{j}"])
                        RED(ssh[j][0:L], h4(sqs[j][0:L]), [f"sqs{j}"], [f"ssh{j}"])
                        ACTV(ssh[j][0:L], ssh[j][0:L], AF.Sqrt, [f"ssh{j}"], [f"ssh{j}"], bias=EPS, scale=1.0 / DH)
                        RECIP(ssh[j][0:L], ssh[j][0:L], [f"ssh{j}"], [f"ssh{j}"])
                        TT("dve", h4(knf[j][0:L]), h4(kraw[j][0:L]), ssh[j][0:L].unsqueeze(2).to_broadcast([L, 4, DH]), ALU.mult,
                           [f"kraw{j}", f"ssh{j}"], [f"knf{j}"])
                        TT("pool", h4(kout[j][0:L]), h4(knf[j][0:L]), gbc[0:L].unsqueeze(1).to_broadcast([L, 4, DH]), ALU.mult,
                           [f"knf{j}", gtok], [f"kout{j}"])
                        CP("act", kbf[j][0:L], kout[j][0:L], [f"kout{j}"], [f"kbf{j}"])
                        if is_smp:
                            dstT, dtok = (kT_s, "kT_s") if cn[0] == "k" else (qT_s, "qT_s")
                        else:
                            dstT, dtok = (kT_grp[gb], f"kT_grp{gb}") if cn[0] == "k" else (qT_grp[gb], "qT_grp0")

                        def ktail(j=j, L=L, dstT=dstT, dtok=dtok, half=half, tt=tt):
                            for hh in range(4):
                                TR(ptr_k[:, hh, 0:L], kbf[j][0:L, hh * 128:(hh + 1) * 128], ident_b[0:L, 0:L], [f"kbf{j}", "ident_b"], ["ptr_k"])
                            CP("dve", dstT[:, half * 4:(half + 1) * 4, tt * L:(tt + 1) * L], ptr_k[:, :, 0:L], ["ptr_k"], [dtok])
                        tails.append(ktail)
                        if cn[0] == "k" and (own or is_smp):
                            if is_smp:
                                dst = k_smp[tt * 32:(tt + 1) * 32, hs]
                            else:
                                dst = k_own[(T - NPT) * 128:(T - NPT + 1) * 128, hs]
                            DMA("sp", dst, kout[j][0:L], reads=[f"kout{j}"], out_store=True)
                    else:
                        CP("act", vout[j][0:L], pm[0:L, :], [ptok], [f"vout{j}"])
                        if is_smp:
                            CP("pool", v_s[:, half * 4:(half + 1) * 4, tt, :], h4(vout[j][0:32, :]), [f"vout{j}"], ["v_s"])
                        else:
                            CP("pool", v_grp[gb][:, half * 4:(half + 1) * 4, tt, :], h4(vout[j][:, :]), [f"vout{j}"], [f"v_grp{gb}"])
                        if own or is_smp:
                            if is_smp:
                                dst = v_smp[tt * 32:(tt + 1) * 32, hs]
                            else:
                                dst = v_own[(T - NPT) * 128:(T - NPT + 1) * 128, hs]
                            DMA("sp", dst, vout[j][0:L], reads=[f"vout{j}"], out_store=True)
                if fe_pending[0] is not None:
                    fe_pending[0]()
                    fe_pending[0] = None
                if fe_head is not None:
                    fe_head()
                    fe_pending[0] = fe_tail_new
            if fe_pending[0] is not None:
                fe_pending[0]()
                fe_pending[0] = None
            while tails:
                tails.pop(0)()
            if not is_smp:
                DMA("sp", kT_scr[:, :, grp * 512:(grp + 1) * 512].rearrange("h p n -> p h n"), kT_grp[gb][:],
                    reads=[f"kT_grp{gb}"], writes=["kT_scr"])
                DMA("sp", v_scr[:, :, grp * 4:(grp + 1) * 4, :].rearrange("h p t d -> p h (t d)"),
                    v_grp[gb][:].rearrange("p h t d -> p h (t d)"), reads=[f"v_grp{gb}"], writes=["v_scr"])
                if own:
                    DMA("sp", qT_scr[:, :, (grp - 12) * 512:(grp - 11) * 512].rearrange("h p n -> p h n"), qT_grp[gb][:],
                        reads=["qT_grp0"], writes=["qT_scr"])
                    DMA("sp", uT_scr[:, :, (grp - 12) * 512:(grp - 11) * 512].rearrange("f p n -> p f n"), uT_grp[gb][:],
                        reads=["uT_grp0"], writes=["uT_scr"])
            else:
                DMA("sp", uT_scr[:, :, NOWN:NOWN + NSMP].rearrange("f p n -> p f n"), uT_grp[gb][:, :, 0:NSMP],
                    reads=["uT_grp0"], writes=["uT_scr"])
        DMA("sp", lf_own.rearrange("(t p) h -> p t h", p=128), logf_all[:, NPT:NT, :],
            reads=[f"logf{T}" for T in range(NPT, NT)], out_store=True, nonc=True)
        DMA("sp", lf_smp.rearrange("(s p) h -> p s h", p=32), logf_smp[:], reads=["logf_smp"], out_store=True, nonc=True)
        P.flush()
        stk.pop()
        pes.close()


    rstd_ssm = sb("rstd_ssm", [128, 20])
    rstd_att = sb("rstd_att", [128, 20])
    if "C" in phases:
        ces = ExitStack()
        stk.append(ces)
        TWO_PI = float(2.0 * np.pi)
        NC = G * PST
        W1re = sb("W1re", [128, NC])
        W1im = sb("W1im", [128, NC])
        W2re = sb("W2re", [128, 32, 128])
        W2im = sb("W2im", [128, 32, 128])
        WEre = sb("WEre", [128, NC], BF16)
        WEim = sb("WEim", [128, NC], BF16)
        Bblk = sb("Bblk", [128, 8, 1024], BF16)
        Cre_b = sb("Cre_b", [128, 8, 4, 128], BF16)
        Cimn_b = sb("Cimn_b", [128, 8, 4, 128], BF16)
        Bpre = sb("Bpre", [128, 32, 32])
        Bpim = sb("Bpim", [128, 32, 32])
        d_t = sb("d_t", [128, 8])
        gssm_t = sb("gssm_t", [128, 8])
        hpre = [sb("hpre_re", [128, 32]), sb("hpre_im", [128, 32])]
        iota1 = sb("iota1", [128, 128])
        sp1 = sb("sp1", [128, 1])
        nsp1 = sb("nsp1", [128, 1])
        nE = sb("nE", [128, 1])
        DMA("sp", iota1[:], c_iota1, writes=["iota1"])
        DMA("sp", sp1[:], c_sp1, writes=["sp1"])
        DMA("sp", d_t[:], ssm_d.rearrange("(fb q) -> q fb", q=128), writes=["d_t"], nonc=True)
        DMA("sp", gssm_t[:], g_out_ssm.rearrange("(fb q) -> q fb", q=128), writes=["gssm_t"], nonc=True)
        TS("dve", nsp1[:], sp1[:], -1.0, None, ALU.mult, None, ["sp1"], ["nsp1"])
        TS("dve", nE[:], sp1[:], -1.0, 128.0, ALU.mult, ALU.add, ["sp1"], ["nE"])

        ses = ExitStack()
        stk.append(ses)
        QN = 1024
        lam_q = sb("lam_q", [128, QN])
        th_q = sb("th_q", [128, QN])
        are_q = sb("are_q", [128, QN])
        aim_q = sb("aim_q", [128, QN])
        fr_q = sb("fr_q", [128, QN])
        fi_q = sb("fi_q", [128, QN])
        tA = sb("tA", [128, QN])
        tB = sb("tB", [128, QN])
        tC = sb("tC", [128, QN])
        tI = sb("tI", [128, QN], I32)
        raw1 = sb("raw1", [128, QN])
        raw2 = sb("raw2", [128, QN])
        step_bc = sb("step_bc", [128, G])
        lam_st = sb("lam_st", [128, 32])
        th_st = sb("th_st", [128, 32])
        are_st = sb("are_st", [128, 32])
        aim_st = sb("aim_st", [128, 32])
        step_st = sb("step_st", [128, 32])

        def trig2(out_c, ctok, out_s, stok_, z, ztok):
            TS("dve", tB[:], z, 1.0, TWO_PI, ALU.mult, ALU.add, [ztok], ["tB"])
            TS("dve", tI[:], tB[:], 1.0 / TWO_PI, None, ALU.mult, None, ["tB"], ["tI"])
            CP("dve", tC[:], tI[:], ["tI"], ["tC"])
            STT("dve", tB[:], tC[:], -TWO_PI, tB[:], ALU.mult, ALU.add, ["tC", "tB"], ["tB"])
            TS("dve", tC[:], tB[:], float(np.pi), -TWO_PI, ALU.is_gt, ALU.mult, ["tB"], ["tC"])
            TT("dve", tB[:], tB[:], tC[:], ALU.add, ["tB", "tC"], ["tB"])
            ACTV(out_s, tB[:], AF.Sin, ["tB"], [stok_])
            STT("dve", tC[:], tB[:], -1.0, tB[:], ALU.mult, ALU.max, ["tB"], ["tC"])
            ACTV(out_c, tC[:], AF.Sin, ["tC"], [ctok], bias=float(np.pi / 2), scale=-1.0)

        DMA("sp", step_bc[:], log_step.broadcast_to([128, G]), writes=["step_bc"])
        ACTV(step_bc[:], step_bc[:], AF.Exp, ["step_bc"], ["step_bc"])
        DMA("sp", are_st[:], a_re.rearrange("o (j q) -> q (o j)", q=128), writes=["are_st"], nonc=True)
        DMA("sp", aim_st[:], a_im.rearrange("o (j q) -> q (o j)", q=128), writes=["aim_st"], nonc=True)
        ls3 = log_step.rearrange("o (j e) -> o j e", e=2)
        DMA("sp", step_st[0:64, :], ls3[:, :, 0].broadcast_to([64, 32]), writes=["step_st"], nonc=True)
        DMA("sp", step_st[64:128, :], ls3[:, :, 1].broadcast_to([64, 32]), writes=["step_st"], nonc=True)
        ACTV(step_st[:], step_st[:], AF.Exp, ["step_st"], ["step_st"])
        TT("dve", lam_st[:], are_st[:], step_st[:], ALU.mult, ["are_st", "step_st"], ["lam_st"])
        TT("dve", th_st[:], aim_st[:], step_st[:], ALU.mult, ["aim_st", "step_st"], ["th_st"])

        for qd in range(4):
            cq = slice(qd * QN, (qd + 1) * QN)
            gq = slice(qd * 16, (qd + 1) * 16)
            g3 = lambda ap: ap.rearrange("p (g q) -> p g q", g=16)
            stepb3 = step_bc[:, gq].unsqueeze(2).to_broadcast([128, 16, PST])
            DMA("sp", are_q[:], a_re[:, cq].broadcast_to([128, QN]), writes=["are_q"])
            DMA("sp", aim_q[:], a_im[:, cq].broadcast_to([128, QN]), writes=["aim_q"])
            TT("dve", g3(lam_q[:]), g3(are_q[:]), stepb3, ALU.mult, ["are_q", "step_bc"], ["lam_q"])
            TT("dve", g3(th_q[:]), g3(aim_q[:]), stepb3, ALU.mult, ["aim_q", "step_bc"], ["th_q"])
            trig2(fr_q[:], "fr_q", fi_q[:], "fi_q", th_q[:], "th_q")
            ACTV(tA[:], lam_q[:], AF.Exp, ["lam_q"], ["tA"])
            TT("dve", fr_q[:], fr_q[:], tA[:], ALU.mult, ["fr_q", "tA"], ["fr_q"])
            TT("dve", fi_q[:], fi_q[:], tA[:], ALU.mult, ["fi_q", "tA"], ["fi_q"])
            TS("dve", fr_q[:], fr_q[:], -1.0, None, ALU.add, None, ["fr_q"], ["fr_q"])
            TT("dve", tA[:], are_q[:], are_q[:], ALU.mult, ["are_q"], ["tA"])
            TT("dve", tB[:], aim_q[:], aim_q[:], ALU.mult, ["aim_q"], ["tB"])
            TT("dve", tA[:], tA[:], tB[:], ALU.add, ["tA", "tB"], ["tA"])
            RECIP(tA[:], tA[:], ["tA"], ["tA"])
            TT("dve", tB[:], fr_q[:], are_q[:], ALU.mult, ["fr_q", "are_q"], ["tB"])
            TT("dve", tC[:], fi_q[:], aim_q[:], ALU.mult, ["fi_q", "aim_q"], ["tC"])
            TT("dve", tB[:], tB[:], tC[:], ALU.add, ["tB", "tC"], ["tB"])
            TT("dve", tC[:], fi_q[:], are_q[:], ALU.mult, ["fi_q", "are_q"], ["tC"])
            TT("dve", fi_q[:], fr_q[:], aim_q[:], ALU.mult, ["fr_q", "aim_q"], ["fi_q"])
            TT("dve", fi_q[:], tC[:], fi_q[:], ALU.subtract, ["tC", "fi_q"], ["fi_q"])
            TT("dve", fr_q[:], tB[:], tA[:], ALU.mult, ["tB", "tA"], ["fr_q"])
            TT("dve", fi_q[:], fi_q[:], tA[:], ALU.mult, ["fi_q", "tA"], ["fi_q"])
            fbq = slice(2 * qd, 2 * qd + 2)
            v2 = lambda ap: ap.rearrange("p (a b) -> p a b", a=2)
            DMA("sp", v2(raw1[:]), bblk_re[:, fbq, :], writes=["raw1"])
            DMA("sp", v2(raw2[:]), bblk_im[:, fbq, :], writes=["raw2"])
            TT("dve", tA[:], raw1[:], fr_q[:], ALU.mult, ["raw1", "fr_q"], ["tA"])
            TT("dve", tB[:], raw2[:], fi_q[:], ALU.mult, ["raw2", "fi_q"], ["tB"])
            TT("dve", Bblk[:, fbq, 0:512], v2(tA[:]), v2(tB[:]), ALU.subtract, ["tA", "tB"], ["Bblk"])
            TT("dve", tA[:], raw2[:], fr_q[:], ALU.mult, ["raw2", "fr_q"], ["tA"])
            TT("dve", tB[:], raw1[:], fi_q[:], ALU.mult, ["raw1", "fi_q"], ["tB"])
            TT("dve", Bblk[:, fbq, 512:1024], v2(tA[:]), v2(tB[:]), ALU.add, ["tA", "tB"], ["Bblk"])
            DMA("sp", raw1[:], cblk_re[:, fbq].rearrange("p a b c -> p (a b c)"), reads=["Bblk"], writes=["raw1"])
            DMA("sp", raw2[:], cblk_im[:, fbq].rearrange("p a b c -> p (a b c)"), reads=["Bblk"], writes=["raw2"])
            CP("pool", Cre_b[:, fbq].rearrange("p a b c -> p (a b c)"), raw1[:], ["raw1"], ["Cre_b"])
            TS("dve", Cimn_b[:, fbq].rearrange("p a b c -> p (a b c)"), raw2[:], -1.0, None, ALU.mult, None, ["raw2"], ["Cimn_b"])
            TS("dve", tA[:], th_q[:], sp1[:, 0:1], None, ALU.mult, None, ["th_q", "sp1"], ["tA"])
            trig2(W1re[:, cq], "W1re", W1im[:, cq], "W1im", tA[:], "tA")
            ACTV(tA[:], lam_q[:], AF.Exp, ["lam_q", "nsp1"], ["tA"], scale=nsp1[:, 0:1])
            TT("dve", W1re[:, cq], W1re[:, cq], tA[:], ALU.mult, ["W1re", "tA"], ["W1re"])
            STT("dve", W1im[:, cq], W1im[:, cq], -1.0, tA[:], ALU.mult, ALU.mult, ["W1im", "tA"], ["W1im"])
            TS("dve", tA[:], th_q[:], nE[:, 0:1], None, ALU.mult, None, ["th_q", "nE"], ["tA"])
            trig2(fr_q[:], "fr_q", fi_q[:], "fi_q", tA[:], "tA")
            ACTV(tA[:], lam_q[:], AF.Exp, ["lam_q", "nE"], ["tA"], scale=nE[:, 0:1])
            TT("dve", WEre[:, cq], fr_q[:], tA[:], ALU.mult, ["fr_q", "tA"], ["WEre"])
            TT("dve", WEim[:, cq], fi_q[:], tA[:], ALU.mult, ["fi_q", "tA"], ["WEim"])
            jq_ = slice(qd * 8, (qd + 1) * 8)
            j3 = lambda ap: ap.rearrange("p (j t) -> p j t", j=8)
            iob = iota1[:].unsqueeze(1).to_broadcast([128, 8, 128])
            w2r = W2re[:, jq_, :].rearrange("p j t -> p (j t)")
            w2i = W2im[:, jq_, :].rearrange("p j t -> p (j t)")
            TT("dve", j3(tA[:]), iob, th_st[:, jq_].unsqueeze(2).to_broadcast([128, 8, 128]), ALU.mult, ["iota1", "th_st"], ["tA"])
            trig2(w2r, "W2re", w2i, "W2im", tA[:], "tA")
            TT("dve", j3(tA[:]), iob, lam_st[:, jq_].unsqueeze(2).to_broadcast([128, 8, 128]), ALU.mult, ["iota1", "lam_st"], ["tA"])
            ACTV(tA[:], tA[:], AF.Exp, ["tA"], ["tA"])
            TT("dve", w2r, w2r, tA[:], ALU.mult, ["W2re", "tA"], ["W2re"])
            TT("dve", w2i, w2i, tA[:], ALU.mult, ["W2im", "tA"], ["W2im"])
        fs = [sb(f"fs{i}", [128, 32]) for i in range(6)]
        nr_s, ni_s, den_s, t_s, fr_s, fi_s = fs
        TS("dve", nr_s[:], W2re[:, :, 0], -1.0, None, ALU.add, None, ["W2re"], ["nr_s"])
        CP("dve", ni_s[:], W2im[:, :, 0], ["W2im"], ["ni_s"])
        TT("dve", den_s[:], are_st[:], are_st[:], ALU.mult, ["are_st"], ["den_s"])
        TT("dve", t_s[:], aim_st[:], aim_st[:], ALU.mult, ["aim_st"], ["t_s"])
        TT("dve", den_s[:], den_s[:], t_s[:], ALU.add, ["den_s", "t_s"], ["den_s"])
        RECIP(den_s[:], den_s[:], ["den_s"], ["den_s"])
        TT("dve", fr_s[:], nr_s[:], are_st[:], ALU.mult, ["nr_s", "are_st"], ["fr_s"])
        TT("dve", t_s[:], ni_s[:], aim_st[:], ALU.mult, ["ni_s", "aim_st"], ["t_s"])
        TT("dve", fr_s[:], fr_s[:], t_s[:], ALU.add, ["fr_s", "t_s"], ["fr_s"])
        TT("dve", fr_s[:], fr_s[:], den_s[:], ALU.mult, ["fr_s", "den_s"], ["fr_s"])
        TT("dve", fi_s[:], ni_s[:], are_st[:], ALU.mult, ["ni_s", "are_st"], ["fi_s"])
        TT("dve", t_s[:], nr_s[:], aim_st[:], ALU.mult, ["nr_s", "aim_st"], ["t_s"])
        TT("dve", fi_s[:], fi_s[:], t_s[:], ALU.subtract, ["fi_s", "t_s"], ["fi_s"])
        TT("dve", fi_s[:], fi_s[:], den_s[:], ALU.mult, ["fi_s", "den_s"], ["fi_s"])
        braw_re = sb("braw_re", [128, 32, 32])
        braw_im = sb("braw_im", [128, 32, 32])
        bt1 = sb("bt1", [128, 32, 32])
        bt2 = sb("bt2", [128, 32, 32])
        DMA("sp", braw_re[:], bpad_re, writes=["braw_re"])
        DMA("sp", braw_im[:], bpad_im, writes=["braw_im"])
        frb = fr_s[:].unsqueeze(2).to_broadcast([128, 32, 32])
        fib = fi_s[:].unsqueeze(2).to_broadcast([128, 32, 32])
        TT("dve", bt1[:], braw_re[:], frb, ALU.mult, ["braw_re", "fr_s"], ["bt1"])
        TT("dve", bt2[:], braw_im[:], fib, ALU.mult, ["braw_im", "fi_s"], ["bt2"])
        TT("dve", Bpre[:], bt1[:], bt2[:], ALU.subtract, ["bt1", "bt2"], ["Bpre"])
        TT("dve", bt1[:], braw_im[:], frb, ALU.mult, ["braw_im", "fr_s"], ["bt1"])
        TT("dve", bt2[:], braw_re[:], fib, ALU.mult, ["braw_re", "fi_s"], ["bt2"])
        TT("dve", Bpim[:], bt1[:], bt2[:], ALU.add, ["bt1", "bt2"], ["Bpim"])
        MEMSET("dve", hpre[0][:], 0.0, ["hpre0"])
        MEMSET("dve", hpre[1][:], 0.0, ["hpre1"])
        P.flush()
        stk.pop()
        ses.close()

        e0 = ExitStack()
        stk.append(e0)
        pE = [ps(f"pE{i}", [128, 8, 2, 32]) for i in range(2)]
        ut = [sb(f"ut{i}", [128, DSSM], BF16) for i in range(2)]
        em2 = [[sb(f"em{q}_{i}", [128, 8, 32]) for i in range(6)] for q in range(2)]
        Ere = sb("Ere", [128, 32])
        Eim = sb("Eim", [128, 32])
        hs = [sb(f"hs{i}", [128, 32]) for i in range(4)]
        A128re = W2re[:, :, 127]
        A128im = W2im[:, :, 127]
        n_pre = NPT if dbg_groups is None else 0
        Ere2 = [Ere, sb("Ere_b2", [128, 32])]
        Eim2 = [Eim, sb("Eim_b2", [128, 32])]
        NR = n_pre * 4

        def e1(r):
            T, jq = r // 4, r % 4
            ub = T % 2
            if jq == 0:
                DMA("sp", ut[ub][:], u_scr[T], writes=[f"ut{ub}"])
            pe_ = pE[r % 2]
            ptk = f"pE{r % 2}"
            for jl in range(8):
                j = jq * 8 + jl
                MM(pe_[:, jl, 0, :], WEre[:, j * 128:(j + 1) * 128], ut[ub][:, j * 32:(j + 1) * 32], True, True, ["WEre", f"ut{ub}"], [ptk])
                MM(pe_[:, jl, 1, :], WEim[:, j * 128:(j + 1) * 128], ut[ub][:, j * 32:(j + 1) * 32], True, True, ["WEim", f"ut{ub}"], [ptk])

        def e2(r):
            T, jq = r // 4, r % 4
            pe_ = pE[r % 2]
            ptk = f"pE{r % 2}"
            js = slice(jq * 8, (jq + 1) * 8)
            em = em2[r % 2]
            eq = r % 2
            TT("dve", em[0][:], pe_[:, :, 0, :], Bpre[:, js, :], ALU.mult, [ptk, "Bpre"], [f"em{eq}0"])
            TT("dve", em[1][:], pe_[:, :, 1, :], Bpim[:, js, :], ALU.mult, [ptk, "Bpim"], [f"em{eq}1"])
            TT("dve", em[2][:], pe_[:, :, 0, :], Bpim[:, js, :], ALU.mult, [ptk, "Bpim"], [f"em{eq}2"])
            TT("dve", em[3][:], pe_[:, :, 1, :], Bpre[:, js, :], ALU.mult, [ptk, "Bpre"], [f"em{eq}3"])

        def e3(r):
            em = em2[r % 2]
            eq = r % 2
            TT("pool", em[4][:], em[0][:], em[1][:], ALU.subtract, [f"em{eq}0", f"em{eq}1"], [f"em{eq}4"])
            TT("pool", em[5][:], em[2][:], em[3][:], ALU.add, [f"em{eq}2", f"em{eq}3"], [f"em{eq}5"])

        def e4(r):
            T, jq = r // 4, r % 4
            js = slice(jq * 8, (jq + 1) * 8)
            em = em2[r % 2]
            eq = r % 2
            tp_ = T % 2
            RED(Ere2[tp_][:, js], em[4][:], [f"em{eq}4"], [f"Ere{tp_}_{jq}"])
            RED(Eim2[tp_][:, js], em[5][:], [f"em{eq}5"], [f"Eim{tp_}_{jq}"])
            if jq != 3:
                return
            TT("dve", hs[0][:], A128re, hpre[0][:], ALU.mult, ["hpre0"], ["hs0"])
            TT("dve", hs[1][:], A128im, hpre[1][:], ALU.mult, ["hpre1"], ["hs1"])
            TT("dve", hs[2][:], A128re, hpre[1][:], ALU.mult, ["hpre1"], ["hs2"])
            TT("dve", hs[3][:], A128im, hpre[0][:], ALU.mult, ["hpre0"], ["hs3"])
            TT("dve", hs[0][:], hs[0][:], hs[1][:], ALU.subtract, ["hs0", "hs1"], ["hs0"])
            TT("dve", hs[2][:], hs[2][:], hs[3][:], ALU.add, ["hs2", "hs3"], ["hs2"])
            TT("dve", hpre[0][:], hs[0][:], Ere2[tp_][:], ALU.add, ["hs0"] + [f"Ere{tp_}_{q_}" for q_ in range(4)], ["hpre0"])
            TT("dve", hpre[1][:], hs[2][:], Eim2[tp_][:], ALU.add, ["hs2"] + [f"Eim{tp_}_{q_}" for q_ in range(4)], ["hpre1"])

        for step in range(NR + 3):
            if step < NR:
                e1(step)
            if 0 <= step - 1 < NR:
                e2(step - 1)
            if 0 <= step - 2 < NR:
                e3(step - 2)
            if 0 <= step - 3 < NR:
                e4(step - 3)
        P.flush()
        stk.pop()
        e0.close()

        c1 = ExitStack()
        stk.append(c1)
        wglu_b = sb("wglu_b", [128, 8, DSSM], BF16)
        for kt in range(8):
            DMA("pool", wglu_b[:, kt, :], w_glu[kt * 128:(kt + 1) * 128, :], writes=["wglu_b"])
        pbu = [ps(f"pbu{i}", [128, 512]) for i in range(2)]
        pST = [ps(f"pST{i}", [128, 4, 128]) for i in range(2)]
        py = ps("py", [128, 128])
        pglu = ps("pglu", [128, 4, 128])
        prs = ps("prs", [128, 1])
        uT_t = [sb(f"uT_t{i}", [128, 8, 128], BF16) for i in range(2)]
        tt_ = [[sb(f"ct{q}_{i}", [128, 256]) for i in range(4)] for q in range(2)]
        Xb = [sb(f"Xb{q}", [128, 512], BF16) for q in range(2)]
        Ab = [sb(f"Ab{q}", [128, 4, 128]) for q in range(2)]
        pp = [[sb(f"pp{q}_{i}", [128, 2, 128]) for i in range(4)] for q in range(2)]
        Hb = [sb(f"Hb{q}", [128, 4, 128], BF16) for q in range(2)]
        yT = sb("yT", [128, 8, 128])
        g1 = sb("g1", [128, 8, 128])
        gyT = sb("gyT", [128, 8, 128])
        gyT_b = sb("gyT_b", [128, 8, 128], BF16)
        so = sb("so", [128, 4, 128])
        sq = sb("sq", [128, 8, 128], BF16)
        mixT_t = sb("mixT_t", [128, 8, 128], BF16)
        hc = [[sb(f"hc{a}{b}", [128, 32]) for b in range(2)] for a in range(2)]
        rs_t = sb("rs_t", [128, 1])

        if dbg_groups is None:
            tiles = [("own", n) for n in range(16)] + [("smp", s_) for s_ in range(4)]
        else:
            tiles = [("own", n) for n in range(1)] + [("smp", s_) for s_ in range(1)]
        yT2 = [yT, sb("yT_b2", [128, 8, 128])]
        units = []
        for ti, (kind, n) in enumerate(tiles):
            for hfb in range(16):
                units.append((ti, kind, n, hfb))

        def uinfo(u):
            ti, kind, n, hfb = units[u]
            L = 128 if kind == "own" else 32
            c0 = n * 128 if kind == "own" else NOWN + 32 * n
            tcol = n if kind == "own" else 16 + n
            return ti, kind, n, hfb, L, c0, tcol, ti % 2, ti % 2, hc[ti % 2], hc[1 - ti % 2], hfb // 2, hfb % 2, u % 2

        def stage1(u):
            ti, kind, n, hfb, L, c0, tcol, ub, pi, hin, hout, fb, hh, q = uinfo(u)
            if hfb == 0:
                DMA("sp", uT_t[ub][:, :, 0:L], uT_scr[:, :, c0:c0 + L].rearrange("f p n -> p f n"), writes=[f"uT_t{ub}"], nonc=(L < 128))
            cre = slice(fb * 512 + hh * 256, fb * 512 + hh * 256 + 256)
            MM(pbu[q][0:L, 0:256], uT_t[ub][:, fb, 0:L], Bblk[:, fb, hh * 256:hh * 256 + 256], True, True, [f"uT_t{ub}", "Bblk"], [f"pbu{q}"])
            MM(pbu[q][0:L, 256:512], uT_t[ub][:, fb, 0:L], Bblk[:, fb, 512 + hh * 256:512 + hh * 256 + 256], True, True,
               [f"uT_t{ub}", "Bblk"], [f"pbu{q}"])
            t_ = tt_[q]
            TT("dve", t_[0][0:L], pbu[q][0:L, 0:256], W1re[0:L, cre], ALU.mult, [f"pbu{q}", "W1re"], [f"ct{q}0"])
            TT("dve", t_[1][0:L], pbu[q][0:L, 256:512], W1im[0:L, cre], ALU.mult, [f"pbu{q}", "W1im"], [f"ct{q}1"])
            TT("dve", t_[2][0:L], pbu[q][0:L, 0:256], W1im[0:L, cre], ALU.mult, [f"pbu{q}", "W1im"], [f"ct{q}2"])
            TT("dve", t_[3][0:L], pbu[q][0:L, 256:512], W1re[0:L, cre], ALU.mult, [f"pbu{q}", "W1re"], [f"ct{q}3"])
            TT("pool", Xb[q][0:L, 0:256], t_[0][0:L], t_[1][0:L], ALU.subtract, [f"ct{q}0", f"ct{q}1"], [f"Xre{q}"])
            TT("pool", Xb[q][0:L, 256:512], t_[2][0:L], t_[3][0:L], ALU.add, [f"ct{q}2", f"ct{q}3"], [f"Xim{q}"])

        def stage2(u):
            ti, kind, n, hfb, L, c0, tcol, ub, pi, hin, hout, fb, hh, q = uinfo(u)
            if hfb == 0:
                if kind == "own" and n == 0:
                    CP("dve", hin[0][:], hpre[0][:], ["hpre0"], [f"hc{pi}0_{f_}" for f_ in range(8)])
                    CP("dve", hin[1][:], hpre[1][:], ["hpre1"], [f"hc{pi}1_{f_}" for f_ in range(8)])
                if kind == "smp":
                    DMA("sp", hin[0][:], sre[n].rearrange("(j q) -> q j", q=128), writes=[f"hc{pi}0_{f_}" for f_ in range(8)], nonc=True)
                    DMA("sp", hin[1][:], sim[n].rearrange("(j q) -> q j", q=128), writes=[f"hc{pi}1_{f_}" for f_ in range(8)], nonc=True)
            for k2 in range(2):
                MM(pST[q][:, k2, 0:L], Xb[q][0:L, k2 * 128:(k2 + 1) * 128], triT_b[0:L, 0:L], True, True, [f"Xre{q}", "triT_b"], [f"pST{q}"])
            for k2 in range(2):
                MM(pST[q][:, 2 + k2, 0:L], Xb[q][0:L, 256 + k2 * 128:256 + (k2 + 1) * 128], triT_b[0:L, 0:L], True, True,
                   [f"Xim{q}", "triT_b"], [f"pST{q}"])
            j0 = fb * 4 + hh * 2
            for k2 in range(2):
                ACTV(Ab[q][:, k2, 0:L], pST[q][:, k2, 0:L], AF.Identity, [f"pST{q}", f"hc{pi}0_{fb}"], [f"Are{q}"], bias=hin[0][:, j0 + k2:j0 + k2 + 1])
            for k2 in range(2):
                ACTV(Ab[q][:, 2 + k2, 0:L], pST[q][:, 2 + k2, 0:L], AF.Identity, [f"pST{q}", f"hc{pi}1_{fb}"], [f"Aim{q}"],
                     bias=hin[1][:, j0 + k2:j0 + k2 + 1])
            js = slice(j0, j0 + 2)
            p_ = pp[q]
            TT("dve", p_[0][:, :, 0:L], Ab[q][:, 0:2, 0:L], W2re[:, js, 0:L], ALU.mult, [f"Are{q}", "W2re"], [f"pp{q}0"])
            TT("dve", p_[1][:, :, 0:L], Ab[q][:, 2:4, 0:L], W2im[:, js, 0:L], ALU.mult, [f"Aim{q}", "W2im"], [f"pp{q}1"])
            TT("dve", p_[2][:, :, 0:L], Ab[q][:, 0:2, 0:L], W2im[:, js, 0:L], ALU.mult, [f"Are{q}", "W2im"], [f"pp{q}2"])
            TT("dve", p_[3][:, :, 0:L], Ab[q][:, 2:4, 0:L], W2re[:, js, 0:L], ALU.mult, [f"Aim{q}", "W2re"], [f"pp{q}3"])
            TT("pool", Hb[q][:, 0:2, 0:L], p_[0][:, :, 0:L], p_[1][:, :, 0:L], ALU.subtract, [f"pp{q}0", f"pp{q}1"], [f"Hre{q}"])
            TT("pool", Hb[q][:, 2:4, 0:L], p_[2][:, :, 0:L], p_[3][:, :, 0:L], ALU.add, [f"pp{q}2", f"pp{q}3"], [f"Him{q}"])
            TT("pool", hout[0][:, js], p_[0][:, :, L - 1], p_[1][:, :, L - 1], ALU.subtract, [f"pp{q}0", f"pp{q}1"], [f"hc{1 - pi}0_{fb}"])
            TT("pool", hout[1][:, js], p_[2][:, :, L - 1], p_[3][:, :, L - 1], ALU.add, [f"pp{q}2", f"pp{q}3"], [f"hc{1 - pi}1_{fb}"])
            if hfb == 15:
                hot = [f"hc{1 - pi}0_{f_}" for f_ in range(8)]
                hot1 = [f"hc{1 - pi}1_{f_}" for f_ in range(8)]
                if kind == "own" and n == 15:
                    DMA("sp", hre_own.rearrange("(j q) -> q j", q=128), hout[0][:], reads=hot, out_store=True, nonc=True)
                    DMA("sp", him_own.rearrange("(j q) -> q j", q=128), hout[1][:], reads=hot1, out_store=True, nonc=True)
                if kind == "smp":
                    DMA("sp", hre_smp[n].rearrange("(j q) -> q j", q=128), hout[0][:], reads=hot, out_store=True, nonc=True)
                    DMA("sp", him_smp[n].rearrange("(j q) -> q j", q=128), hout[1][:], reads=hot1, out_store=True, nonc=True)

        def stage3(u):
            ti, kind, n, hfb, L, c0, tcol, ub, pi, hin, hout, fb, hh, q = uinfo(u)
            yT_ = yT2[ti % 2]
            yp = ti % 2
            for k2 in range(2):
                MM(py[:, 0:L], Cre_b[:, fb, hh * 2 + k2, :], Hb[q][:, k2, 0:L], hh == 0 and k2 == 0, False, ["Cre_b", f"Hre{q}"], ["py"])
            for k2 in range(2):
                MM(py[:, 0:L], Cimn_b[:, fb, hh * 2 + k2, :], Hb[q][:, 2 + k2, 0:L], False, hh == 1 and k2 == 1, ["Cimn_b", f"Him{q}"], ["py"])
            if hh == 1:
                STT("dve", yT_[:, fb, 0:L], uT_t[ub][:, fb, 0:L], d_t[:, fb:fb + 1], py[:, 0:L], ALU.mult, ALU.add,
                    [f"uT_t{ub}", "d_t", "py"], [f"yT{yp}_{fb}"])
            if hfb != 15:
                return
            ytoks = [f"yT{yp}_{f_}" for f_ in range(8)]
            TT("pool", g1[:, :, 0:L], yT_[:, :, 0:L], yT_[:, :, 0:L], ALU.mult, ytoks, ["g1"])
            TS("dve", g1[:, :, 0:L], g1[:, :, 0:L], 0.044715, 1.0, ALU.mult, ALU.add, ["g1"], ["g1"])
            TT("pool", g1[:, :, 0:L], g1[:, :, 0:L], yT_[:, :, 0:L], ALU.mult, ["g1"] + ytoks, ["g1"])
            ACTV(g1[:, :, 0:L], g1[:, :, 0:L], AF.Sigmoid, ["g1"], ["g1"], scale=1.5957691216057308)
            TT("dve", gyT[:, :, 0:L], yT_[:, :, 0:L], g1[:, :, 0:L], ALU.mult, ytoks + ["g1"], ["gyT"])
            CP("pool", gyT_b[:, :, 0:L], gyT[:, :, 0:L], ["gyT"], ["gyT_b"])
            for half in range(2):
                for fo4 in range(4):
                    fo = half * 4 + fo4
                    for fi_ in range(8):
                        MM(pglu[:, fo4, 0:L], wglu_b[:, fi_, fo * 128:(fo + 1) * 128], gyT_b[:, fi_, 0:L], fi_ == 0, fi_ == 7,
                           ["wglu_b", "gyT_b"], ["pglu"])
                fs_ = slice(half * 4, half * 4 + 4)
                ACTV(so[:, :, 0:L], pglu[:, :, 0:L], AF.Sigmoid, ["pglu"], ["so"])
                TT("dve", so[:, :, 0:L], so[:, :, 0:L], gyT[:, fs_, 0:L], ALU.mult, ["so", "gyT"], ["so"])
                sqc = 0 if kind == "own" else 32 * n
                ACTV(sq[:, fs_, sqc:sqc + L], so[:, :, 0:L], AF.Square, ["so"], [f"sq{half}"])
                TT("pool", mixT_t[:, fs_, 0:L], so[:, :, 0:L], gssm_t[:, fs_].unsqueeze(2).to_broadcast([128, 4, L]), ALU.mult,
                   ["so", "gssm_t"], [f"mixT_t{half}"])
            if kind == "own" or n == 3 or dbg_groups is not None:
                LL = 128 if (kind == "own" or dbg_groups is None) else 32
                rc = tcol if kind == "own" else 16
                for fo in range(8):
                    MM(prs[0:LL, :], sq[:, fo, 0:LL], ones_b[:, 0:1], fo == 0, fo == 7, ["sq0", "sq1", "ones_b"], ["prs"])
                ACTV(rs_t[0:LL], prs[0:LL, :], AF.Sqrt, ["prs"], ["rs_t"], bias=EPS, scale=1.0 / DSSM)
                RECIP(rstd_ssm[0:LL, rc:rc + 1], rs_t[0:LL], ["rs_t"], [f"rstd_ssm{rc}"])
            DMA("sp", mixT_scr[0:8, :, c0:c0 + L].rearrange("f p n -> p f n"), mixT_t[:, :, 0:L], reads=["mixT_t0", "mixT_t1"],
                writes=["mixT_scr_s"], nonc=(L < 128))

        NU = len(units)
        for step in range(NU + 2):
            if step < NU:
                stage1(step)
            if 0 <= step - 1 < NU:
                stage2(step - 1)
            if 0 <= step - 2 < NU:
                stage3(step - 2)
        P.flush()
        stk.pop()
        c1.close()
        stk.pop()
        ces.close()


    SCALE = float(DH ** -0.5)
    rss_att = sb("rss_att", [128, 20])
    if "D" in phases:
        dd = ExitStack()
        stk.append(dd)
        gatt_t = sb("gatt_t", [128, 8])
        DMA("sp", gatt_t[:], g_out_attn.rearrange("(fb q) -> q fb", q=128), writes=["gatt_t"], nonc=True)
        MEMSET("dve", rss_att[:], 0.0, ["rss_att"])
        biasT = sb("biasT", [128, 4, NH, NT])
        bias_s = sb("bias_s", [128, 4, NH, 16])
        bias_n = sb("bias_n", [32, 4, NH])
        d0 = ExitStack()
        stk.append(d0)
        c_all = sb("c_all", [128, NT, NH])
        tot = sb("tot", [128, NT, NH])
        carry = sb("carry", [128, NT, NH])
        cref = sb("cref", [128, 4, NH])
        pc = ps("pc", [128, 512])
        pt = ps("pt", [128, 512])
        lf2 = logf_all[:].rearrange("p t h -> p (t h)")
        MM(pc[:], triT_f[:], lf2, True, True, ["triT_f"], ["pc"])
        MM(pt[:], ones_f[:], lf2, True, True, ["ones_f"], ["pt"])
        CP("dve", tot[:].rearrange("p t h -> p (t h)"), pt[:], ["pt"], ["tot"])
        MEMSET("dve", carry[:, 0, :], 0.0, ["carry"])
        for T in range(1, NT):
            TT("dve", carry[:, T, :], carry[:, T - 1, :], tot[:, T - 1, :], ALU.add, ["carry", "tot"], ["carry"])
        TT("dve", c_all[:].rearrange("p t h -> p (t h)"), pc[:], carry[:].rearrange("p t h -> p (t h)"), ALU.add, ["pc", "carry"], ["c_all"])
        for qg in range(4):
            T = NPT + 4 * qg + 3
            TT("dve", cref[:, qg, :], carry[:, T, :], tot[:, T, :], ALU.add, ["carry", "tot"], ["cref"])
        for qg in range(4):
            for h in range(NH):
                TS("dve", biasT[:, qg, h, :], c_all[:, :, h], cref[:, qg, h:h + 1], -1.0, ALU.subtract, ALU.mult, ["c_all", "cref"], ["biasT"])
                TT("dve", biasT[:, qg, h, :], biasT[:, qg, h, :], kmask[:], ALU.add, ["biasT", "kmask"], ["biasT"])
        clf_sb = sb("clf_sb", [128, 4, 16, NH])
        cw_s = sb("cw_s", [128, 4, 16, NH])
        tot_s = sb("tot_s", [128, 4, 16, NH])
        car_s = sb("car_s", [128, 4, 16, NH])
        totn = sb("totn", [128, 4, NH])
        total = sb("total", [128, 4, NH])
        for s_ in range(4):
            DMA("sp", clf_sb[:, s_], clf[s_].rearrange("(t p) h -> p t h", p=128), writes=["clf_sb"], nonc=True)
        cl2 = clf_sb[:].rearrange("p s t h -> p (s t h)")
        MM(pc[:], triT_f[:], cl2, True, True, ["triT_f", "clf_sb"], ["pc"])
        MM(pt[:], ones_f[:], cl2, True, True, ["ones_f", "clf_sb"], ["pt"])
        CP("dve", cw_s[:].rearrange("p s t h -> p (s t h)"), pc[:], ["pc"], ["cw_s"])
        CP("dve", tot_s[:].rearrange("p s t h -> p (s t h)"), pt[:], ["pt"], ["tot_s"])
        MEMSET("dve", car_s[:, :, 0, :], 0.0, ["car_s"])
        for T in range(1, 16):
            TT("dve", car_s[:, :, T, :], car_s[:, :, T - 1, :], tot_s[:, :, T - 1, :], ALU.add, ["car_s", "tot_s"], ["car_s"])
        ln2 = logf_smp[:].rearrange("p s h -> p (s h)")
        MM(pc[0:32, 0:32], triT_f[0:32, 0:32], ln2, True, True, ["triT_f", "logf_smp"], ["pc"])
        MM(pt[:, 0:32], ones_f[0:32, :], ln2, True, True, ["ones_f", "logf_smp"], ["pt"])
        CP("dve", totn[:].rearrange("p s h -> p (s h)"), pt[:, 0:32], ["pt"], ["totn"])
        TT("dve", total[:], car_s[:, :, 15, :], tot_s[:, :, 15, :], ALU.add, ["car_s", "tot_s"], ["total"])
        TT("dve", total[:], total[:], totn[:], ALU.add, ["total", "totn"], ["total"])
        TT("dve", cw_s[:], cw_s[:], car_s[:], ALU.add, ["cw_s", "car_s"], ["cw_s"])
        for T in range(16):
            TT("dve", bias_s[:, :, :, T], total[:], cw_s[:, :, T, :], ALU.subtract, ["cw_s", "total"], ["bias_s"])
        TT("dve", bias_n[:].rearrange("p s h -> p (s h)"), totn[0:32].rearrange("p s h -> p (s h)"), pc[0:32, 0:32], ALU.subtract,
           ["pc", "totn"], ["bias_n"])
        P.flush()
        stk.pop()
        d0.close()

        d1 = ExitStack()
        stk.append(d1)
        KT_h = [sb(f"KT_h{i}", [128, NCTX], BF16) for i in range(2)]
        V_h = [sb(f"V_h{i}", [128, NT, DH], BF16) for i in range(2)]
        QT_h = [sb(f"QT_h{i}", [128, NOWN], BF16) for i in range(2)]
        Pt = [sb(f"Pt{i}", [128, 512], BF16) for i in range(3)]
        pS = [ps(f"pS{i}", [128, 512]) for i in range(3)]
        pO = [ps(f"pO{i}", [128, 512]) for i in range(2)]
        pD = [ps(f"pD{i}", [128, 512]) for i in range(2)]
        prs2 = ps("prs2", [128, 4])
        rden = [sb(f"rden{i}", [128, 512]) for i in range(2)]
        at = [sb(f"at{i}", [128, 512]) for i in range(2)]
        sqa = [sb(f"sqa{i}", [128, 512], BF16) for i in range(2)]
        mixA = [sb(f"mixA{i}", [128, 512], BF16) for i in range(2)]
        heads = range(NH) if dbg_groups is None else range(1)
        qgs = range(4) if dbg_groups is None else range(1)
        sc = 0
        for hi, h in enumerate(heads):
            hb = hi % 2
            DMA("sp", KT_h[hb][:], kT_scr[h], writes=[f"KT_h{hb}"])
            DMA("sp", V_h[hb][:].rearrange("p t d -> p (t d)"), v_scr[h].rearrange("p t d -> p (t d)"), writes=[f"V_h{hb}"])
            DMA("sp", QT_h[hb][:], qT_scr[h], writes=[f"QT_h{hb}"])
            for qg in qgs:
                q0 = qg * 512
                ob = (hi * 4 + qg) % 2
                tl = [(T, 0) for T in range(NPT + 4 * qg)] + [(NPT + 4 * qg + m_, 128 * m_) for m_ in range(4)]
                n_t = len(tl)

                def emit_S(i):
                    T, off = tl[i]
                    b_ = (sc + i) % 3
                    MM(pS[b_][:, off:512], KT_h[hb][:, T * 128:(T + 1) * 128], QT_h[hb][:, q0 + off:q0 + 512], True, True,
                       [f"KT_h{hb}", f"QT_h{hb}"], [f"pS{b_}"])

                emit_S(0)
                emit_S(1)
                for i in range(n_t):
                    T, off = tl[i]
                    b_ = (sc + i) % 3
                    ACTV(Pt[b_][:, off:512], pS[b_][:, off:512], AF.Exp, [f"pS{b_}", "biasT"], [f"Pt{b_}"],
                         bias=biasT[:, qg, h, T:T + 1], scale=SCALE)
                    if T >= NPT + 4 * qg:
                        TT("pool", Pt[b_][:, off:off + 128], Pt[b_][:, off:off + 128], triT_b[:], ALU.mult, [f"Pt{b_}", "triT_b"], [f"Pt{b_}"])
                    if i + 2 < n_t:
                        emit_S(i + 2)
                    MM(pO[ob][:, off:512], V_h[hb][:, T, :], Pt[b_][:, off:512], i == 0, i == n_t - 1, [f"V_h{hb}", f"Pt{b_}"], [f"pO{ob}"])
                    MM(pD[ob][:, off:512], ones_b[:], Pt[b_][:, off:512], i == 0, i == n_t - 1, ["ones_b", f"Pt{b_}"], [f"pD{ob}"])
                sc += n_t
                RECIP(rden[ob][:], pD[ob][:], [f"pD{ob}"], [f"rden{ob}"])
                TT("dve", at[ob][:], pO[ob][:], rden[ob][:], ALU.mult, [f"pO{ob}", f"rden{ob}"], [f"at{ob}"])
                ACTV(sqa[ob][:], at[ob][:], AF.Square, [f"at{ob}"], [f"sqa{ob}"])
                TS("dve", mixA[ob][:], at[ob][:], gatt_t[:, h:h + 1], None, ALU.mult, None, [f"at{ob}", "gatt_t"], [f"mixA{ob}"])
                DMA("sp", mixT_scr[8 + h, :, q0:q0 + 512], mixA[ob][:], reads=[f"mixA{ob}"], writes=["mixT_scr_a"])
                for t4 in range(4):
                    MM(prs2[:, t4:t4 + 1], sqa[ob][:, t4 * 128:(t4 + 1) * 128], ones_b[:, 0:1], True, True, [f"sqa{ob}", "ones_b"], ["prs2"])
                TT("dve", rss_att[:, qg * 4:qg * 4 + 4], rss_att[:, qg * 4:qg * 4 + 4], prs2[:, 0:4], ALU.add, ["rss_att", "prs2"], ["rss_att"])
        P.flush()
        stk.pop()
        d1.close()

        d2 = ExitStack()
        stk.append(d2)
        kc_f = [sb(f"kc_f{i}", [128, 16, DH]) for i in range(2)]
        vc_f = [sb(f"vc_f{i}", [128, 16, DH]) for i in range(2)]
        kc_b = sb("kc_b", [128, 16, DH], BF16)
        vc_b = [sb(f"vc_b{i}", [128, 16, DH], BF16) for i in range(2)]
        KTc = [sb(f"KTc{i}", [128, 16, 128], BF16) for i in range(2)]
        ptK = [ps(f"ptK{i}", [128, 8, 128], BF16) for i in range(2)]
        pSs = ps("pSs", [128, 16, 32])
        pSn = ps("pSn", [32, 32])
        pOs = ps("pOs", [128, 32])
        pDs = ps("pDs", [128, 32])
        tmpS = sb("tmpS", [128, 16, 32])
        Pts = sb("Pts", [128, 16, 32], BF16)
        Ptn = sb("Ptn", [32, 32], BF16)
        rdn = sb("rdn", [128, 32])
        ats = sb("ats", [128, 32])
        sqs_all = sb("sqs_all", [128, NH, NSMP], BF16)
        prs4 = ps("prs4", [128, 1])
        mixS = sb("mixS", [128, NH, NSMP], BF16)
        ck4 = ck.rearrange("s (t p) h d -> s p t h d", p=128)
        cv4 = cv.rearrange("s (t p) h d -> s p t h d", p=128)
        sh_list = [(s_, h) for s_ in range(4) for h in range(NH)] if dbg_groups is None else [(0, 0)]
        def smp_A(ii):
            s_, h = sh_list[ii]
            b2 = ii % 2
            DMA("sp", kc_f[b2][:], ck4[s_, :, :, h, :], writes=[f"kc_f{b2}"])
            DMA("sp", vc_f[b2][:], cv4[s_, :, :, h, :], writes=[f"vc_f{b2}"])
            CP("dve", kc_b[:], kc_f[b2][:], [f"kc_f{b2}"], ["kc_b"])
            CP("act", vc_b[b2][:], vc_f[b2][:], [f"vc_f{b2}"], [f"vc_b{b2}"])
            for T in range(16):
                TR(ptK[T // 8][:, T % 8, :], kc_b[:, T, :], ident_b[:], ["kc_b", "ident_b"], [f"ptK{T // 8}"])
            CP("act", KTc[b2][:, 0:8, :], ptK[0][:], ["ptK0"], [f"KTc{b2}"])
            CP("dve", KTc[b2][:, 8:16, :], ptK[1][:], ["ptK1"], [f"KTc{b2}"])

        smp_A(0)
        for ii, (s_, h) in enumerate(sh_list):
            b2 = ii % 2
            if ii + 1 < len(sh_list):
                smp_A(ii + 1)
            qs = slice(32 * s_, 32 * s_ + 32)
            for T in range(16):
                MM(pSs[:, T, :], KTc[b2][:, T, :], qT_s[:, h, qs], True, True, [f"KTc{b2}", "qT_s"], ["pSs"])
            MM(pSn[:, :], kT_s[:, h, qs], qT_s[:, h, qs], True, True, ["kT_s", "qT_s"], ["pSn"])
            STT("dve", tmpS[:], pSs[:], SCALE, bias_s[:, s_, h, :].unsqueeze(2).to_broadcast([128, 16, 32]), ALU.mult, ALU.add,
                ["pSs", "bias_s"], ["tmpS"])
            ACTV(Pts[:], tmpS[:], AF.Exp, ["tmpS"], ["Pts"])
            ACTV(Ptn[:], pSn[:], AF.Exp, ["pSn", "bias_n"], ["Ptn"], bias=bias_n[:, s_, h:h + 1], scale=SCALE)
            TT("pool", Ptn[:], Ptn[:], triT_b[0:32, 0:32], ALU.mult, ["Ptn", "triT_b"], ["Ptn"])
            for T in range(16):
                MM(pOs[:], vc_b[b2][:, T, :], Pts[:, T, :], T == 0, False, [f"vc_b{b2}", "Pts"], ["pOs"])
            MM(pOs[:], v_s[:, h, s_, :], Ptn[:], False, True, ["v_s", "Ptn"], ["pOs"])
            for T in range(16):
                MM(pDs[:], ones_b[:], Pts[:, T, :], T == 0, False, ["ones_b", "Pts"], ["pDs"])
            MM(pDs[:], ones_b[0:32, :], Ptn[:], False, True, ["ones_b", "Ptn"], ["pDs"])
            RECIP(rdn[:], pDs[:], ["pDs"], ["rdn"])
            TT("dve", ats[:], pOs[:], rdn[:], ALU.mult, ["pOs", "rdn"], ["ats"])
            ACTV(sqs_all[:, h, qs], ats[:], AF.Square, ["ats"], ["sqs_all"])
            TS("dve", mixS[:, h, qs], ats[:], gatt_t[:, h:h + 1], None, ALU.mult, None, ["ats", "gatt_t"], ["mixS"])
        DMA("sp", mixT_scr[8:16, :, NOWN:NOWN + NSMP].rearrange("f p n -> p f n"), mixS[:], reads=["mixS"], writes=["mixT_scr_a"])
        if dbg_groups is None:
            for h in range(NH):
                MM(prs4[:, :], sqs_all[:, h, :], ones_b[:, 0:1], h == 0, h == NH - 1, ["sqs_all", "ones_b"], ["prs4"])
            CP("dve", rss_att[:, 16:17], prs4[:, :], ["prs4"], ["rss_att"])
        ACTV(rstd_att[:], rss_att[:], AF.Sqrt, ["rss_att"], ["rstd_att"], bias=EPS, scale=1.0 / DSSM)
        RECIP(rstd_att[:], rstd_att[:], ["rstd_att"], ["rstd_att"])
        P.flush()
        stk.pop()
        d2.close()
        stk.pop()
        dd.close()


    if "E" in phases:
        if conv_jobs:
            convert_E_weights()
            P.flush()
        ee = ExitStack()
        stk.append(ee)
        m = linear_machinery(nw=2, nh=1, with_xt=False)
        hT, wbuf, pmm = m.hT, m.wbuf, m.pmm
        pq = [ps(f"pq{i}", [128, 512]) for i in range(4)]
        x1 = sb("x1", [128, 4, D])
        aT = sb("aT", [128, 64, 512], BF16)
        mixT = sb("mixT", [128, 16, 512], BF16)
        rr = [sb(f"rr{i}", [128, 512]) for i in range(2)]
        yb = [sb(f"yb{i}", [128, 512]) for i in range(2)]
        if dbg_groups is None:
            egroups = [("own", g_) for g_ in range(4)] + [("smp", 0)]
        else:
            egroups = [("own", 0), ("smp", 0)]
        cnt = 0
        eloads = []
        for _ in egroups:
            eloads += [(scr_wout[c], f"scr_wout{c}") for c in range(4)]
            eloads += [(scr_wup[c], f"scr_wup{c}") for c in range(16)]
            eloads += [(scr_wdn[d_, f_], f"scr_wdn{d_}_{f_}") for d_ in range(4) for f_ in range(4)]
        ewq = []
        epos = [0]

        def e_next_chunk():
            if not ewq:
                ewq.append(load_wchunk(m, *eloads[epos[0]]))
                epos[0] += 1
            wi_ = ewq.pop(0)
            if epos[0] < len(eloads):
                ewq.append(load_wchunk(m, *eloads[epos[0]]))
                epos[0] += 1
            return wi_

        for kind, g_ in egroups:
            L = 128
            ntl = 4 if kind == "own" else 1
            ncol = ntl * L
            c0 = g_ * 512 if kind == "own" else NOWN
            DMA("sp", mixT[:, :, 0:ncol], mixT_scr[:, :, c0:c0 + ncol].rearrange("f p n -> p f n"), writes=["mixT"], nonc=(ncol < 512))
            for tt in range(ntl):
                src = x_ctx[NPRE + c0 + tt * 128:NPRE + c0 + (tt + 1) * 128, :] if kind == "own" else x_smp[0:128, :]
                DMA("sp", x1[0:L, tt, :], src, writes=[f"x1_{tt}"])
            for c in range(4):
                wi = e_next_chunk()
                cs = slice(c * 512, (c + 1) * 512)
                for tt in range(ntl):
                    tcol = (g_ * 4 + tt) if kind == "own" else 16
                    pa, pb = pq[(cnt % 2) * 2], pq[(cnt % 2) * 2 + 1]
                    ta, tb = f"pq{(cnt % 2) * 2}", f"pq{(cnt % 2) * 2 + 1}"
                    cnt += 1
                    for ft in range(8):
                        MM(pa[0:L, :], mixT[:, ft, tt * L:(tt + 1) * L], wbuf[wi][:, ft, :], ft == 0, ft == 7, ["mixT", f"wbuf{wi}"], [ta])
                    for ft in range(8, 16):
                        MM(pb[0:L, :], mixT[:, ft, tt * L:(tt + 1) * L], wbuf[wi][:, ft, :], ft == 8, ft == 15, ["mixT", f"wbuf{wi}"], [tb])
                    STT("dve", x1[0:L, tt, cs], pa[0:L, :], rstd_ssm[0:L, tcol:tcol + 1], x1[0:L, tt, cs], ALU.mult, ALU.add,
                        [ta, f"x1_{tt}"], [f"x1_{tt}"])
                    STT("dve", x1[0:L, tt, cs], pb[0:L, :], rstd_att[0:L, tcol:tcol + 1], x1[0:L, tt, cs], ALU.mult, ALU.add,
                        [tb, f"x1_{tt}"], [f"x1_{tt}"])
            for tt in range(ntl):
                front_end(m, None, x1[:, tt, :], f"x1_{tt}", L, gmlpT, "gmlpT", 0, tt * L, f"hT0_{tt}")
            htoks = [f"hT0_{tt}" for tt in range(ntl)]
            for ffc in range(16):
                wi = e_next_chunk()
                for f4 in range(4):
                    fft = ffc * 4 + f4
                    j = fft % 2
                    for kt in range(KT):
                        MM(pmm[j][:, 0:ncol], wbuf[wi][:, kt, f4 * 128:(f4 + 1) * 128], hT[0][:, kt, 0:ncol], kt == 0, kt == KT - 1,
                           htoks + [f"wbuf{wi}"], [f"pmm{j}"])
                    ACTV(rr[j][:, 0:ncol], pmm[j][:, 0:ncol], AF.Relu, [f"pmm{j}"], [f"rr{j}"])
                    TT("pool" if fft % 4 == 3 else "dve", aT[:, fft, 0:ncol], rr[j][:, 0:ncol], rr[j][:, 0:ncol], ALU.mult, [f"rr{j}"], [f"aT{fft}"])
            for dmc in range(4):
                ds_ = slice(dmc * 512, (dmc + 1) * 512)
                for fq in range(4):
                    wi = e_next_chunk()
                    for tt in range(ntl):
                        for kt in range(KT):
                            fft = fq * 16 + kt
                            MM(pq[tt][0:L, :], aT[:, fft, tt * L:(tt + 1) * L], wbuf[wi][:, kt, :], fq == 0 and kt == 0, fq == 3 and kt == KT - 1,
                               [f"aT{fft}", f"wbuf{wi}"], [f"pq{tt}"])
                for tt in range(ntl):
                    j = tt % 2
                    TT("dve", yb[j][0:L], pq[tt][0:L, :], x1[0:L, tt, ds_], ALU.add, [f"pq{tt}", f"x1_{tt}"], [f"yb{j}"])
                    if kind == "own":
                        dst = y_own[c0 + tt * 128:c0 + (tt + 1) * 128, ds_]
                    else:
                        dst = y_smp[0:128, ds_]
                    DMA("sp", dst, yb[j][0:L], reads=[f"yb{j}"], out_store=True)
        P.flush()
        stk.pop()
        ee.close()

    if P.ops:
        P.flush()
    es.close()
    return nc, P


def _host_consts():
    ident = np.eye(128, dtype=np.float32)
    s = np.arange(128)
    triT = (s[:, None] <= s[None, :]).astype(np.float32)
    iota1 = np.broadcast_to((np.arange(128) + 1).astype(np.float32)[None, :], (128, 128)).copy()
    sp1 = (np.arange(128) + 1).astype(np.float32)[:, None].copy()
    return ident, triT, iota1, sp1


def _prep_inputs(inp):
    f32 = np.float32
    ident, triT, iota1, sp1 = _host_consts()
    b_re = np.asarray(inp["ssm_b_re"], f32)[0]
    b_im = np.asarray(inp["ssm_b_im"], f32)[0]
    c_re = np.asarray(inp["ssm_c_re"], f32)[0]
    c_im = np.asarray(inp["ssm_c_im"], f32)[0]

    def bblk(b):
        out = np.zeros((128, 8, 512), f32)
        for fb in range(8):
            for gl in range(8):
                g = fb * 8 + gl
                out[gl * 16:(gl + 1) * 16, fb, gl * 64:(gl + 1) * 64] = b[g].T
        return out

    def cblk(c):
        out = np.zeros((128, 8, 4, 128), f32)
        for fb in range(8):
            for k in range(4):
                for e in range(2):
                    g = fb * 8 + 2 * k + e
                    out[e * 64:(e + 1) * 64, fb, k, 16 * (2 * k + e):16 * (2 * k + e + 1)] = c[g].T
        return out

    def bpad(b):
        out = np.zeros((128, 32, 32), f32)
        for j in range(32):
            for e in range(2):
                out[e * 64:(e + 1) * 64, j, e * 16:(e + 1) * 16] = b[2 * j + e]
        return out

    shared = {
        "g_norm_mix": np.asarray(inp["g_norm_mix"], f32)[0],
        "w_in": np.asarray(inp["w_in"], f32)[0],
        "b_f": np.asarray(inp["b_f"], f32),
        "ssm_a_re": np.asarray(inp["ssm_a_re"], f32).reshape(1, G * PST),
        "ssm_a_im": np.asarray(inp["ssm_a_im"], f32).reshape(1, G * PST),
        "ssm_log_step": np.asarray(inp["ssm_log_step"], f32).reshape(1, G),
        "bblk_re": bblk(b_re), "bblk_im": bblk(b_im),
        "cblk_re": cblk(c_re), "cblk_im": cblk(c_im),
        "bpad_re": bpad(b_re), "bpad_im": bpad(b_im),
        "ssm_d": np.asarray(inp["ssm_d"], f32).reshape(G * 16),
        "w_glu": np.asarray(inp["w_glu"], f32)[0],
        "g_q": np.asarray(inp["g_q"], f32), "g_k": np.asarray(inp["g_k"], f32),
        "g_out_ssm": np.asarray(inp["g_out_ssm"], f32)[0],
        "g_out_attn": np.asarray(inp["g_out_attn"], f32)[0],
        "w_out": np.asarray(inp["w_out"], f32)[0],
        "g_norm_mlp": np.asarray(inp["g_norm_mlp"], f32)[0],
        "w_up": np.asarray(inp["w_up"], f32)[0],
        "w_down": np.asarray(inp["w_down"], f32)[0],
        "c_ident": ident, "c_triT": triT, "c_iota1": iota1, "c_sp1": sp1,
    }
    xp = np.asarray(inp["x_prompt"], f32)
    xs = np.asarray(inp["x_sample"], f32)
    cks = np.asarray(inp["cache_k"], f32)[0]
    cvs = np.asarray(inp["cache_v"], f32)[0]
    clfs = np.asarray(inp["cache_logf"], f32)[0]
    sres = np.asarray(inp["state_ssm_re"], f32)[0].reshape(32, G * PST)
    sims = np.asarray(inp["state_ssm_im"], f32)[0].reshape(32, G * PST)
    maps = []
    for c in range(8):
        b, j = c // 4, c % 4
        x_ctx = np.zeros((NCTX, D), f32)
        nv = (3 - j) * NOWN
        x_ctx[nv:] = xp[b, 0:(j + 1) * NOWN]
        km = np.zeros((NCTX,), f32)
        km[:nv] = -30000.0
        m = dict(shared)
        m.update({
            "x_ctx": x_ctx,
            "x_smp": np.ascontiguousarray(xs[4 * c:4 * c + 4].reshape(NSMP, D)),
            "ck": np.ascontiguousarray(cks[4 * c:4 * c + 4]),
            "cv": np.ascontiguousarray(cvs[4 * c:4 * c + 4]),
            "clf": np.ascontiguousarray(clfs[4 * c:4 * c + 4]),
            "sre": np.ascontiguousarray(sres[4 * c:4 * c + 4]),
            "sim": np.ascontiguousarray(sims[4 * c:4 * c + 4]),
            "c_kmask": np.ascontiguousarray(km.reshape(NT, 128).T),
        })
        maps.append(m)
    return maps


_CACHE = {}


def kernel(**inputs):
    if "nc" not in _CACHE:
        _CACHE["nc"] = build_program()[0]
    nc = _CACHE["nc"]
    maps = _prep_inputs(inputs)
    res = run_bass_kernel_spmd(nc, maps, core_ids=list(range(8)))
    R = res.results
    f32 = np.float32
    y_p = np.zeros((2, 8192, D), f32)
    k_p = np.zeros((1, 2, 8192, NH, DH), f32)
    v_p = np.zeros((1, 2, 8192, NH, DH), f32)
    lf_p = np.zeros((1, 2, 8192, NH), f32)
    hre_p = np.zeros((1, 2, G, PST), f32)
    him_p = np.zeros((1, 2, G, PST), f32)
    y_s = np.zeros((32, 32, D), f32)
    k_s = np.zeros((1, 32, 32, NH, DH), f32)
    v_s = np.zeros((1, 32, 32, NH, DH), f32)
    lf_s = np.zeros((1, 32, 32, NH), f32)
    hre_s = np.zeros((1, 32, G, PST), f32)
    him_s = np.zeros((1, 32, G, PST), f32)
    for c in range(8):
        b, j = c // 4, c % 4
        r = R[c]
        sl = slice(j * NOWN, (j + 1) * NOWN)
        y_p[b, sl] = r["y_own"]
        k_p[0, b, sl] = r["k_own"].reshape(NOWN, NH, DH)
        v_p[0, b, sl] = r["v_own"].reshape(NOWN, NH, DH)
        lf_p[0, b, sl] = r["lf_own"]
        if j == 3:
            hre_p[0, b] = r["hre_own"].reshape(G, PST)
            him_p[0, b] = r["him_own"].reshape(G, PST)
        s4 = slice(4 * c, 4 * c + 4)
        y_s[s4] = r["y_smp"].reshape(4, 32, D)
        k_s[0, s4] = r["k_smp"].reshape(4, 32, NH, DH)
        v_s[0, s4] = r["v_smp"].reshape(4, 32, NH, DH)
        lf_s[0, s4] = r["lf_smp"].reshape(4, 32, NH)
        hre_s[0, s4] = r["hre_smp"].reshape(4, G, PST)
        him_s[0, s4] = r["him_smp"].reshape(4, G, PST)
    return (y_p, y_s, k_p, v_p, lf_p, hre_p, him_p, k_s, v_s, lf_s, hre_s, him_s)
```

```python
import numpy as np
from contextlib import ExitStack
import concourse.bass as bass
import concourse.mybir as mybir
from concourse.bass_utils import run_bass_kernel_spmd

F32 = mybir.dt.float32
BF16 = mybir.dt.bfloat16
I32 = mybir.dt.int32
AF = mybir.ActivationFunctionType
ALU = mybir.AluOpType
AX = mybir.AxisListType

ENGINES = ("pe", "act", "dve", "pool", "sp")
WAR_GUARD = True
QS = ("sp", "pool", "act")


class Op:
    __slots__ = ("eng", "fn", "reads", "writes", "dma", "deps", "ticket", "has_dep",
                 "dsem", "dval", "prev_same_sem", "out_store")

    def __init__(self, eng, fn, reads, writes, dma=False, out_store=False):
        self.eng = eng
        self.fn = fn
        self.reads = tuple(reads)
        self.writes = tuple(writes)
        self.dma = dma
        self.deps = ()
        self.ticket = None
        self.has_dep = False
        self.dsem = None
        self.dval = None
        self.prev_same_sem = None
        self.out_store = out_store


class Prog:
    def __init__(self, nc, esems, dsems, n_dma_sems):
        self.nc = nc
        self.ops = []
        self.n = n_dma_sems
        self.esems = esems
        self.dsems = dsems
        self.cnt = {e: 0 for e in ENGINES}
        self.k = {q: 0 for q in QS}
        self.semcnt = [0] * (n_dma_sems * len(QS))
        self.nseg = 0
        self.tot_ops = 0
        self.psum_tokens = set()

    def add(self, eng, fn, reads=(), writes=(), dma=False, out_store=False):
        xs = [r + "#x" for r in reads if r in self.psum_tokens]
        if xs:
            reads = tuple(reads) + tuple(xs)
            writes = tuple(writes) + tuple(xs)
        self.ops.append(Op(eng, fn, reads, writes, dma, out_store))

    def pe(self, fn, reads=(), writes=()):
        self.add("pe", fn, reads, writes)

    def act(self, fn, reads=(), writes=()):
        self.add("act", fn, reads, writes)

    def dve(self, fn, reads=(), writes=()):
        self.add("dve", fn, reads, writes)

    def pool(self, fn, reads=(), writes=()):
        self.add("pool", fn, reads, writes)

    def dma(self, q, fn, reads=(), writes=(), out_store=False):
        self.add(q, fn, reads, writes, dma=True, out_store=out_store)

    def _resolve(self):
        last_w = {}
        readers = {}
        ops = self.ops
        eng_ops = {e: [] for e in ENGINES}
        eng_pos = {}
        for i, op in enumerate(ops):
            if not op.dma:
                eng_pos[i] = len(eng_ops[op.eng])
                eng_ops[op.eng].append(i)
        for i, op in enumerate(ops):
            deps = set()
            for r in op.reads:
                w = last_w.get(r)
                if w is not None:
                    deps.add(w)
            for w_ in op.writes:
                w = last_w.get(w_)
                if w is not None:
                    deps.add(w)
                rl = readers.get(w_, ())
                lastc = {}
                for rd in rl:
                    if ops[rd].dma:
                        deps.add(rd)
                    else:
                        lastc[ops[rd].eng] = rd
                for e_, rd in lastc.items():
                    if WAR_GUARD and e_ != op.eng:
                        lst = eng_ops[e_]
                        pos = eng_pos[rd]
                        if pos + 1 < len(lst) and lst[pos + 1] < i:
                            rd = lst[pos + 1]
                    deps.add(rd)
            deps.discard(i)
            if op.eng == "pe" and not op.dma:
                deps = {d for d in deps if not (ops[d].eng == "pe" and not ops[d].dma)}
            op.deps = tuple(sorted(deps))
            for d in op.deps:
                ops[d].has_dep = True
            for r in op.reads:
                readers.setdefault(r, []).append(i)
            for w_ in op.writes:
                last_w[w_] = i
                readers[w_] = []
        last = {}
        for op in ops:
            if not op.dma:
                last[op.eng] = op
        for op in last.values():
            op.has_dep = True
        for op in ops:
            if not op.dma and op.has_dep:
                self.cnt[op.eng] += 1
                op.ticket = self.cnt[op.eng]
        n = self.n
        for op in ops:
            if op.dma:
                qi = QS.index(op.eng)
                s = qi * n + (self.k[op.eng] % n)
                self.k[op.eng] += 1
                op.prev_same_sem = self.semcnt[s]
                self.semcnt[s] += 16
                op.dsem = s
                op.dval = self.semcnt[s]

    def flush(self):
        self._resolve()
        ops = self.ops
        esems, dsems = self.esems, self.dsems
        per = {e: [] for e in ENGINES}
        for i, op in enumerate(ops):
            per[op.eng].append(i)
        fin_e = dict(self.cnt)
        fin_d = list(self.semcnt)
        seg = self.nseg

        def run(eng_name, eng):
            seen_e = {e: 0 for e in ENGINES}
            seen_d = [0] * len(dsems)
            for i in per[eng_name]:
                op = ops[i]
                for d in op.deps:
                    dop = ops[d]
                    if dop.dma:
                        if seen_d[dop.dsem] < dop.dval:
                            eng.wait_ge(dsems[dop.dsem], dop.dval)
                            seen_d[dop.dsem] = dop.dval
                    else:
                        if seen_e[dop.eng] < dop.ticket:
                            eng.wait_ge(esems[dop.eng], dop.ticket)
                            seen_e[dop.eng] = dop.ticket
                if op.dma:
                    if op.prev_same_sem > 0 and seen_d[op.dsem] < op.prev_same_sem:
                        eng.wait_ge(dsems[op.dsem], op.prev_same_sem)
                        seen_d[op.dsem] = op.prev_same_sem
                    ins = op.fn(eng)
                    ins.then_inc(dsems[op.dsem], 16)
                else:
                    ins = op.fn(eng)
                    if op.ticket is not None:
                        ins.then_inc(esems[op.eng], 1)
            for e2 in ("pe", "act", "dve", "pool"):
                if fin_e[e2] > 0 and seen_e[e2] < fin_e[e2]:
                    eng.wait_ge(esems[e2], fin_e[e2])
            for si, v in enumerate(fin_d):
                if v > 0 and seen_d[si] < v:
                    eng.wait_ge(dsems[si], v)

        with self.nc.Block(f"seg{seg}") as block:
            @block.sync
            def _(e):
                run("sp", e)

            @block.tensor
            def _(e):
                run("pe", e)

            @block.scalar
            def _(e):
                run("act", e)

            @block.vector
            def _(e):
                run("dve", e)

            @block.gpsimd
            def _(e):
                run("pool", e)
        self.tot_ops += len(ops)
        self.ops = []
        self.nseg += 1


D = 2048
KT = 16
DIN = 4104
NH = 8
DH = 128
DSSM = 1024
G = 64
PST = 64
DFF = 8192
NCTX = 8192
NPRE = 6144
NOWN = 2048
NT = 64
NPT = 48
NSMP = 128
PAST = 2048
EPS = 1e-6
NTOK = NOWN + NSMP

PHASES = ("B", "C", "D", "E")


def build_program(phases=PHASES, dbg_groups=None, dbg_chunks=None):
    nc = bass.Bass("TRN2", target_bir_lowering=False)
    es = ExitStack()

    def din(name, shape, dt=F32):
        return nc.dram_tensor(name, list(shape), dt, kind="ExternalInput").ap()

    def dout(name, shape, dt=F32):
        return nc.dram_tensor(name, list(shape), dt, kind="ExternalOutput").ap()

    def dscr(name, shape, dt=BF16):
        return nc.dram_tensor(name, list(shape), dt).ap()

    x_ctx = din("x_ctx", [NCTX, D])
    x_smp = din("x_smp", [NSMP, D])
    ck = din("ck", [4, PAST, NH, DH])
    cv = din("cv", [4, PAST, NH, DH])
    clf = din("clf", [4, PAST, NH])
    sre = din("sre", [4, G * PST])
    sim = din("sim", [4, G * PST])
    g_norm_mix = din("g_norm_mix", [D])
    w_in = din("w_in", [D, DIN])
    b_f = din("b_f", [1, NH])
    a_re = din("ssm_a_re", [1, G * PST])
    a_im = din("ssm_a_im", [1, G * PST])
    log_step = din("ssm_log_step", [1, G])
    bblk_re = din("bblk_re", [128, 8, 512])
    bblk_im = din("bblk_im", [128, 8, 512])
    cblk_re = din("cblk_re", [128, 8, 4, 128])
    cblk_im = din("cblk_im", [128, 8, 4, 128])
    bpad_re = din("bpad_re", [128, 32, 32])
    bpad_im = din("bpad_im", [128, 32, 32])
    ssm_d = din("ssm_d", [G * 16])
    w_glu = din("w_glu", [DSSM, DSSM])
    g_q = din("g_q", [1, DH])
    g_k = din("g_k", [1, DH])
    g_out_ssm = din("g_out_ssm", [DSSM])
    g_out_attn = din("g_out_attn", [DSSM])
    w_out = din("w_out", [D, D])
    g_norm_mlp = din("g_norm_mlp", [D])
    w_up = din("w_up", [D, DFF])
    w_down = din("w_down", [DFF, D])
    c_ident = din("c_ident", [128, 128])
    c_triT = din("c_triT", [128, 128])
    c_kmask = din("c_kmask", [128, NT])
    c_iota1 = din("c_iota1", [128, 128])
    c_sp1 = din("c_sp1", [128, 1])

    y_own = dout("y_own", [NOWN, D])
    y_smp = dout("y_smp", [NSMP, D])
    k_own = dout("k_own", [NOWN, NH * DH])
    v_own = dout("v_own", [NOWN, NH * DH])
    lf_own = dout("lf_own", [NOWN, NH])
    hre_own = dout("hre_own", [G * PST])
    him_own = dout("him_own", [G * PST])
    k_smp = dout("k_smp", [NSMP, NH * DH])
    v_smp = dout("v_smp", [NSMP, NH * DH])
    lf_smp = dout("lf_smp", [NSMP, NH])
    hre_smp = dout("hre_smp", [4, G * PST])
    him_smp = dout("him_smp", [4, G * PST])

    scr_win = dscr("scr_win", [8, 128, KT, 512])
    scr_wout = dscr("scr_wout", [4, 128, KT, 512])
    scr_wup = dscr("scr_wup", [16, 128, KT, 512])
    scr_wdn = dscr("scr_wdn", [4, 4, 128, KT, 512])
    kT_scr = dscr("kT_scr", [NH, 128, NCTX])
    v_scr = dscr("v_scr", [NH, 128, NT, DH])
    qT_scr = dscr("qT_scr", [NH, 128, NOWN])
    uT_scr = dscr("uT_scr", [8, 128, NTOK])
    u_scr = dscr("u_scr", [NPT, 128, DSSM])
    mixT_scr = dscr("mixT_scr", [16, 128, NTOK])

    stk = [es]

    uid = [0]

    def sb(name, shape, dt=F32):
        uid[0] += 1
        return stk[-1].enter_context(nc.sbuf_tensor(f"{name}_{uid[0]}", list(shape), dt))

    def ps(name, shape, dt=F32, toks=None):
        P.psum_tokens.update(toks if toks is not None else [name])
        uid[0] += 1
        return stk[-1].enter_context(nc.psum_tensor(f"{name}_{uid[0]}", list(shape), dt))

    NDS = 16
    sems = [es.enter_context(nc.semaphore(f"s{i}")) for i in range(4 + NDS * len(QS))]
    esems = {"pe": sems[0], "act": sems[1], "dve": sems[2], "pool": sems[3]}
    P = Prog(nc, esems, sems[4:], NDS)

    def DMA(q, out, in_, reads=(), writes=(), out_store=False, nonc=False):
        if nonc:
            def fn(e, out=out, in_=in_):
                with nc.allow_non_contiguous_dma(reason="small strided table"):
                    return e.dma_start(out=out, in_=in_)
        else:
            def fn(e, out=out, in_=in_):
                return e.dma_start(out=out, in_=in_)
        P.dma(q, fn, reads=reads, writes=writes, out_store=out_store)

    ident_f = sb("ident_f", [128, 128])
    ident_b = sb("ident_b", [128, 128], BF16)
    triT_f = sb("triT_f", [128, 128])
    triT_b = sb("triT_b", [128, 128], BF16)
    ones_b = sb("ones_b", [128, 128], BF16)
    ones_f = sb("ones_f", [128, 128])
    kmask = sb("kmask", [128, NT])
    gmixT = sb("gmixT", [128, KT])
    gmlpT = sb("gmlpT", [128, KT])
    gq_bc = sb("gq_bc", [128, DH])
    gk_bc = sb("gk_bc", [128, DH])
    bf_bc = sb("bf_bc", [128, NH])
    wf_b = sb("wf_b", [128, KT, NH], BF16)
    logf_all = sb("logf_all", [128, NT, NH])
    logf_smp = sb("logf_smp", [32, 4, NH])
    kT_s = sb("kT_s", [128, NH, NSMP], BF16)
    qT_s = sb("qT_s", [128, NH, NSMP], BF16)
    v_s = sb("v_s", [32, NH, 4, DH], BF16)

    DMA("sp", ident_f[:], c_ident, writes=["ident_f"])
    DMA("sp", triT_f[:], c_triT, writes=["triT_f"])
    DMA("sp", kmask[:], c_kmask, writes=["kmask"])
    DMA("sp", gmixT[:], g_norm_mix.rearrange("(kt p) -> p kt", p=128), writes=["gmixT"], nonc=True)
    DMA("sp", gmlpT[:], g_norm_mlp.rearrange("(kt p) -> p kt", p=128), writes=["gmlpT"], nonc=True)
    DMA("sp", gq_bc[:], g_q.broadcast_to([128, DH]), writes=["gq_bc"])
    DMA("sp", gk_bc[:], g_k.broadcast_to([128, DH]), writes=["gk_bc"])
    DMA("sp", bf_bc[:], b_f.broadcast_to([128, NH]), writes=["bf_bc"])
    P.dve(lambda e: e.tensor_copy(ident_b[:], ident_f[:]), reads=["ident_f"], writes=["ident_b"])
    P.dve(lambda e: e.tensor_copy(triT_b[:], triT_f[:]), reads=["triT_f"], writes=["triT_b"])
    P.dve(lambda e: e.memset(ones_b[:], 1.0), writes=["ones_b"])
    P.dve(lambda e: e.memset(ones_f[:], 1.0), writes=["ones_f"])

    def conv_chunk(dst, src_w, c0, tok):
        DMA("pool", dst, src_w[:, c0:c0 + 512].rearrange("(kt p) n -> p kt n", p=128), writes=[tok])

    WIN_CH = {"u0": 0, "u1": 1, "q0": 2, "q1": 3, "k0": 4, "k1": 5, "v0": 6, "v1": 7}
    DMA("pool", wf_b[:], w_in[:, 4096:4104].rearrange("(kt p) n -> p kt n", p=128), writes=["wf_b"], nonc=True)
    conv_jobs = []
    for ci in range(4):
        conv_jobs.append((scr_wout[ci], w_out[:, ci * 512:(ci + 1) * 512].rearrange("(kt p) n -> p kt n", p=128), f"scr_wout{ci}"))
    for ci in range(16):
        conv_jobs.append((scr_wup[ci], w_up[:, ci * 512:(ci + 1) * 512].rearrange("(kt p) n -> p kt n", p=128), f"scr_wup{ci}"))
    for dmc in range(4):
        for fq in range(4):
            conv_jobs.append((scr_wdn[dmc, fq],
                              w_down[fq * 2048:(fq + 1) * 2048, dmc * 512:(dmc + 1) * 512].rearrange("(kt p) n -> p kt n", p=128),
                              f"scr_wdn{dmc}_{fq}"))

    def convert_E_weights(n=None):
        k_ = 0
        while conv_jobs and (n is None or k_ < n):
            dst_, src_, tok_ = conv_jobs.pop(0)
            DMA("pool", dst_, src_, writes=[tok_])
            k_ += 1

    conv_done = [False]

    def MM(out, lhsT, rhs, start, stop, reads, writes):
        P.pe(lambda e: e.matmul(out, lhsT, rhs, start=start, stop=stop), reads, writes)

    def TR(out, in_, idn, reads, writes):
        P.pe(lambda e: e.transpose(out, in_, idn), reads, writes)

    def ACTV(out, in_, func, reads, writes, **kw):
        P.act(lambda e: e.activation(out, in_, func, **kw), reads, writes)

    def ENG(eng):
        return {"dve": P.dve, "pool": P.pool, "act": P.act}[eng]

    def TT(eng, out, in0, in1, op, reads, writes):
        ENG(eng)(lambda e: e.tensor_tensor(out, in0, in1, op), reads, writes)

    def TS(eng, out, in0, s1, s2, op0, op1, reads, writes):
        if op1 is None:
            ENG(eng)(lambda e: e.tensor_scalar(out, in0, s1, None, op0), reads, writes)
        else:
            ENG(eng)(lambda e: e.tensor_scalar(out, in0, s1, s2, op0, op1), reads, writes)

    def STT(eng, out, in0, scalar, in1, op0, op1, reads, writes):
        ENG(eng)(lambda e: e.scalar_tensor_tensor(out, in0, scalar, in1, op0, op1), reads, writes)

    def CP(eng, out, in_, reads, writes):
        if eng == "act":
            P.act(lambda e: e.copy(out, in_), reads, writes)
        else:
            ENG(eng)(lambda e: e.tensor_copy(out, in_), reads, writes)

    def RED(out, in_, reads, writes):
        P.dve(lambda e: e.tensor_reduce(out, in_, AX.X, ALU.add), reads, writes)

    def RECIP(out, in_, reads, writes):
        P.dve(lambda e: e.reciprocal(out, in_), reads, writes)

    def MEMSET(eng, ap, val, writes):
        ENG(eng)(lambda e: e.memset(ap, val), (), writes)

    def h4(ap):
        return ap.rearrange("p (h d) -> p h d", h=4)

    a0 = ExitStack()
    stk.append(a0)
    stg_f = [sb(f"stg_f{i}", [128, 2, 512]) for i in range(4)]
    stg_b = [sb(f"stg_b{i}", [128, KT, 512], BF16) for i in range(2)]
    sc_ = 0
    for n_, nm in enumerate(["u0", "u1", "k0", "k1", "v0", "v1", "q0", "q1"]):
        ci = WIN_CH[nm]
        bb = n_ % 2
        for k2 in range(KT // 2):
            fb_ = sc_ % 4
            sc_ += 1
            DMA("sp", stg_f[fb_][:], w_in[k2 * 256:(k2 + 1) * 256, ci * 512:(ci + 1) * 512].rearrange("(kt p) n -> p kt n", p=128),
                writes=[f"stg_f{fb_}"])
            CP("act" if k2 % 2 == 0 else "dve", stg_b[bb][:, 2 * k2:2 * k2 + 2, :], stg_f[fb_][:], [f"stg_f{fb_}"], [f"stg_b{bb}"])
        DMA("sp", scr_win[ci], stg_b[bb][:], reads=[f"stg_b{bb}"], writes=[f"scr_win{ci}"])
    P.flush()
    stk.pop()
    a0.close()

    class LM:
        pass

    def linear_machinery(nw=3, nh=2, with_xt=True, npm=2):
        m = LM()
        m.nw = nw
        m.wbuf = [sb(f"wbuf{i}", [128, KT, 512], BF16) for i in range(nw)]
        m.hT = [sb(f"hT{i}", [128, KT, 512], BF16) for i in range(nh)]
        m.xt = [sb(f"xt{i}", [128, D]) for i in range(2)] if with_xt else None
        m.xn = [sb(f"xn{i}", [128, D], BF16) for i in range(2)]
        m.ss = [sb(f"ss{i}", [128, 1]) for i in range(2)]
        m.rstd = [sb(f"rstd{i}", [128, 1]) for i in range(2)]
        m.pxT = ps("pxT", [128, KT, 128], BF16, toks=["pxT0", "pxT1"])
        m.pmm = [ps(f"pmm{i}", [128, 512]) for i in range(npm)]
        m.wcount = 0
        m.fe_count = 0
        return m

    def load_wchunk(m, src_ap, src_tok):
        i = m.wcount % m.nw
        m.wcount += 1
        DMA("sp", m.wbuf[i][:], src_ap, reads=[src_tok], writes=[f"wbuf{i}"])
        return i

    def front_end(m, src_dram, src_sb, src_tok, L, gT, gT_tok, hbuf, col0, htok, defer=False):
        b = m.fe_count % 2
        m.fe_count += 1
        xt, xn, ss, rstd, pxT, hT = m.xt, m.xn, m.ss, m.rstd, m.pxT, m.hT
        if src_dram is not None:
            DMA("sp", xt[b][0:L], src_dram, writes=[f"xt{b}"])
            src = xt[b]
            stok = f"xt{b}"
        else:
            src = src_sb
            stok = src_tok
        def head():
            MEMSET("dve", ss[b][0:L], 0.0, [f"ss{b}"])
            ACTV(xn[b][0:L], src[0:L], AF.Square, [stok, f"ss{b}"], [f"xn{b}", f"ss{b}"], accum_out=ss[b][0:L, 0:1])
            ACTV(rstd[b][0:L], ss[b][0:L], AF.Sqrt, [f"ss{b}"], [f"rstd{b}"], bias=EPS, scale=1.0 / D)
            RECIP(rstd[b][0:L], rstd[b][0:L], [f"rstd{b}"], [f"rstd{b}"])
            ACTV(xn[b][0:L], src[0:L], AF.Copy, [stok, f"rstd{b}"], [f"xn{b}"], scale=rstd[b][0:L, 0:1])

        if defer != "3":
            head()

        def tail():
            for kt in range(KT):
                TR(pxT[:, kt, 0:L], xn[b][0:L, kt * 128:(kt + 1) * 128], ident_b[0:L, 0:L], [f"xn{b}", "ident_b"], [f"pxT{kt // 8}"])
            for half in range(2):
                TT("dve", hT[hbuf][:, half * 8:(half + 1) * 8, col0:col0 + L], pxT[:, half * 8:(half + 1) * 8, 0:L],
                   gT[:, half * 8:(half + 1) * 8].unsqueeze(2).to_broadcast([128, 8, L]), ALU.mult,
                   [f"pxT{half}", gT_tok], [htok])

        if defer == "3":
            return head, tail
        if defer:
            return tail
        tail()
        return None

    if "B" in phases:
        pes = ExitStack()
        stk.append(pes)
        m = linear_machinery(npm=3)
        hT, wbuf, pmm = m.hT, m.wbuf, m.pmm
        ptr_k = ps("ptr_k", [128, 4, 128], BF16)
        ptr_u = ps("ptr_u", [128, 8, 128], BF16)
        pf4 = ps("pf4", [128, 4, NH])
        ft4 = sb("ft4", [128, 4, NH])
        sqs = [sb(f"sqs{i}", [128, 512]) for i in range(2)]
        ssh = [sb(f"ssh{i}", [128, 4]) for i in range(2)]
        knf = [sb(f"knf{i}", [128, 512]) for i in range(2)]
        kraw = [sb(f"kraw{i}", [128, 512]) for i in range(2)]
        kout = [sb(f"kout{i}", [128, 512]) for i in range(2)]
        kbf = [sb(f"kbf{i}", [128, 512], BF16) for i in range(2)]
        vout = [sb(f"vout{i}", [128, 512]) for i in range(2)]
        kT_grp = [sb(f"kT_grp{i}", [128, NH, 512], BF16) for i in range(2)]
        qT_grp = [sb("qT_grp0", [128, NH, 512], BF16)] * 2
        v_grp = [sb(f"v_grp{i}", [128, NH, 4, DH], BF16) for i in range(2)]
        u_bf = [sb(f"u_bf{i}", [128, DSSM], BF16) for i in range(4)]
        uT_grp = [sb("uT_grp0", [128, 8, 512], BF16)] * 2
        ft = sb("ft", [128, NH])
        it = [0]
        tails = []

        def fe_tile(grp, tt, defer=False):
            is_smp = grp == 16
            L = 32 if is_smp else 128
            src = x_smp[tt * 32:(tt + 1) * 32, :] if is_smp else x_ctx[(grp * 4 + tt) * 128:(grp * 4 + tt + 1) * 128, :]
            gi_ = grp_list.index(grp)
            return front_end(m, src, None, None, L, gmixT, "gmixT", gi_ % 2, tt * L, f"hT{gi_ % 2}_{tt}", defer=defer)

        grp_list = list(range(17)) if dbg_groups is None else list(dbg_groups)
        for tt in range(4):
            if grp_list:
                fe_tile(grp_list[0], tt)

        def chunks_of(grp):
            is_smp_ = grp == 16
            own_ = grp >= 12 and not is_smp_
            ch_ = ["u0", "u1"] + (["q0", "q1"] if (own_ or is_smp_) else []) + ["k0", "k1", "v0", "v1", "f"]
            if dbg_chunks is not None:
                ch_ = [c_ for c_ in dbg_chunks if c_ in ch_]
            return ch_

        flat = [(gi_, ci_, cn_) for gi_, g_ in enumerate(grp_list) for ci_, cn_ in enumerate(chunks_of(g_))]
        loads = [(gi_, ci_, cn_) for (gi_, ci_, cn_) in flat if cn_ != "f"]
        sched_next = {}
        for k_ in range(len(loads) - 1):
            sched_next[(loads[k_][0], loads[k_][1])] = loads[k_ + 1][2]
        wq = []
        fe_pending = [None]
        if loads:
            wq.append(load_wchunk(m, scr_win[WIN_CH[loads[0][2]]], f"scr_win{WIN_CH[loads[0][2]]}"))
        for gi, grp in enumerate(grp_list):
            is_smp = grp == 16
            own = grp >= 12 and not is_smp
            L = 32 if is_smp else 128
            gb = gi % 2
            chunks = ["u0", "u1"] + (["q0", "q1"] if (own or is_smp) else []) + ["k0", "k1", "v0", "v1", "f"]
            if dbg_chunks is not None:
                chunks = [c_ for c_ in dbg_chunks if c_ in chunks]
            for cidx, cn in enumerate(chunks):
                if "E" in phases and gi >= 1:
                    convert_E_weights(1)
                fe_head = None
                if cidx < 4 and gi + 1 < len(grp_list):
                    fe_head, fe_tail_new = fe_tile(grp_list[gi + 1], cidx, defer="3")
                if cn != "f":
                    wi = wq.pop(0)
                nxt = sched_next.get((gi, cidx))
                if nxt is not None:
                    wq.append(load_wchunk(m, scr_win[WIN_CH[nxt]], f"scr_win{WIN_CH[nxt]}"))
                for tt in range(4):
                    T = grp * 4 + tt
                    htok = f"hT{gb}_{tt}"
                    j = it[0] % 2
                    jp = it[0] % 3
                    it[0] += 1
                    if cn == "f":
                        for kt in range(KT):
                            MM(pf4[0:L, tt, :], hT[gb][:, kt, tt * L:(tt + 1) * L], wf_b[:, kt, :], kt == 0, kt == KT - 1, [htok, "wf_b"], ["pf4"])
                        if tt == 3:
                            TT("dve", ft4[0:L], pf4[0:L], bf_bc[0:L].unsqueeze(1).to_broadcast([L, 4, NH]), ALU.add, ["pf4", "bf_bc"], ["ft4"])
                            ACTV(ft4[0:L], ft4[0:L], AF.Exp, ["ft4"], ["ft4"], scale=-1.0)
                            ACTV(ft4[0:L], ft4[0:L], AF.Ln, ["ft4"], ["ft4"], bias=1.0)
                            if is_smp:
                                TS("dve", logf_smp[:, :, :], ft4[0:32], -1.0, None, ALU.mult, None, ["ft4"], ["logf_smp"])
                            else:
                                TS("dve", logf_all[:, grp * 4:grp * 4 + 4, :], ft4[:], -1.0, None, ALU.mult, None, ["ft4"],
                                   [f"logf{grp * 4 + t_}" for t_ in range(4)])
                        continue
                    pm = pmm[jp]
                    ptok = f"pmm{jp}"
                    for kt in range(KT):
                        MM(pm[0:L, :], hT[gb][:, kt, tt * L:(tt + 1) * L], wbuf[wi][:, kt, :], kt == 0, kt == KT - 1,
                           [htok, f"wbuf{wi}"], [ptok])
                    while len(tails) >= 2:
                        tails.pop(0)()
                    half = int(cn[1])
                    hs = slice(half * 512, (half + 1) * 512)
                    if cn[0] == "u":
                        CP("act", u_bf[tt][0:L, hs], pm[0:L, :], [ptok], [f"u_bf{tt}_{half}"])
                        if half == 1 and not (own or is_smp):
                            DMA("sp", u_scr[T], u_bf[tt][:], reads=[f"u_bf{tt}_0", f"u_bf{tt}_1"], writes=["u_scr"])
                        if half == 1 and (own or is_smp):
                            def utail(tt=tt, L=L, gb=gb):
                                for f8 in range(8):
                                    TR(ptr_u[:, f8, 0:L], u_bf[tt][0:L, f8 * 128:(f8 + 1) * 128], ident_b[0:L, 0:L],
                                       [f"u_bf{tt}_0", f"u_bf{tt}_1", "ident_b"], ["ptr_u"])
                                CP("act", uT_grp[gb][:, :, tt * L:(tt + 1) * L], ptr_u[:, :, 0:L], ["ptr_u"], ["uT_grp0"])
                            tails.append(utail)
                    elif cn[0] in "qk":
                        gbc, gtok = (gq_bc, "gq_bc") if cn[0] == "q" else (gk_bc, "gk_bc")
                        ACTV(sqs[j][0:L], pm[0:L, :], AF.Square, [ptok], [f"sqs{j}"])
                        CP("dve", kraw[j][0:L], pm[0:L, :], [ptok], [f"kraw{j}"])
                        RED(ssh[j][0:L], h4(sqs[j][0:L]), [f"sqs{j}"], [f"ssh{j}"])
                        ACTV(ssh[j][0:L], ssh[j][0:L], AF.Sqrt, [f"ssh{j}"], [f"ssh{j}"], bias=EPS, scale=1.0 / DH)
                        RECIP(ssh[j][0:L], ssh[j][0:L], [f"ssh{j}"], [f"ssh{j}"])
                        TT("dve", h4(knf[j][0:L]), h4(kraw[j][0:L]), ssh[j][0:L].unsqueeze(2).to_broadcast([L, 4, DH]), ALU.mult,
                           [f"kraw{j}", f"ssh{j}"], [f"knf{j}"])
                        TT("pool", h4(kout[j][0:L]), h4(knf[j][0:L]), gbc[0:L].unsqueeze(1).to_broadcast([L, 4, DH]), ALU.mult,
                           [f"knf{j}", gtok], [f"kout{j}"])
                        CP("act", kbf[j][0:L], kout[j][0:L], [f"kout{j}"], [f"kbf{j}"])
                        if is_smp:
                            dstT, dtok = (kT_s, "kT_s") if cn[0] == "k" else (qT_s, "qT_s")
                        else:
                            dstT, dtok = (kT_grp[gb], f"kT_grp{gb}") if cn[0] == "k" else (qT_grp[gb], "qT_grp0")

                        def ktail(j=j, L=L, dstT=dstT, dtok=dtok, half=half, tt=tt):
                            for hh in range(4):
                                TR(ptr_k[:, hh, 0:L], kbf[j][0:L, hh * 128:(hh + 1) * 128], ident_b[0:L, 0:L], [f"kbf{j}", "ident_b"], ["ptr_k"])
                            CP("dve", dstT[:, half * 4:(half + 1) * 4, tt * L:(tt + 1) * L], ptr_k[:, :, 0:L], ["ptr_k"], [dtok])
                        tails.append(ktail)
                        if cn[0] == "k" and (own or is_smp):
                            if is_smp:
                                dst = k_smp[tt * 32:(tt + 1) * 32, hs]
                            else:
                                dst = k_own[(T - NPT) * 128:(T - NPT + 1) * 128, hs]
                            DMA("sp", dst, kout[j][0:L], reads=[f"kout{j}"], out_store=True)
                    else:
                        CP("act", vout[j][0:L], pm[0:L, :], [ptok], [f"vout{j}"])
                        if is_smp:
                            CP("pool", v_s[:, half * 4:(half + 1) * 4, tt, :], h4(vout[j][0:32, :]), [f"vout{j}"], ["v_s"])
                        else:
                            CP("pool", v_grp[gb][:, half * 4:(half + 1) * 4, tt, :], h4(vout[j][:, :]), [f"vout{j}"], [f"v_grp{gb}"])
                        if own or is_smp:
                            if is_smp:
                                dst = v_smp[tt * 32:(tt + 1) * 32, hs]
                            else:
                                dst = v_own[(T - NPT) * 128:(T - NPT + 1) * 128, hs]
                            DMA("sp", dst, vout[j][0:L], reads=[f"vout{j}"], out_store=True)
                if fe_pending[0] is not None:
                    fe_pending[0]()
                    fe_pending[0] = None
                if fe_head is not None:
                    fe_head()
                    fe_pending[0] = fe_tail_new
            if fe_pending[0] is not None:
                fe_pending[0]()
                fe_pending[0] = None
            while tails:
                tails.pop(0)()
            if not is_smp:
                DMA("sp", kT_scr[:, :, grp * 512:(grp + 1) * 512].rearrange("h p n -> p h n"), kT_grp[gb][:],
                    reads=[f"kT_grp{gb}"], writes=["kT_scr"])
                DMA("sp", v_scr[:, :, grp * 4:(grp + 1) * 4, :].rearrange("h p t d -> p h (t d)"),
                    v_grp[gb][:].rearrange("p h t d -> p h (t d)"), reads=[f"v_grp{gb}"], writes=["v_scr"])
                if own:
                    DMA("sp", qT_scr[:, :, (grp - 12) * 512:(grp - 11) * 512].rearrange("h p n -> p h n"), qT_grp[gb][:],
                        reads=["qT_grp0"], writes=["qT_scr"])
                    DMA("sp", uT_scr[:, :, (grp - 12) * 512:(grp - 11) * 512].rearrange("f p n -> p f n"), uT_grp[gb][:],
                        reads=["uT_grp0"], writes=["uT_scr"])
            else:
                DMA("sp", uT_scr[:, :, NOWN:NOWN + NSMP].rearrange("f p n -> p f n"), uT_grp[gb][:, :, 0:NSMP],
                    reads=["uT_grp0"], writes=["uT_scr"])
        DMA("sp", lf_own.rearrange("(t p) h -> p t h", p=128), logf_all[:, NPT:NT, :],
            reads=[f"logf{T}" for T in range(NPT, NT)], out_store=True, nonc=True)
        DMA("sp", lf_smp.rearrange("(s p) h -> p s h", p=32), logf_smp[:], reads=["logf_smp"], out_store=True, nonc=True)
        P.flush()
        stk.pop()
        pes.close()


    rstd_ssm = sb("rstd_ssm", [128, 20])
    rstd_att = sb("rstd_att", [128, 20])
    if "C" in phases:
        ces = ExitStack()
        stk.append(ces)
        TWO_PI = float(2.0 * np.pi)
        NC = G * PST
        W1re = sb("W1re", [128, NC])
        W1im = sb("W1im", [128, NC])
        W2re = sb("W2re", [128, 32, 128])
        W2im = sb("W2im", [128, 32, 128])
        WEre = sb("WEre", [128, NC], BF16)
        WEim = sb("WEim", [128, NC], BF16)
        Bblk = sb("Bblk", [128, 8, 1024], BF16)
        Cre_b = sb("Cre_b", [128, 8, 4, 128], BF16)
        Cimn_b = sb("Cimn_b", [128, 8, 4, 128], BF16)
        Bpre = sb("Bpre", [128, 32, 32])
        Bpim = sb("Bpim", [128, 32, 32])
        d_t = sb("d_t", [128, 8])
        gssm_t = sb("gssm_t", [128, 8])
        hpre = [sb("hpre_re", [128, 32]), sb("hpre_im", [128, 32])]
        iota1 = sb("iota1", [128, 128])
        sp1 = sb("sp1", [128, 1])
        nsp1 = sb("nsp1", [128, 1])
        nE = sb("nE", [128, 1])
        DMA("sp", iota1[:], c_iota1, writes=["iota1"])
        DMA("sp", sp1[:], c_sp1, writes=["sp1"])
        DMA("sp", d_t[:], ssm_d.rearrange("(fb q) -> q fb", q=128), writes=["d_t"], nonc=True)
        DMA("sp", gssm_t[:], g_out_ssm.rearrange("(fb q) -> q fb", q=128), writes=["gssm_t"], nonc=True)
        TS("dve", nsp1[:], sp1[:], -1.0, None, ALU.mult, None, ["sp1"], ["nsp1"])
        TS("dve", nE[:], sp1[:], -1.0, 128.0, ALU.mult, ALU.add, ["sp1"], ["nE"])

        ses = ExitStack()
        stk.append(ses)
        QN = 1024
        lam_q = sb("lam_q", [128, QN])
        th_q = sb("th_q", [128, QN])
        are_q = sb("are_q", [128, QN])
        aim_q = sb("aim_q", [128, QN])
        fr_q = sb("fr_q", [128, QN])
        fi_q = sb("fi_q", [128, QN])
        tA = sb("tA", [128, QN])
        tB = sb("tB", [128, QN])
        tC = sb("tC", [128, QN])
        tI = sb("tI", [128, QN], I32)
        raw1 = sb("raw1", [128, QN])
        raw2 = sb("raw2", [128, QN])
        step_bc = sb("step_bc", [128, G])
        lam_st = sb("lam_st", [128, 32])
        th_st = sb("th_st", [128, 32])
        are_st = sb("are_st", [128, 32])
        aim_st = sb("aim_st", [128, 32])
        step_st = sb("step_st", [128, 32])

        def trig2(out_c, ctok, out_s, stok_, z, ztok):
            TS("dve", tB[:], z, 1.0, TWO_PI, ALU.mult, ALU.add, [ztok], ["tB"])
            TS("dve", tI[:], tB[:], 1.0 / TWO_PI, None, ALU.mult, None, ["tB"], ["tI"])
            CP("dve", tC[:], tI[:], ["tI"], ["tC"])
            STT("dve", tB[:], tC[:], -TWO_PI, tB[:], ALU.mult, ALU.add, ["tC", "tB"], ["tB"])
            TS("dve", tC[:], tB[:], float(np.pi), -TWO_PI, ALU.is_gt, ALU.mult, ["tB"], ["tC"])
            TT("dve", tB[:], tB[:], tC[:], ALU.add, ["tB", "tC"], ["tB"])
            ACTV(out_s, tB[:], AF.Sin, ["tB"], [stok_])
            STT("dve", tC[:], tB[:], -1.0, tB[:], ALU.mult, ALU.max, ["tB"], ["tC"])
            ACTV(out_c, tC[:], AF.Sin, ["tC"], [ctok], bias=float(np.pi / 2), scale=-1.0)

        DMA("sp", step_bc[:], log_step.broadcast_to([128, G]), writes=["step_bc"])
        ACTV(step_bc[:], step_bc[:], AF.Exp, ["step_bc"], ["step_bc"])
        DMA("sp", are_st[:], a_re.rearrange("o (j q) -> q (o j)", q=128), writes=["are_st"], nonc=True)
        DMA("sp", aim_st[:], a_im.rearrange("o (j q) -> q (o j)", q=128), writes=["aim_st"], nonc=True)
        ls3 = log_step.rearrange("o (j e) -> o j e", e=2)
        DMA("sp", step_st[0:64, :], ls3[:, :, 0].broadcast_to([64, 32]), writes=["step_st"], nonc=True)
        DMA("sp", step_st[64:128, :], ls3[:, :, 1].broadcast_to([64, 32]), writes=["step_st"], nonc=True)
        ACTV(step_st[:], step_st[:], AF.Exp, ["step_st"], ["step_st"])
        TT("dve", lam_st[:], are_st[:], step_st[:], ALU.mult, ["are_st", "step_st"], ["lam_st"])
        TT("dve", th_st[:], aim_st[:], step_st[:], ALU.mult, ["aim_st", "step_st"], ["th_st"])

        for qd in range(4):
            cq = slice(qd * QN, (qd + 1) * QN)
            gq = slice(qd * 16, (qd + 1) * 16)
            g3 = lambda ap: ap.rearrange("p (g q) -> p g q", g=16)
            stepb3 = step_bc[:, gq].unsqueeze(2).to_broadcast([128, 16, PST])
            DMA("sp", are_q[:], a_re[:, cq].broadcast_to([128, QN]), writes=["are_q"])
            DMA("sp", aim_q[:], a_im[:, cq].broadcast_to([128, QN]), writes=["aim_q"])
            TT("dve", g3(lam_q[:]), g3(are_q[:]), stepb3, ALU.mult, ["are_q", "step_bc"], ["lam_q"])
            TT("dve", g3(th_q[:]), g3(aim_q[:]), stepb3, ALU.mult, ["aim_q", "step_bc"], ["th_q"])
            trig2(fr_q[:], "fr_q", fi_q[:], "fi_q", th_q[:], "th_q")
            ACTV(tA[:], lam_q[:], AF.Exp, ["lam_q"], ["tA"])
            TT("dve", fr_q[:], fr_q[:], tA[:], ALU.mult, ["fr_q", "tA"], ["fr_q"])
            TT("dve", fi_q[:], fi_q[:], tA[:], ALU.mult, ["fi_q", "tA"], ["fi_q"])
            TS("dve", fr_q[:], fr_q[:], -1.0, None, ALU.add, None, ["fr_q"], ["fr_q"])
            TT("dve", tA[:], are_q[:], are_q[:], ALU.mult, ["are_q"], ["tA"])
            TT("dve", tB[:], aim_q[:], aim_q[:], ALU.mult, ["aim_q"], ["tB"])
            TT("dve", tA[:], tA[:], tB[:], ALU.add, ["tA", "tB"], ["tA"])
            RECIP(tA[:], tA[:], ["tA"], ["tA"])
            TT("dve", tB[:], fr_q[:], are_q[:], ALU.mult, ["fr_q", "are_q"], ["tB"])
            TT("dve", tC[:], fi_q[:], aim_q[:], ALU.mult, ["fi_q", "aim_q"], ["tC"])
            TT("dve", tB[:], tB[:], tC[:], ALU.add, ["tB", "tC"], ["tB"])
            TT("dve", tC[:], fi_q[:], are_q[:], ALU.mult, ["fi_q", "are_q"], ["tC"])
            TT("dve", fi_q[:], fr_q[:], aim_q[:], ALU.mult, ["fr_q", "aim_q"], ["fi_q"])
            TT("dve", fi_q[:], tC[:], fi_q[:], ALU.subtract, ["tC", "fi_q"], ["fi_q"])
            TT("dve", fr_q[:], tB[:], tA[:], ALU.mult, ["tB", "tA"], ["fr_q"])
            TT("dve", fi_q[:], fi_q[:], tA[:], ALU.mult, ["fi_q", "tA"], ["fi_q"])
            fbq = slice(2 * qd, 2 * qd + 2)
            v2 = lambda ap: ap.rearrange("p (a b) -> p a b", a=2)
            DMA("sp", v2(raw1[:]), bblk_re[:, fbq, :], writes=["raw1"])
            DMA("sp", v2(raw2[:]), bblk_im[:, fbq, :], writes=["raw2"])
            TT("dve", tA[:], raw1[:], fr_q[:], ALU.mult, ["raw1", "fr_q"], ["tA"])
            TT("dve", tB[:], raw2[:], fi_q[:], ALU.mult, ["raw2", "fi_q"], ["tB"])
            TT("dve", Bblk[:, fbq, 0:512], v2(tA[:]), v2(tB[:]), ALU.subtract, ["tA", "tB"], ["Bblk"])
            TT("dve", tA[:], raw2[:], fr_q[:], ALU.mult, ["raw2", "fr_q"], ["tA"])
            TT("dve", tB[:], raw1[:], fi_q[:], ALU.mult, ["raw1", "fi_q"], ["tB"])
            TT("dve", Bblk[:, fbq, 512:1024], v2(tA[:]), v2(tB[:]), ALU.add, ["tA", "tB"], ["Bblk"])
            DMA("sp", raw1[:], cblk_re[:, fbq].rearrange("p a b c -> p (a b c)"), reads=["Bblk"], writes=["raw1"])
            DMA("sp", raw2[:], cblk_im[:, fbq].rearrange("p a b c -> p (a b c)"), reads=["Bblk"], writes=["raw2"])
            CP("pool", Cre_b[:, fbq].rearrange("p a b c -> p (a b c)"), raw1[:], ["raw1"], ["Cre_b"])
            TS("dve", Cimn_b[:, fbq].rearrange("p a b c -> p (a b c)"), raw2[:], -1.0, None, ALU.mult, None, ["raw2"], ["Cimn_b"])
            TS("dve", tA[:], th_q[:], sp1[:, 0:1], None, ALU.mult, None, ["th_q", "sp1"], ["tA"])
            trig2(W1re[:, cq], "W1re", W1im[:, cq], "W1im", tA[:], "tA")
            ACTV(tA[:], lam_q[:], AF.Exp, ["lam_q", "nsp1"], ["tA"], scale=nsp1[:, 0:1])
            TT("dve", W1re[:, cq], W1re[:, cq], tA[:], ALU.mult, ["W1re", "tA"], ["W1re"])
            STT("dve", W1im[:, cq], W1im[:, cq], -1.0, tA[:], ALU.mult, ALU.mult, ["W1im", "tA"], ["W1im"])
            TS("dve", tA[:], th_q[:], nE[:, 0:1], None, ALU.mult, None, ["th_q", "nE"], ["tA"])
            trig2(fr_q[:], "fr_q", fi_q[:], "fi_q", tA[:], "tA")
            ACTV(tA[:], lam_q[:], AF.Exp, ["lam_q", "nE"], ["tA"], scale=nE[:, 0:1])
            TT("dve", WEre[:, cq], fr_q[:], tA[:], ALU.mult, ["fr_q", "tA"], ["WEre"])
            TT("dve", WEim[:, cq], fi_q[:], tA[:], ALU.mult, ["fi_q", "tA"], ["WEim"])
            jq_ = slice(qd * 8, (qd + 1) * 8)
            j3 = lambda ap: ap.rearrange("p (j t) -> p j t", j=8)
            iob = iota1[:].unsqueeze(1).to_broadcast([128, 8, 128])
            w2r = W2re[:, jq_, :].rearrange("p j t -> p (j t)")
            w2i = W2im[:, jq_, :].rearrange("p j t -> p (j t)")
            TT("dve", j3(tA[:]), iob, th_st[:, jq_].unsqueeze(2).to_broadcast([128, 8, 128]), ALU.mult, ["iota1", "th_st"], ["tA"])
            trig2(w2r, "W2re", w2i, "W2im", tA[:], "tA")
            TT("dve", j3(tA[:]), iob, lam_st[:, jq_].unsqueeze(2).to_broadcast([128, 8, 128]), ALU.mult, ["iota1", "lam_st"], ["tA"])
            ACTV(tA[:], tA[:], AF.Exp, ["tA"], ["tA"])
            TT("dve", w2r, w2r, tA[:], ALU.mult, ["W2re", "tA"], ["W2re"])
            TT("dve", w2i, w2i, tA[:], ALU.mult, ["W2im", "tA"], ["W2im"])
        fs = [sb(f"fs{i}", [128, 32]) for i in range(6)]
        nr_s, ni_s, den_s, t_s, fr_s, fi_s = fs
        TS("dve", nr_s[:], W2re[:, :, 0], -1.0, None, ALU.add, None, ["W2re"], ["nr_s"])
        CP("dve", ni_s[:], W2im[:, :, 0], ["W2im"], ["ni_s"])
        TT("dve", den_s[:], are_st[:], are_st[:], ALU.mult, ["are_st"], ["den_s"])
        TT("dve", t_s[:], aim_st[:], aim_st[:], ALU.mult, ["aim_st"], ["t_s"])
        TT("dve", den_s[:], den_s[:], t_s[:], ALU.add, ["den_s", "t_s"], ["den_s"])
        RECIP(den_s[:], den_s[:], ["den_s"], ["den_s"])
        TT("dve", fr_s[:], nr_s[:], are_st[:], ALU.mult, ["nr_s", "are_st"], ["fr_s"])
        TT("dve", t_s[:], ni_s[:], aim_st[:], ALU.mult, ["ni_s", "aim_st"], ["t_s"])
        TT("dve", fr_s[:], fr_s[:], t_s[:], ALU.add, ["fr_s", "t_s"], ["fr_s"])
        TT("dve", fr_s[:], fr_s[:], den_s[:], ALU.mult, ["fr_s", "den_s"], ["fr_s"])
        TT("dve", fi_s[:], ni_s[:], are_st[:], ALU.mult, ["ni_s", "are_st"], ["fi_s"])
        TT("dve", t_s[:], nr_s[:], aim_st[:], ALU.mult, ["nr_s", "aim_st"], ["t_s"])
        TT("dve", fi_s[:], fi_s[:], t_s[:], ALU.subtract, ["fi_s", "t_s"], ["fi_s"])
        TT("dve", fi_s[:], fi_s[:], den_s[:], ALU.mult, ["fi_s", "den_s"], ["fi_s"])
        braw_re = sb("braw_re", [128, 32, 32])
        braw_im = sb("braw_im", [128, 32, 32])
        bt1 = sb("bt1", [128, 32, 32])
        bt2 = sb("bt2", [128, 32, 32])
        DMA("sp", braw_re[:], bpad_re, writes=["braw_re"])
        DMA("sp", braw_im[:], bpad_im, writes=["braw_im"])
        frb = fr_s[:].unsqueeze(2).to_broadcast([128, 32, 32])
        fib = fi_s[:].unsqueeze(2).to_broadcast([128, 32, 32])
        TT("dve", bt1[:], braw_re[:], frb, ALU.mult, ["braw_re", "fr_s"], ["bt1"])
        TT("dve", bt2[:], braw_im[:], fib, ALU.mult, ["braw_im", "fi_s"], ["bt2"])
        TT("dve", Bpre[:], bt1[:], bt2[:], ALU.subtract, ["bt1", "bt2"], ["Bpre"])
        TT("dve", bt1[:], braw_im[:], frb, ALU.mult, ["braw_im", "fr_s"], ["bt1"])
        TT("dve", bt2[:], braw_re[:], fib, ALU.mult, ["braw_re", "fi_s"], ["bt2"])
        TT("dve", Bpim[:], bt1[:], bt2[:], ALU.add, ["bt1", "bt2"], ["Bpim"])
        MEMSET("dve", hpre[0][:], 0.0, ["hpre0"])
        MEMSET("dve", hpre[1][:], 0.0, ["hpre1"])
        P.flush()
        stk.pop()
        ses.close()

        e0 = ExitStack()
        stk.append(e0)
        pE = [ps(f"pE{i}", [128, 8, 2, 32]) for i in range(2)]
        ut = [sb(f"ut{i}", [128, DSSM], BF16) for i in range(2)]
        em2 = [[sb(f"em{q}_{i}", [128, 8, 32]) for i in range(6)] for q in range(2)]
        Ere = sb("Ere", [128, 32])
        Eim = sb("Eim", [128, 32])
        hs = [sb(f"hs{i}", [128, 32]) for i in range(4)]
        A128re = W2re[:, :, 127]
        A128im = W2im[:, :, 127]
        n_pre = NPT if dbg_groups is None else 0
        ema = [sb(f"ema{q_}", [128, 8, 2, 32]) for q_ in range(2)]
        emb = [sb(f"emb{q_}", [128, 8, 2, 32]) for q_ in range(2)]
        Ere2 = [Ere, sb("Ere_b2", [128, 32])]
        Eim2 = [Eim, sb("Eim_b2", [128, 32])]
        NR = n_pre * 4

        def e1(r):
            T, jq = r // 4, r % 4
            ub = T % 2
            if jq == 0:
                DMA("sp", ut[ub][:], u_scr[T], writes=[f"ut{ub}"])
            pe_ = pE[r % 2]
            ptk = f"pE{r % 2}"
            for jl in range(8):
                j = jq * 8 + jl
                MM(pe_[:, jl, 0, :], WEre[:, j * 128:(j + 1) * 128], ut[ub][:, j * 32:(j + 1) * 32], True, True, ["WEre", f"ut{ub}"], [ptk])
                MM(pe_[:, jl, 1, :], WEim[:, j * 128:(j + 1) * 128], ut[ub][:, j * 32:(j + 1) * 32], True, True, ["WEim", f"ut{ub}"], [ptk])

        def e2(r):
            T, jq = r // 4, r % 4
            pe_ = pE[r % 2]
            ptk = f"pE{r % 2}"
            js = slice(jq * 8, (jq + 1) * 8)
            em = em2[r % 2]
            eq = r % 2
            TT("dve", ema[eq][:], pe_[:, :, :, :], Bpre[:, js, :].unsqueeze(2).to_broadcast([128, 8, 2, 32]), ALU.mult, [ptk, "Bpre"], [f"ema{eq}"])
            TT("dve", emb[eq][:], pe_[:, :, :, :], Bpim[:, js, :].unsqueeze(2).to_broadcast([128, 8, 2, 32]), ALU.mult, [ptk, "Bpim"], [f"emb{eq}"])

        def e3(r):
            em = em2[r % 2]
            eq = r % 2
            TT("pool", em[4][:], ema[eq][:, :, 0, :], emb[eq][:, :, 1, :], ALU.subtract, [f"ema{eq}", f"emb{eq}"], [f"em{eq}4"])
            TT("pool", em[5][:], emb[eq][:, :, 0, :], ema[eq][:, :, 1, :], ALU.add, [f"ema{eq}", f"emb{eq}"], [f"em{eq}5"])

        def e4(r):
            T, jq = r // 4, r % 4
            js = slice(jq * 8, (jq + 1) * 8)
            em = em2[r % 2]
            eq = r % 2
            tp_ = T % 2
            RED(Ere2[tp_][:, js], em[4][:], [f"em{eq}4"], [f"Ere{tp_}_{jq}"])
            RED(Eim2[tp_][:, js], em[5][:], [f"em{eq}5"], [f"Eim{tp_}_{jq}"])
            if jq != 3:
                return
            TT("dve", hs[0][:], A128re, hpre[0][:], ALU.mult, ["hpre0"], ["hs0"])
            TT("dve", hs[1][:], A128im, hpre[1][:], ALU.mult, ["hpre1"], ["hs1"])
            TT("dve", hs[2][:], A128re, hpre[1][:], ALU.mult, ["hpre1"], ["hs2"])
            TT("dve", hs[3][:], A128im, hpre[0][:], ALU.mult, ["hpre0"], ["hs3"])
            TT("dve", hs[0][:], hs[0][:], hs[1][:], ALU.subtract, ["hs0", "hs1"], ["hs0"])
            TT("dve", hs[2][:], hs[2][:], hs[3][:], ALU.add, ["hs2", "hs3"], ["hs2"])
            TT("dve", hpre[0][:], hs[0][:], Ere2[tp_][:], ALU.add, ["hs0"] + [f"Ere{tp_}_{q_}" for q_ in range(4)], ["hpre0"])
            TT("dve", hpre[1][:], hs[2][:], Eim2[tp_][:], ALU.add, ["hs2"] + [f"Eim{tp_}_{q_}" for q_ in range(4)], ["hpre1"])

        for step in range(NR + 3):
            if step < NR:
                e1(step)
            if 0 <= step - 1 < NR:
                e2(step - 1)
            if 0 <= step - 2 < NR:
                e3(step - 2)
            if 0 <= step - 3 < NR:
                e4(step - 3)
        P.flush()
        stk.pop()
        e0.close()

        c1 = ExitStack()
        stk.append(c1)
        wglu_b = sb("wglu_b", [128, 8, DSSM], BF16)
        for kt in range(8):
            DMA("pool", wglu_b[:, kt, :], w_glu[kt * 128:(kt + 1) * 128, :], writes=["wglu_b"])
        pbu = [ps(f"pbu{i}", [128, 512]) for i in range(2)]
        pST = [ps(f"pST{i}", [128, 4, 128]) for i in range(2)]
        py = ps("py", [128, 128])
        pglu = ps("pglu", [128, 4, 128])
        prs = ps("prs", [128, 1])
        uT_t = [sb(f"uT_t{i}", [128, 8, 128], BF16) for i in range(2)]
        tt_ = [[sb(f"ct{q}_{i}", [128, 256]) for i in range(4)] for q in range(2)]
        Xb = [sb(f"Xb{q}", [128, 512], BF16) for q in range(2)]
        Ab = [sb(f"Ab{q}", [128, 4, 128]) for q in range(2)]
        pp = [[sb(f"pp{q}_{i}", [128, 2, 128]) for i in range(4)] for q in range(2)]
        Hb = [sb(f"Hb{q}", [128, 4, 128], BF16) for q in range(2)]
        yT = sb("yT", [128, 8, 128])
        g1 = sb("g1", [128, 8, 128])
        gyT = sb("gyT", [128, 8, 128])
        gyT_b = sb("gyT_b", [128, 8, 128], BF16)
        so = sb("so", [128, 4, 128])
        sq = sb("sq", [128, 8, 128], BF16)
        mixT_t = sb("mixT_t", [128, 8, 128], BF16)
        hc = [[sb(f"hc{a}{b}", [128, 32]) for b in range(2)] for a in range(2)]
        rs_t = sb("rs_t", [128, 1])

        if dbg_groups is None:
            tiles = [("own", n) for n in range(16)] + [("smp", s_) for s_ in range(4)]
        else:
            tiles = [("own", n) for n in range(1)] + [("smp", s_) for s_ in range(1)]
        yT2 = [yT, sb("yT_b2", [128, 8, 128])]
        units = []
        for ti, (kind, n) in enumerate(tiles):
            for hfb in range(16):
                units.append((ti, kind, n, hfb))

        def uinfo(u):
            ti, kind, n, hfb = units[u]
            L = 128 if kind == "own" else 32
            c0 = n * 128 if kind == "own" else NOWN + 32 * n
            tcol = n if kind == "own" else 16 + n
            return ti, kind, n, hfb, L, c0, tcol, ti % 2, ti % 2, hc[ti % 2], hc[1 - ti % 2], hfb // 2, hfb % 2, u % 2

        def stage1(u):
            ti, kind, n, hfb, L, c0, tcol, ub, pi, hin, hout, fb, hh, q = uinfo(u)
            if hfb == 0:
                DMA("sp", uT_t[ub][:, :, 0:L], uT_scr[:, :, c0:c0 + L].rearrange("f p n -> p f n"), writes=[f"uT_t{ub}"], nonc=(L < 128))
            cre = slice(fb * 512 + hh * 256, fb * 512 + hh * 256 + 256)
            MM(pbu[q][0:L, 0:256], uT_t[ub][:, fb, 0:L], Bblk[:, fb, hh * 256:hh * 256 + 256], True, True, [f"uT_t{ub}", "Bblk"], [f"pbu{q}"])
            MM(pbu[q][0:L, 256:512], uT_t[ub][:, fb, 0:L], Bblk[:, fb, 512 + hh * 256:512 + hh * 256 + 256], True, True,
               [f"uT_t{ub}", "Bblk"], [f"pbu{q}"])
            t_ = tt_[q]
            TT("dve", t_[0][0:L], pbu[q][0:L, 0:256], W1re[0:L, cre], ALU.mult, [f"pbu{q}", "W1re"], [f"ct{q}0"])
            TT("dve", t_[1][0:L], pbu[q][0:L, 256:512], W1im[0:L, cre], ALU.mult, [f"pbu{q}", "W1im"], [f"ct{q}1"])
            TT("dve", t_[2][0:L], pbu[q][0:L, 0:256], W1im[0:L, cre], ALU.mult, [f"pbu{q}", "W1im"], [f"ct{q}2"])
            TT("dve", t_[3][0:L], pbu[q][0:L, 256:512], W1re[0:L, cre], ALU.mult, [f"pbu{q}", "W1re"], [f"ct{q}3"])
            TT("pool", Xb[q][0:L, 0:256], t_[0][0:L], t_[1][0:L], ALU.subtract, [f"ct{q}0", f"ct{q}1"], [f"Xre{q}"])
            TT("pool", Xb[q][0:L, 256:512], t_[2][0:L], t_[3][0:L], ALU.add, [f"ct{q}2", f"ct{q}3"], [f"Xim{q}"])

        def stage2(u):
            ti, kind, n, hfb, L, c0, tcol, ub, pi, hin, hout, fb, hh, q = uinfo(u)
            if hfb == 0:
                if kind == "own" and n == 0:
                    CP("dve", hin[0][:], hpre[0][:], ["hpre0"], [f"hc{pi}0_{f_}" for f_ in range(8)])
                    CP("dve", hin[1][:], hpre[1][:], ["hpre1"], [f"hc{pi}1_{f_}" for f_ in range(8)])
                if kind == "smp":
                    DMA("sp", hin[0][:], sre[n].rearrange("(j q) -> q j", q=128), writes=[f"hc{pi}0_{f_}" for f_ in range(8)], nonc=True)
                    DMA("sp", hin[1][:], sim[n].rearrange("(j q) -> q j", q=128), writes=[f"hc{pi}1_{f_}" for f_ in range(8)], nonc=True)
            for k2 in range(2):
                MM(pST[q][:, k2, 0:L], Xb[q][0:L, k2 * 128:(k2 + 1) * 128], triT_b[0:L, 0:L], True, True, [f"Xre{q}", "triT_b"], [f"pST{q}"])
            for k2 in range(2):
                MM(pST[q][:, 2 + k2, 0:L], Xb[q][0:L, 256 + k2 * 128:256 + (k2 + 1) * 128], triT_b[0:L, 0:L], True, True,
                   [f"Xim{q}", "triT_b"], [f"pST{q}"])
            j0 = fb * 4 + hh * 2
            for k2 in range(2):
                ACTV(Ab[q][:, k2, 0:L], pST[q][:, k2, 0:L], AF.Identity, [f"pST{q}", f"hc{pi}0_{fb}"], [f"Are{q}"], bias=hin[0][:, j0 + k2:j0 + k2 + 1])
            for k2 in range(2):
                ACTV(Ab[q][:, 2 + k2, 0:L], pST[q][:, 2 + k2, 0:L], AF.Identity, [f"pST{q}", f"hc{pi}1_{fb}"], [f"Aim{q}"],
                     bias=hin[1][:, j0 + k2:j0 + k2 + 1])
            js = slice(j0, j0 + 2)
            p_ = pp[q]
            TT("dve", p_[0][:, :, 0:L], Ab[q][:, 0:2, 0:L], W2re[:, js, 0:L], ALU.mult, [f"Are{q}", "W2re"], [f"pp{q}0"])
            TT("dve", p_[1][:, :, 0:L], Ab[q][:, 2:4, 0:L], W2im[:, js, 0:L], ALU.mult, [f"Aim{q}", "W2im"], [f"pp{q}1"])
            TT("dve", p_[2][:, :, 0:L], Ab[q][:, 0:2, 0:L], W2im[:, js, 0:L], ALU.mult, [f"Are{q}", "W2im"], [f"pp{q}2"])
            TT("dve", p_[3][:, :, 0:L], Ab[q][:, 2:4, 0:L], W2re[:, js, 0:L], ALU.mult, [f"Aim{q}", "W2re"], [f"pp{q}3"])
            TT("pool", Hb[q][:, 0:2, 0:L], p_[0][:, :, 0:L], p_[1][:, :, 0:L], ALU.subtract, [f"pp{q}0", f"pp{q}1"], [f"Hre{q}"])
            TT("pool", Hb[q][:, 2:4, 0:L], p_[2][:, :, 0:L], p_[3][:, :, 0:L], ALU.add, [f"pp{q}2", f"pp{q}3"], [f"Him{q}"])
            TT("pool", hout[0][:, js], p_[0][:, :, L - 1], p_[1][:, :, L - 1], ALU.subtract, [f"pp{q}0", f"pp{q}1"], [f"hc{1 - pi}0_{fb}"])
            TT("pool", hout[1][:, js], p_[2][:, :, L - 1], p_[3][:, :, L - 1], ALU.add, [f"pp{q}2", f"pp{q}3"], [f"hc{1 - pi}1_{fb}"])
            if hfb == 15:
                hot = [f"hc{1 - pi}0_{f_}" for f_ in range(8)]
                hot1 = [f"hc{1 - pi}1_{f_}" for f_ in range(8)]
                if kind == "own" and n == 15:
                    DMA("sp", hre_own.rearrange("(j q) -> q j", q=128), hout[0][:], reads=hot, out_store=True, nonc=True)
                    DMA("sp", him_own.rearrange("(j q) -> q j", q=128), hout[1][:], reads=hot1, out_store=True, nonc=True)
                if kind == "smp":
                    DMA("sp", hre_smp[n].rearrange("(j q) -> q j", q=128), hout[0][:], reads=hot, out_store=True, nonc=True)
                    DMA("sp", him_smp[n].rearrange("(j q) -> q j", q=128), hout[1][:], reads=hot1, out_store=True, nonc=True)

        def stage3(u):
            ti, kind, n, hfb, L, c0, tcol, ub, pi, hin, hout, fb, hh, q = uinfo(u)
            yT_ = yT2[ti % 2]
            yp = ti % 2
            for k2 in range(2):
                MM(py[:, 0:L], Cre_b[:, fb, hh * 2 + k2, :], Hb[q][:, k2, 0:L], hh == 0 and k2 == 0, False, ["Cre_b", f"Hre{q}"], ["py"])
            for k2 in range(2):
                MM(py[:, 0:L], Cimn_b[:, fb, hh * 2 + k2, :], Hb[q][:, 2 + k2, 0:L], False, hh == 1 and k2 == 1, ["Cimn_b", f"Him{q}"], ["py"])
            if hh == 1:
                STT("dve", yT_[:, fb, 0:L], uT_t[ub][:, fb, 0:L], d_t[:, fb:fb + 1], py[:, 0:L], ALU.mult, ALU.add,
                    [f"uT_t{ub}", "d_t", "py"], [f"yT{yp}_{fb}"])
            if hfb != 15:
                return
            ytoks = [f"yT{yp}_{f_}" for f_ in range(8)]
            TT("pool", g1[:, :, 0:L], yT_[:, :, 0:L], yT_[:, :, 0:L], ALU.mult, ytoks, ["g1"])
            TS("dve", g1[:, :, 0:L], g1[:, :, 0:L], 0.044715, 1.0, ALU.mult, ALU.add, ["g1"], ["g1"])
            TT("pool", g1[:, :, 0:L], g1[:, :, 0:L], yT_[:, :, 0:L], ALU.mult, ["g1"] + ytoks, ["g1"])
            ACTV(g1[:, :, 0:L], g1[:, :, 0:L], AF.Sigmoid, ["g1"], ["g1"], scale=1.5957691216057308)
            TT("dve", gyT[:, :, 0:L], yT_[:, :, 0:L], g1[:, :, 0:L], ALU.mult, ytoks + ["g1"], ["gyT"])
            CP("pool", gyT_b[:, :, 0:L], gyT[:, :, 0:L], ["gyT"], ["gyT_b"])
            for half in range(2):
                for fo4 in range(4):
                    fo = half * 4 + fo4
                    for fi_ in range(8):
                        MM(pglu[:, fo4, 0:L], wglu_b[:, fi_, fo * 128:(fo + 1) * 128], gyT_b[:, fi_, 0:L], fi_ == 0, fi_ == 7,
                           ["wglu_b", "gyT_b"], ["pglu"])
                fs_ = slice(half * 4, half * 4 + 4)
                ACTV(so[:, :, 0:L], pglu[:, :, 0:L], AF.Sigmoid, ["pglu"], ["so"])
                TT("dve", so[:, :, 0:L], so[:, :, 0:L], gyT[:, fs_, 0:L], ALU.mult, ["so", "gyT"], ["so"])
                sqc = 0 if kind == "own" else 32 * n
                ACTV(sq[:, fs_, sqc:sqc + L], so[:, :, 0:L], AF.Square, ["so"], [f"sq{half}"])
                TT("pool", mixT_t[:, fs_, 0:L], so[:, :, 0:L], gssm_t[:, fs_].unsqueeze(2).to_broadcast([128, 4, L]), ALU.mult,
                   ["so", "gssm_t"], [f"mixT_t{half}"])
            if kind == "own" or n == 3 or dbg_groups is not None:
                LL = 128 if (kind == "own" or dbg_groups is None) else 32
                rc = tcol if kind == "own" else 16
                for fo in range(8):
                    MM(prs[0:LL, :], sq[:, fo, 0:LL], ones_b[:, 0:1], fo == 0, fo == 7, ["sq0", "sq1", "ones_b"], ["prs"])
                ACTV(rs_t[0:LL], prs[0:LL, :], AF.Sqrt, ["prs"], ["rs_t"], bias=EPS, scale=1.0 / DSSM)
                RECIP(rstd_ssm[0:LL, rc:rc + 1], rs_t[0:LL], ["rs_t"], [f"rstd_ssm{rc}"])
            DMA("sp", mixT_scr[0:8, :, c0:c0 + L].rearrange("f p n -> p f n"), mixT_t[:, :, 0:L], reads=["mixT_t0", "mixT_t1"],
                writes=["mixT_scr_s"], nonc=(L < 128))

        NU = len(units)
        for step in range(NU + 2):
            if step < NU:
                stage1(step)
            if 0 <= step - 1 < NU:
                stage2(step - 1)
            if 0 <= step - 2 < NU:
                stage3(step - 2)
        P.flush()
        stk.pop()
        c1.close()
        stk.pop()
        ces.close()


    SCALE = float(DH ** -0.5)
    rss_att = sb("rss_att", [128, 20])
    if "D" in phases:
        dd = ExitStack()
        stk.append(dd)
        gatt_t = sb("gatt_t", [128, 8])
        DMA("sp", gatt_t[:], g_out_attn.rearrange("(fb q) -> q fb", q=128), writes=["gatt_t"], nonc=True)
        MEMSET("dve", rss_att[:], 0.0, ["rss_att"])
        biasT = sb("biasT", [128, 4, NH, NT])
        bias_s = sb("bias_s", [128, 4, NH, 16])
        bias_n = sb("bias_n", [32, 4, NH])
        d0 = ExitStack()
        stk.append(d0)
        c_all = sb("c_all", [128, NT, NH])
        tot = sb("tot", [128, NT, NH])
        carry = sb("carry", [128, NT, NH])
        cref = sb("cref", [128, 4, NH])
        pc = ps("pc", [128, 512])
        pt = ps("pt", [128, 512])
        lf2 = logf_all[:].rearrange("p t h -> p (t h)")
        MM(pc[:], triT_f[:], lf2, True, True, ["triT_f"], ["pc"])
        MM(pt[:], ones_f[:], lf2, True, True, ["ones_f"], ["pt"])
        CP("dve", tot[:].rearrange("p t h -> p (t h)"), pt[:], ["pt"], ["tot"])
        MEMSET("dve", carry[:, 0, :], 0.0, ["carry"])
        for T in range(1, NT):
            TT("dve", carry[:, T, :], carry[:, T - 1, :], tot[:, T - 1, :], ALU.add, ["carry", "tot"], ["carry"])
        TT("dve", c_all[:].rearrange("p t h -> p (t h)"), pc[:], carry[:].rearrange("p t h -> p (t h)"), ALU.add, ["pc", "carry"], ["c_all"])
        for qg in range(4):
            T = NPT + 4 * qg + 3
            TT("dve", cref[:, qg, :], carry[:, T, :], tot[:, T, :], ALU.add, ["carry", "tot"], ["cref"])
        for qg in range(4):
            for h in range(NH):
                TS("dve", biasT[:, qg, h, :], c_all[:, :, h], cref[:, qg, h:h + 1], -1.0, ALU.subtract, ALU.mult, ["c_all", "cref"], ["biasT"])
                TT("dve", biasT[:, qg, h, :], biasT[:, qg, h, :], kmask[:], ALU.add, ["biasT", "kmask"], ["biasT"])
        clf_sb = sb("clf_sb", [128, 4, 16, NH])
        cw_s = sb("cw_s", [128, 4, 16, NH])
        tot_s = sb("tot_s", [128, 4, 16, NH])
        car_s = sb("car_s", [128, 4, 16, NH])
        totn = sb("totn", [128, 4, NH])
        total = sb("total", [128, 4, NH])
        for s_ in range(4):
            DMA("sp", clf_sb[:, s_], clf[s_].rearrange("(t p) h -> p t h", p=128), writes=["clf_sb"], nonc=True)
        cl2 = clf_sb[:].rearrange("p s t h -> p (s t h)")
        MM(pc[:], triT_f[:], cl2, True, True, ["triT_f", "clf_sb"], ["pc"])
        MM(pt[:], ones_f[:], cl2, True, True, ["ones_f", "clf_sb"], ["pt"])
        CP("dve", cw_s[:].rearrange("p s t h -> p (s t h)"), pc[:], ["pc"], ["cw_s"])
        CP("dve", tot_s[:].rearrange("p s t h -> p (s t h)"), pt[:], ["pt"], ["tot_s"])
        MEMSET("dve", car_s[:, :, 0, :], 0.0, ["car_s"])
        for T in range(1, 16):
            TT("dve", car_s[:, :, T, :], car_s[:, :, T - 1, :], tot_s[:, :, T - 1, :], ALU.add, ["car_s", "tot_s"], ["car_s"])
        ln2 = logf_smp[:].rearrange("p s h -> p (s h)")
        MM(pc[0:32, 0:32], triT_f[0:32, 0:32], ln2, True, True, ["triT_f", "logf_smp"], ["pc"])
        MM(pt[:, 0:32], ones_f[0:32, :], ln2, True, True, ["ones_f", "logf_smp"], ["pt"])
        CP("dve", totn[:].rearrange("p s h -> p (s h)"), pt[:, 0:32], ["pt"], ["totn"])
        TT("dve", total[:], car_s[:, :, 15, :], tot_s[:, :, 15, :], ALU.add, ["car_s", "tot_s"], ["total"])
        TT("dve", total[:], total[:], totn[:], ALU.add, ["total", "totn"], ["total"])
        TT("dve", cw_s[:], cw_s[:], car_s[:], ALU.add, ["cw_s", "car_s"], ["cw_s"])
        for T in range(16):
            TT("dve", bias_s[:, :, :, T], total[:], cw_s[:, :, T, :], ALU.subtract, ["cw_s", "total"], ["bias_s"])
        TT("dve", bias_n[:].rearrange("p s h -> p (s h)"), totn[0:32].rearrange("p s h -> p (s h)"), pc[0:32, 0:32], ALU.subtract,
           ["pc", "totn"], ["bias_n"])
        P.flush()
        stk.pop()
        d0.close()

        d1 = ExitStack()
        stk.append(d1)
        KT_h = [sb(f"KT_h{i}", [128, NCTX], BF16) for i in range(2)]
        V_h = [sb(f"V_h{i}", [128, NT, DH], BF16) for i in range(2)]
        QT_h = [sb(f"QT_h{i}", [128, NOWN], BF16) for i in range(2)]
        Pt = [sb(f"Pt{i}", [128, 512], BF16) for i in range(3)]
        pS = [ps(f"pS{i}", [128, 512]) for i in range(3)]
        pO = [ps(f"pO{i}", [128, 512]) for i in range(2)]
        pD = [ps(f"pD{i}", [128, 512]) for i in range(2)]
        prs2 = ps("prs2", [128, 4])
        rden = [sb(f"rden{i}", [128, 512]) for i in range(2)]
        at = [sb(f"at{i}", [128, 512]) for i in range(2)]
        sqa = [sb(f"sqa{i}", [128, 512], BF16) for i in range(2)]
        mixA = [sb(f"mixA{i}", [128, 512], BF16) for i in range(2)]
        heads = range(NH) if dbg_groups is None else range(1)
        qgs = range(4) if dbg_groups is None else range(1)
        sc = 0
        for hi, h in enumerate(heads):
            hb = hi % 2
            DMA("sp", KT_h[hb][:], kT_scr[h], writes=[f"KT_h{hb}"])
            DMA("sp", V_h[hb][:].rearrange("p t d -> p (t d)"), v_scr[h].rearrange("p t d -> p (t d)"), writes=[f"V_h{hb}"])
            DMA("sp", QT_h[hb][:], qT_scr[h], writes=[f"QT_h{hb}"])
            for qg in qgs:
                q0 = qg * 512
                ob = (hi * 4 + qg) % 2
                tl = [(T, 0) for T in range(NPT + 4 * qg)] + [(NPT + 4 * qg + m_, 128 * m_) for m_ in range(4)]
                n_t = len(tl)

                def emit_S(i):
                    T, off = tl[i]
                    b_ = (sc + i) % 3
                    MM(pS[b_][:, off:512], KT_h[hb][:, T * 128:(T + 1) * 128], QT_h[hb][:, q0 + off:q0 + 512], True, True,
                       [f"KT_h{hb}", f"QT_h{hb}"], [f"pS{b_}"])

                emit_S(0)
                emit_S(1)
                for i in range(n_t):
                    T, off = tl[i]
                    b_ = (sc + i) % 3
                    ACTV(Pt[b_][:, off:512], pS[b_][:, off:512], AF.Exp, [f"pS{b_}", "biasT"], [f"Pt{b_}"],
                         bias=biasT[:, qg, h, T:T + 1], scale=SCALE)
                    if T >= NPT + 4 * qg:
                        TT("pool", Pt[b_][:, off:off + 128], Pt[b_][:, off:off + 128], triT_b[:], ALU.mult, [f"Pt{b_}", "triT_b"], [f"Pt{b_}"])
                    if i + 2 < n_t:
                        emit_S(i + 2)
                    MM(pO[ob][:, off:512], V_h[hb][:, T, :], Pt[b_][:, off:512], i == 0, i == n_t - 1, [f"V_h{hb}", f"Pt{b_}"], [f"pO{ob}"])
                    MM(pD[ob][:, off:512], ones_b[:], Pt[b_][:, off:512], i == 0, i == n_t - 1, ["ones_b", f"Pt{b_}"], [f"pD{ob}"])
                sc += n_t
                RECIP(rden[ob][:], pD[ob][:], [f"pD{ob}"], [f"rden{ob}"])
                TT("dve", at[ob][:], pO[ob][:], rden[ob][:], ALU.mult, [f"pO{ob}", f"rden{ob}"], [f"at{ob}"])
                ACTV(sqa[ob][:], at[ob][:], AF.Square, [f"at{ob}"], [f"sqa{ob}"])
                TS("dve", mixA[ob][:], at[ob][:], gatt_t[:, h:h + 1], None, ALU.mult, None, [f"at{ob}", "gatt_t"], [f"mixA{ob}"])
                DMA("sp", mixT_scr[8 + h, :, q0:q0 + 512], mixA[ob][:], reads=[f"mixA{ob}"], writes=["mixT_scr_a"])
                for t4 in range(4):
                    MM(prs2[:, t4:t4 + 1], sqa[ob][:, t4 * 128:(t4 + 1) * 128], ones_b[:, 0:1], True, True, [f"sqa{ob}", "ones_b"], ["prs2"])
                TT("dve", rss_att[:, qg * 4:qg * 4 + 4], rss_att[:, qg * 4:qg * 4 + 4], prs2[:, 0:4], ALU.add, ["rss_att", "prs2"], ["rss_att"])
        P.flush()
        stk.pop()
        d1.close()

        d2 = ExitStack()
        stk.append(d2)
        kc_f = [sb(f"kc_f{i}", [128, 16, DH]) for i in range(2)]
        vc_f = [sb(f"vc_f{i}", [128, 16, DH]) for i in range(2)]
        kc_b = sb("kc_b", [128, 16, DH], BF16)
        vc_b = [sb(f"vc_b{i}", [128, 16, DH], BF16) for i in range(2)]
        KTc = [sb(f"KTc{i}", [128, 16, 128], BF16) for i in range(2)]
        ptK = [ps(f"ptK{i}", [128, 8, 128], BF16) for i in range(2)]
        pSs = ps("pSs", [128, 16, 32])
        pSn = ps("pSn", [32, 32])
        pOs = ps("pOs", [128, 32])
        pDs = ps("pDs", [128, 32])
        tmpS = sb("tmpS", [128, 16, 32])
        Pts = sb("Pts", [128, 16, 32], BF16)
        Ptn = sb("Ptn", [32, 32], BF16)
        rdn = sb("rdn", [128, 32])
        ats = sb("ats", [128, 32])
        sqs_all = sb("sqs_all", [128, NH, NSMP], BF16)
        prs4 = ps("prs4", [128, 1])
        mixS = sb("mixS", [128, NH, NSMP], BF16)
        ck4 = ck.rearrange("s (t p) h d -> s p t h d", p=128)
        cv4 = cv.rearrange("s (t p) h d -> s p t h d", p=128)
        sh_list = [(s_, h) for s_ in range(4) for h in range(NH)] if dbg_groups is None else [(0, 0)]
        def smp_A(ii):
            s_, h = sh_list[ii]
            b2 = ii % 2
            DMA("sp", kc_f[b2][:], ck4[s_, :, :, h, :], writes=[f"kc_f{b2}"])
            DMA("sp", vc_f[b2][:], cv4[s_, :, :, h, :], writes=[f"vc_f{b2}"])
            CP("dve", kc_b[:], kc_f[b2][:], [f"kc_f{b2}"], ["kc_b"])
            CP("act", vc_b[b2][:], vc_f[b2][:], [f"vc_f{b2}"], [f"vc_b{b2}"])
            for T in range(16):
                TR(ptK[T // 8][:, T % 8, :], kc_b[:, T, :], ident_b[:], ["kc_b", "ident_b"], [f"ptK{T // 8}"])
            CP("act", KTc[b2][:, 0:8, :], ptK[0][:], ["ptK0"], [f"KTc{b2}"])
            CP("dve", KTc[b2][:, 8:16, :], ptK[1][:], ["ptK1"], [f"KTc{b2}"])

        smp_A(0)
        for ii, (s_, h) in enumerate(sh_list):
            b2 = ii % 2
            if ii + 1 < len(sh_list):
                smp_A(ii + 1)
            qs = slice(32 * s_, 32 * s_ + 32)
            for T in range(16):
                MM(pSs[:, T, :], KTc[b2][:, T, :], qT_s[:, h, qs], True, True, [f"KTc{b2}", "qT_s"], ["pSs"])
            MM(pSn[:, :], kT_s[:, h, qs], qT_s[:, h, qs], True, True, ["kT_s", "qT_s"], ["pSn"])
            STT("dve", tmpS[:], pSs[:], SCALE, bias_s[:, s_, h, :].unsqueeze(2).to_broadcast([128, 16, 32]), ALU.mult, ALU.add,
                ["pSs", "bias_s"], ["tmpS"])
            ACTV(Pts[:], tmpS[:], AF.Exp, ["tmpS"], ["Pts"])
            ACTV(Ptn[:], pSn[:], AF.Exp, ["pSn", "bias_n"], ["Ptn"], bias=bias_n[:, s_, h:h + 1], scale=SCALE)
            TT("pool", Ptn[:], Ptn[:], triT_b[0:32, 0:32], ALU.mult, ["Ptn", "triT_b"], ["Ptn"])
            for T in range(16):
                MM(pOs[:], vc_b[b2][:, T, :], Pts[:, T, :], T == 0, False, [f"vc_b{b2}", "Pts"], ["pOs"])
            MM(pOs[:], v_s[:, h, s_, :], Ptn[:], False, True, ["v_s", "Ptn"], ["pOs"])
            for T in range(16):
                MM(pDs[:], ones_b[:], Pts[:, T, :], T == 0, False, ["ones_b", "Pts"], ["pDs"])
            MM(pDs[:], ones_b[0:32, :], Ptn[:], False, True, ["ones_b", "Ptn"], ["pDs"])
            RECIP(rdn[:], pDs[:], ["pDs"], ["rdn"])
            TT("dve", ats[:], pOs[:], rdn[:], ALU.mult, ["pOs", "rdn"], ["ats"])
            ACTV(sqs_all[:, h, qs], ats[:], AF.Square, ["ats"], ["sqs_all"])
            TS("dve", mixS[:, h, qs], ats[:], gatt_t[:, h:h + 1], None, ALU.mult, None, ["ats", "gatt_t"], ["mixS"])
        DMA("sp", mixT_scr[8:16, :, NOWN:NOWN + NSMP].rearrange("f p n -> p f n"), mixS[:], reads=["mixS"], writes=["mixT_scr_a"])
        if dbg_groups is None:
            for h in range(NH):
                MM(prs4[:, :], sqs_all[:, h, :], ones_b[:, 0:1], h == 0, h == NH - 1, ["sqs_all", "ones_b"], ["prs4"])
            CP("dve", rss_att[:, 16:17], prs4[:, :], ["prs4"], ["rss_att"])
        ACTV(rstd_att[:], rss_att[:], AF.Sqrt, ["rss_att"], ["rstd_att"], bias=EPS, scale=1.0 / DSSM)
        RECIP(rstd_att[:], rstd_att[:], ["rstd_att"], ["rstd_att"])
        P.flush()
        stk.pop()
        d2.close()
        stk.pop()
        dd.close()


    if "E" in phases:
        if conv_jobs:
            convert_E_weights()
            P.flush()
        ee = ExitStack()
        stk.append(ee)
        m = linear_machinery(nw=2, nh=1, with_xt=False)
        hT, wbuf, pmm = m.hT, m.wbuf, m.pmm
        pq = [ps(f"pq{i}", [128, 512]) for i in range(4)]
        x1 = sb("x1", [128, 4, D])
        aT = sb("aT", [128, 64, 512], BF16)
        mixT = sb("mixT", [128, 16, 512], BF16)
        rr = [sb(f"rr{i}", [128, 512]) for i in range(2)]
        yb = [sb(f"yb{i}", [128, 512]) for i in range(2)]
        if dbg_groups is None:
            egroups = [("own", g_) for g_ in range(4)] + [("smp", 0)]
        else:
            egroups = [("own", 0), ("smp", 0)]
        cnt = 0
        eloads = []
        for _ in egroups:
            eloads += [(scr_wout[c], f"scr_wout{c}") for c in range(4)]
            eloads += [(scr_wup[c], f"scr_wup{c}") for c in range(16)]
            eloads += [(scr_wdn[d_, f_], f"scr_wdn{d_}_{f_}") for d_ in range(4) for f_ in range(4)]
        ewq = []
        epos = [0]

        def e_next_chunk():
            if not ewq:
                ewq.append(load_wchunk(m, *eloads[epos[0]]))
                epos[0] += 1
            wi_ = ewq.pop(0)
            if epos[0] < len(eloads):
                ewq.append(load_wchunk(m, *eloads[epos[0]]))
                epos[0] += 1
            return wi_

        for kind, g_ in egroups:
            L = 128
            ntl = 4 if kind == "own" else 1
            ncol = ntl * L
            c0 = g_ * 512 if kind == "own" else NOWN
            DMA("sp", mixT[:, :, 0:ncol], mixT_scr[:, :, c0:c0 + ncol].rearrange("f p n -> p f n"), writes=["mixT"], nonc=(ncol < 512))
            for tt in range(ntl):
                src = x_ctx[NPRE + c0 + tt * 128:NPRE + c0 + (tt + 1) * 128, :] if kind == "own" else x_smp[0:128, :]
                DMA("sp", x1[0:L, tt, :], src, writes=[f"x1_{tt}"])
            for c in range(4):
                wi = e_next_chunk()
                cs = slice(c * 512, (c + 1) * 512)
                for tt in range(ntl):
                    tcol = (g_ * 4 + tt) if kind == "own" else 16
                    pa, pb = pq[(cnt % 2) * 2], pq[(cnt % 2) * 2 + 1]
                    ta, tb = f"pq{(cnt % 2) * 2}", f"pq{(cnt % 2) * 2 + 1}"
                    cnt += 1
                    for ft in range(8):
                        MM(pa[0:L, :], mixT[:, ft, tt * L:(tt + 1) * L], wbuf[wi][:, ft, :], ft == 0, ft == 7, ["mixT", f"wbuf{wi}"], [ta])
                    for ft in range(8, 16):
                        MM(pb[0:L, :], mixT[:, ft, tt * L:(tt + 1) * L], wbuf[wi][:, ft, :], ft == 8, ft == 15, ["mixT", f"wbuf{wi}"], [tb])
                    STT("dve", x1[0:L, tt, cs], pa[0:L, :], rstd_ssm[0:L, tcol:tcol + 1], x1[0:L, tt, cs], ALU.mult, ALU.add,
                        [ta, f"x1_{tt}"], [f"x1_{tt}"])
                    STT("dve", x1[0:L, tt, cs], pb[0:L, :], rstd_att[0:L, tcol:tcol + 1], x1[0:L, tt, cs], ALU.mult, ALU.add,
                        [tb, f"x1_{tt}"], [f"x1_{tt}"])
            for tt in range(ntl):
                front_end(m, None, x1[:, tt, :], f"x1_{tt}", L, gmlpT, "gmlpT", 0, tt * L, f"hT0_{tt}")
            htoks = [f"hT0_{tt}" for tt in range(ntl)]
            for ffc in range(16):
                wi = e_next_chunk()
                for f4 in range(4):
                    fft = ffc * 4 + f4
                    j = fft % 2
                    for kt in range(KT):
                        MM(pmm[j][:, 0:ncol], wbuf[wi][:, kt, f4 * 128:(f4 + 1) * 128], hT[0][:, kt, 0:ncol], kt == 0, kt == KT - 1,
                           htoks + [f"wbuf{wi}"], [f"pmm{j}"])
                    ACTV(rr[j][:, 0:ncol], pmm[j][:, 0:ncol], AF.Relu, [f"pmm{j}"], [f"rr{j}"])
                    TT("pool" if fft % 4 == 3 else "dve", aT[:, fft, 0:ncol], rr[j][:, 0:ncol], rr[j][:, 0:ncol], ALU.mult, [f"rr{j}"], [f"aT{fft}"])
            for dmc in range(4):
                ds_ = slice(dmc * 512, (dmc + 1) * 512)
                for fq in range(4):
                    wi = e_next_chunk()
                    for tt in range(ntl):
                        for kt in range(KT):
                            fft = fq * 16 + kt
                            MM(pq[tt][0:L, :], aT[:, fft, tt * L:(tt + 1) * L], wbuf[wi][:, kt, :], fq == 0 and kt == 0, fq == 3 and kt == KT - 1,
                               [f"aT{fft}", f"wbuf{wi}"], [f"pq{tt}"])
                for tt in range(ntl):
                    j = tt % 2
                    TT("dve", yb[j][0:L], pq[tt][0:L, :], x1[0:L, tt, ds_], ALU.add, [f"pq{tt}", f"x1_{tt}"], [f"yb{j}"])
                    if kind == "own":
                        dst = y_own[c0 + tt * 128:c0 + (tt + 1) * 128, ds_]
                    else:
                        dst = y_smp[0:128, ds_]
                    DMA("sp", dst, yb[j][0:L], reads=[f"yb{j}"], out_store=True)
        P.flush()
        stk.pop()
        ee.close()

    if P.ops:
        P.flush()
    es.close()
    return nc, P


def _host_consts():
    ident = np.eye(128, dtype=np.float32)
    s = np.arange(128)
    triT = (s[:, None] <= s[None, :]).astype(np.float32)
    iota1 = np.broadcast_to((np.arange(128) + 1).astype(np.float32)[None, :], (128, 128)).copy()
    sp1 = (np.arange(128) + 1).astype(np.float32)[:, None].copy()
    return ident, triT, iota1, sp1


def _prep_inputs(inp):
    f32 = np.float32
    ident, triT, iota1, sp1 = _host_consts()
    b_re = np.asarray(inp["ssm_b_re"], f32)[0]
    b_im = np.asarray(inp["ssm_b_im"], f32)[0]
    c_re = np.asarray(inp["ssm_c_re"], f32)[0]
    c_im = np.asarray(inp["ssm_c_im"], f32)[0]

    def bblk(b):
        out = np.zeros((128, 8, 512), f32)
        for fb in range(8):
            for gl in range(8):
                g = fb * 8 + gl
                out[gl * 16:(gl + 1) * 16, fb, gl * 64:(gl + 1) * 64] = b[g].T
        return out

    def cblk(c):
        out = np.zeros((128, 8, 4, 128), f32)
        for fb in range(8):
            for k in range(4):
                for e in range(2):
                    g = fb * 8 + 2 * k + e
                    out[e * 64:(e + 1) * 64, fb, k, 16 * (2 * k + e):16 * (2 * k + e + 1)] = c[g].T
        return out

    def bpad(b):
        out = np.zeros((128, 32, 32), f32)
        for j in range(32):
            for e in range(2):
                out[e * 64:(e + 1) * 64, j, e * 16:(e + 1) * 16] = b[2 * j + e]
        return out

    shared = {
        "g_norm_mix": np.asarray(inp["g_norm_mix"], f32)[0],
        "w_in": np.asarray(inp["w_in"], f32)[0],
        "b_f": np.asarray(inp["b_f"], f32),
        "ssm_a_re": np.asarray(inp["ssm_a_re"], f32).reshape(1, G * PST),
        "ssm_a_im": np.asarray(inp["ssm_a_im"], f32).reshape(1, G * PST),
        "ssm_log_step": np.asarray(inp["ssm_log_step"], f32).reshape(1, G),
        "bblk_re": bblk(b_re), "bblk_im": bblk(b_im),
        "cblk_re": cblk(c_re), "cblk_im": cblk(c_im),
        "bpad_re": bpad(b_re), "bpad_im": bpad(b_im),
        "ssm_d": np.asarray(inp["ssm_d"], f32).reshape(G * 16),
        "w_glu": np.asarray(inp["w_glu"], f32)[0],
        "g_q": np.asarray(inp["g_q"], f32), "g_k": np.asarray(inp["g_k"], f32),
        "g_out_ssm": np.asarray(inp["g_out_ssm"], f32)[0],
        "g_out_attn": np.asarray(inp["g_out_attn"], f32)[0],
        "w_out": np.asarray(inp["w_out"], f32)[0],
        "g_norm_mlp": np.asarray(inp["g_norm_mlp"], f32)[0],
        "w_up": np.asarray(inp["w_up"], f32)[0],
        "w_down": np.asarray(inp["w_down"], f32)[0],
        "c_ident": ident, "c_triT": triT, "c_iota1": iota1, "c_sp1": sp1,
    }
    xp = np.asarray(inp["x_prompt"], f32)
    xs = np.asarray(inp["x_sample"], f32)
    cks = np.asarray(inp["cache_k"], f32)[0]
    cvs = np.asarray(inp["cache_v"], f32)[0]
    clfs = np.asarray(inp["cache_logf"], f32)[0]
    sres = np.asarray(inp["state_ssm_re"], f32)[0].reshape(32, G * PST)
    sims = np.asarray(inp["state_ssm_im"], f32)[0].reshape(32, G * PST)
    maps = []
    for c in range(8):
        b, j = c // 4, c % 4
        x_ctx = np.zeros((NCTX, D), f32)
        nv = (3 - j) * NOWN
        x_ctx[nv:] = xp[b, 0:(j + 1) * NOWN]
        km = np.zeros((NCTX,), f32)
        km[:nv] = -30000.0
        m = dict(shared)
        m.update({
            "x_ctx": x_ctx,
            "x_smp": np.ascontiguousarray(xs[4 * c:4 * c + 4].reshape(NSMP, D)),
            "ck": np.ascontiguousarray(cks[4 * c:4 * c + 4]),
            "cv": np.ascontiguousarray(cvs[4 * c:4 * c + 4]),
            "clf": np.ascontiguousarray(clfs[4 * c:4 * c + 4]),
            "sre": np.ascontiguousarray(sres[4 * c:4 * c + 4]),
            "sim": np.ascontiguousarray(sims[4 * c:4 * c + 4]),
            "c_kmask": np.ascontiguousarray(km.reshape(NT, 128).T),
        })
        maps.append(m)
    return maps


_CACHE = {}


def kernel(**inputs):
    if "nc" not in _CACHE:
        _CACHE["nc"] = build_program()[0]
    nc = _CACHE["nc"]
    maps = _prep_inputs(inputs)
    res = run_bass_kernel_spmd(nc, maps, core_ids=list(range(8)))
    R = res.results
    f32 = np.float32
    y_p = np.zeros((2, 8192, D), f32)
    k_p = np.zeros((1, 2, 8192, NH, DH), f32)
    v_p = np.zeros((1, 2, 8192, NH, DH), f32)
    lf_p = np.zeros((1, 2, 8192, NH), f32)
    hre_p = np.zeros((1, 2, G, PST), f32)
    him_p = np.zeros((1, 2, G, PST), f32)
    y_s = np.zeros((32, 32, D), f32)
    k_s = np.zeros((1, 32, 32, NH, DH), f32)
    v_s = np.zeros((1, 32, 32, NH, DH), f32)
    lf_s = np.zeros((1, 32, 32, NH), f32)
    hre_s = np.zeros((1, 32, G, PST), f32)
    him_s = np.zeros((1, 32, G, PST), f32)
    for c in range(8):
        b, j = c // 4, c % 4
        r = R[c]
        sl = slice(j * NOWN, (j + 1) * NOWN)
        y_p[b, sl] = r["y_own"]
        k_p[0, b, sl] = r["k_own"].reshape(NOWN, NH, DH)
        v_p[0, b, sl] = r["v_own"].reshape(NOWN, NH, DH)
        lf_p[0, b, sl] = r["lf_own"]
        if j == 3:
            hre_p[0, b] = r["hre_own"].reshape(G, PST)
            him_p[0, b] = r["him_own"].reshape(G, PST)
        s4 = slice(4 * c, 4 * c + 4)
        y_s[s4] = r["y_smp"].reshape(4, 32, D)
        k_s[0, s4] = r["k_smp"].reshape(4, 32, NH, DH)
        v_s[0, s4] = r["v_smp"].reshape(4, 32, NH, DH)
        lf_s[0, s4] = r["lf_smp"].reshape(4, 32, NH)
        hre_s[0, s4] = r["hre_smp"].reshape(4, G, PST)
        him_s[0, s4] = r["him_smp"].reshape(4, G, PST)
    return (y_p, y_s, k_p, v_p, lf_p, hre_p, him_p, k_s, v_s, lf_s, hre_s, him_s)
```
